# Optimizing a Trainium2 kernel written in Bass

```python
import jax, jax.numpy as jnp
from jax import lax
import numpy as np

D_MODEL = 1024
BATCH = 16
SEQ = 2048
DEPTH = 4

N_BRANCH = 4
MIX_W = D_MODEL // N_BRANCH
HEAD_DIM = 64
CONV_A_K = 31
NA_HEADS = MIX_W // HEAD_DIM
NA_ROWS = 8
NA_COLS = 16
GRID_W = 64
DIL_HEADS = MIX_W // HEAD_DIM
DIL_PAIRS = ((128, 1), (512, 4), (2048, 16))
SWA_Q_HEADS = MIX_W // HEAD_DIM
SWA_KV_HEADS = SWA_Q_HEADS // 2
SWA_WINDOW = 128
Q_BLOCK = 128
ROPE_THETA = 500000.0
ROPE_DIMS = HEAD_DIM // 4
D_FF = 2816
FFN_CONV_K = 3
EPS = 1e-6
NEG_INF = -1e30

IN_SIZES = (
    N_BRANCH * D_MODEL,
    2 * MIX_W,
    3 * NA_HEADS * HEAD_DIM,
    3 * DIL_HEADS * HEAD_DIM,
    SWA_Q_HEADS * HEAD_DIM,
    2 * SWA_KV_HEADS * HEAD_DIM,
)
N_IN = sum(IN_SIZES)
IN_SPLITS = [int(s) for s in np.cumsum(IN_SIZES)[:-1]]

kernel_name = 'hybrid_gated_parallel_encoder'


def _rmsnorm(x, g):
    xf = x.astype(jnp.float32)
    y = xf * lax.rsqrt(jnp.mean(jnp.square(xf), axis=-1, keepdims=True) + EPS)
    return (y * g.astype(jnp.float32)).astype(x.dtype)


def _layernorm(x, g, b):
    xf = x.astype(jnp.float32)
    mu = jnp.mean(xf, axis=-1, keepdims=True)
    var = jnp.mean(jnp.square(xf - mu), axis=-1, keepdims=True)
    y = (xf - mu) * lax.rsqrt(var + EPS)
    return (y * g.astype(jnp.float32) + b.astype(jnp.float32)).astype(x.dtype)


def _rope_tables(seq):
    pos = jnp.arange(seq, dtype=jnp.float32)
    inv = ROPE_THETA ** (-jnp.arange(0, ROPE_DIMS, 2, dtype=jnp.float32) / ROPE_DIMS)
    ang = pos[:, None] * inv[None, :]
    return jnp.cos(ang), jnp.sin(ang)


def _partial_rope(x, cos, sin):
    half = ROPE_DIMS // 2
    xf = x.astype(jnp.float32)
    x1, x2, rest = xf[..., :half], xf[..., half:ROPE_DIMS], xf[..., ROPE_DIMS:]
    c, s = cos[None, :, None, :], sin[None, :, None, :]
    out = jnp.concatenate([x1 * c - x2 * s, x2 * c + x1 * s, rest], axis=-1)
    return out.astype(x.dtype)


def _depthwise_conv(u, w, b):
    k, ch = w.shape
    p = (k - 1) // 2
    y = lax.conv_general_dilated(u, w[:, None, :].astype(u.dtype), (1,), [(p, p)],
                                 dimension_numbers=('NWC', 'WIO', 'NWC'),
                                 feature_group_count=ch)
    return y + b.astype(u.dtype)


def _conformer_conv(a_in, conv_w, conv_b, ln_g, ln_b):
    a, g = jnp.split(a_in, 2, axis=-1)
    u = a * jax.nn.sigmoid(g)
    u = _depthwise_conv(u, conv_w, conv_b)
    u = _layernorm(u, ln_g, ln_b)
    return jax.nn.silu(u)


def _neighbourhood_attn(q, k, v, rpb):
    bsz, seq, heads, dh = q.shape
    rows = seq // GRID_W
    kr = min(NA_ROWS, rows)
    kc = min(NA_COLS, GRID_W)
    qg = q.reshape(bsz, rows, GRID_W, heads, dh)
    kg = k.reshape(bsz, rows, GRID_W, heads, dh)
    vg = v.reshape(bsz, rows, GRID_W, heads, dh)
    col = np.arange(GRID_W)
    col_start = np.clip(col - kc // 2, 0, GRID_W - kc)
    col_idx = col_start[:, None] + np.arange(kc)[None, :]
    dc = col_idx - col[:, None] + (NA_COLS - 1)
    bias_c = rpb[:, :, dc].astype(jnp.float32)
    scale = dh ** -0.5

    def row_block(r):
        row_start = jnp.clip(r - kr // 2, 0, rows - kr)
        row_idx = row_start + jnp.arange(kr)
        dr = row_idx - r + (NA_ROWS - 1)
        q_r = lax.dynamic_index_in_dim(qg, r, axis=1, keepdims=False)
        k_r = jnp.take(kg, row_idx, axis=1)[:, :, col_idx]
        v_r = jnp.take(vg, row_idx, axis=1)[:, :, col_idx]
        s = jnp.einsum('bwhd,brwkhd->bhwrk', q_r, k_r).astype(jnp.float32) * scale
        bias = jnp.transpose(jnp.take(bias_c, dr, axis=1), (0, 2, 1, 3))
        s = s + bias[None]
        p = jax.nn.softmax(s.reshape(bsz, heads, GRID_W, kr * kc), axis=-1)
        p = p.reshape(bsz, heads, GRID_W, kr, kc).astype(v.dtype)
        return jnp.einsum('bhwrk,brwkhd->bwhd', p, v_r)

    out = lax.map(row_block, jnp.arange(rows))
    return jnp.transpose(out, (1, 0, 2, 3, 4)).reshape(bsz, seq, heads * dh)


def _dilated_group(q, k, v, dil, n_side):
    bsz, seq, heads, dh = q.shape
    offs = dil * np.arange(-n_side, n_side + 1)
    scale = dh ** -0.5

    def block(i):
        start = i * Q_BLOCK
        kpos = start + jnp.arange(Q_BLOCK)[:, None] + offs[None, :]
        valid = (kpos >= 0) & (kpos < seq)
        kidx = jnp.clip(kpos, 0, seq - 1)
        q_b = lax.dynamic_slice_in_dim(q, start, Q_BLOCK, axis=1)
        k_b = jnp.take(k, kidx, axis=1)
        v_b = jnp.take(v, kidx, axis=1)
        s = jnp.einsum('bqhd,bqjhd->bhqj', q_b, k_b).astype(jnp.float32) * scale
        s = jnp.where(valid[None, None], s, NEG_INF)
        m = jnp.max(s, axis=-1, keepdims=True)
        p = jnp.exp(s - m)
        l = jnp.sum(p, axis=-1)
        o = jnp.einsum('bhqj,bqjhd->bqhd', p, v_b.astype(jnp.float32))
        o = o / jnp.transpose(l, (0, 2, 1))[..., None]
        lse = jnp.transpose(m[..., 0] + jnp.log(l), (0, 2, 1))
        return o, lse

    o, lse = lax.map(block, jnp.arange(seq // Q_BLOCK))
    o = jnp.transpose(o, (1, 0, 2, 3, 4)).reshape(bsz, seq, heads, dh)
    lse = jnp.transpose(lse, (1, 0, 2, 3)).reshape(bsz, seq, heads)
    return o, lse


def _dilated_attn(q, k, v):
    bsz, seq, heads, dh = q.shape
    outs, lses = [], []
    for window, dil in DIL_PAIRS:
        o, lse = _dilated_group(q, k, v, dil, window // (2 * dil))
        outs.append(o)
        lses.append(lse)
    wts = jax.nn.softmax(jnp.stack(lses, axis=0), axis=0)
    out = jnp.sum(wts[..., None] * jnp.stack(outs, axis=0), axis=0)
    return out.reshape(bsz, seq, heads * dh).astype(q.dtype)


def _window_gqa_sink(q, k, v, sink):
    bsz, seq, hq, dh = q.shape
    hkv = k.shape[2]
    grp = hq // hkv
    pad = SWA_WINDOW
    klen = Q_BLOCK + 2 * pad
    kp = jnp.pad(k, ((0, 0), (pad, pad), (0, 0), (0, 0)))
    vp = jnp.pad(v, ((0, 0), (pad, pad), (0, 0), (0, 0)))
    qg = q.reshape(bsz, seq, hkv, grp, dh)
    sink_g = sink.astype(jnp.float32).reshape(hkv, grp)[None, :, :, None]
    scale = dh ** -0.5

    def block(i):
        start = i * Q_BLOCK
        q_b = lax.dynamic_slice_in_dim(qg, start, Q_BLOCK, axis=1)
        k_b = lax.dynamic_slice_in_dim(kp, start, klen, axis=1)
        v_b = lax.dynamic_slice_in_dim(vp, start, klen, axis=1)
        qpos = start + jnp.arange(Q_BLOCK)
        kpos = start - pad + jnp.arange(klen)
        valid = (jnp.abs(qpos[:, None] - kpos[None, :]) <= SWA_WINDOW) & (kpos >= 0)[None] & (kpos < seq)[None]
        s = jnp.einsum('bqkgd,bskd->bkgqs', q_b, k_b).astype(jnp.float32) * scale
        s = jnp.where(valid[None, None, None], s, NEG_INF)
        m = jnp.maximum(jnp.max(s, axis=-1), sink_g)
        p = jnp.exp(s - m[..., None])
        denom = jnp.sum(p, axis=-1) + jnp.exp(sink_g - m)
        o = jnp.einsum('bkgqs,bskd->bqkgd', p, v_b.astype(jnp.float32))
        o = o / jnp.transpose(denom, (0, 3, 1, 2))[..., None]
        return o.reshape(bsz, Q_BLOCK, hq * dh).astype(q.dtype)

    out = lax.map(block, jnp.arange(seq // Q_BLOCK))
    return jnp.transpose(out, (1, 0, 2, 3)).reshape(bsz, seq, hq * dh)


def _conv_ffn(h, w_up, conv_w, conv_b, w_down):
    u = _depthwise_conv(h @ w_up, conv_w, conv_b)
    gate, up = jnp.split(u, 2, axis=-1)
    return (jax.nn.silu(gate) * up) @ w_down


def setup_inputs(seed: int = 0) -> dict:
    key = jax.random.key(seed)
    ks = jax.random.split(key, 24)
    f32 = jnp.float32

    def nrm(k, shape, scale):
        return jax.random.normal(k, shape, f32) * scale

    def gain(k, shape):
        return 1.0 + 0.02 * jax.random.normal(k, shape, f32)

    return {
        'x': nrm(ks[0], (BATCH, SEQ, D_MODEL), 1.0),
        'g_mix': gain(ks[1], (DEPTH, D_MODEL)),
        'w_in': nrm(ks[2], (DEPTH, D_MODEL, N_IN), D_MODEL ** -0.5),
        'gate_b': nrm(ks[3], (DEPTH, N_BRANCH, D_MODEL), 0.01),
        'a_conv_w': nrm(ks[4], (DEPTH, CONV_A_K, MIX_W), CONV_A_K ** -0.5),
        'a_conv_b': nrm(ks[5], (DEPTH, MIX_W), 0.01),
        'a_ln_g': gain(ks[6], (DEPTH, MIX_W)),
        'a_ln_b': nrm(ks[7], (DEPTH, MIX_W), 0.01),
        'na_qn': gain(ks[8], (DEPTH, HEAD_DIM)),
        'na_kn': gain(ks[9], (DEPTH, HEAD_DIM)),
        'na_rpb': nrm(ks[10], (DEPTH, NA_HEADS, 2 * NA_ROWS - 1, 2 * NA_COLS - 1), 0.1),
        'dil_qn': gain(ks[11], (DEPTH, HEAD_DIM)),
        'dil_kn': gain(ks[12], (DEPTH, HEAD_DIM)),
        'swa_qn': gain(ks[13], (DEPTH, HEAD_DIM)),
        'swa_kn': gain(ks[14], (DEPTH, HEAD_DIM)),
        'swa_sink': nrm(ks[15], (DEPTH, SWA_Q_HEADS), 1.0),
        'w_branch': nrm(ks[16], (DEPTH, N_BRANCH, MIX_W, D_MODEL), MIX_W ** -0.5),
        'w_out': nrm(ks[17], (DEPTH, D_MODEL, D_MODEL), D_MODEL ** -0.5),
        'g_ffn': gain(ks[18], (DEPTH, D_MODEL)),
        'w_up': nrm(ks[19], (DEPTH, D_MODEL, 2 * D_FF), D_MODEL ** -0.5),
        'ffn_conv_w': nrm(ks[20], (DEPTH, FFN_CONV_K, 2 * D_FF), FFN_CONV_K ** -0.5),
        'ffn_conv_b': nrm(ks[21], (DEPTH, 2 * D_FF), 0.01),
        'w_down': nrm(ks[22], (DEPTH, D_FF, D_MODEL), D_FF ** -0.5),
    }


def reference(x, g_mix, w_in, gate_b, a_conv_w, a_conv_b, a_ln_g, a_ln_b,
              na_qn, na_kn, na_rpb, dil_qn, dil_kn, swa_qn, swa_kn, swa_sink,
              w_branch, w_out, g_ffn, w_up, ffn_conv_w, ffn_conv_b, w_down):
    bsz, seq, _ = x.shape
    cos, sin = _rope_tables(seq)
    for l in range(DEPTH):
        h = _rmsnorm(x, g_mix[l])
        proj = h @ w_in[l]
        gate_pre, a_in, na_qkv, dil_qkv, swa_q, swa_kv = jnp.split(proj, IN_SPLITS, axis=-1)

        y_a = _conformer_conv(a_in, a_conv_w[l], a_conv_b[l], a_ln_g[l], a_ln_b[l])

        bq, bk, bv = [t.reshape(bsz, seq, NA_HEADS, HEAD_DIM) for t in jnp.split(na_qkv, 3, axis=-1)]
        y_b = _neighbourhood_attn(_rmsnorm(bq, na_qn[l]), _rmsnorm(bk, na_kn[l]), bv, na_rpb[l])

        cq, ck, cv = [t.reshape(bsz, seq, DIL_HEADS, HEAD_DIM) for t in jnp.split(dil_qkv, 3, axis=-1)]
        cq = _partial_rope(_rmsnorm(cq, dil_qn[l]), cos, sin)
        ck = _partial_rope(_rmsnorm(ck, dil_kn[l]), cos, sin)
        y_c = _dilated_attn(cq, ck, cv)

        dq = swa_q.reshape(bsz, seq, SWA_Q_HEADS, HEAD_DIM)
        dk, dv = [t.reshape(bsz, seq, SWA_KV_HEADS, HEAD_DIM) for t in jnp.split(swa_kv, 2, axis=-1)]
        dq = _partial_rope(_rmsnorm(dq, swa_qn[l]), cos, sin)
        dk = _partial_rope(_rmsnorm(dk, swa_kn[l]), cos, sin)
        y_d = _window_gqa_sink(dq, dk, dv, swa_sink[l])

        ys = jnp.stack([y_a, y_b, y_c, y_d], axis=2)
        branch = jnp.einsum('bsnc,ncd->bsnd', ys, w_branch[l])
        gates = jax.nn.sigmoid(gate_pre.reshape(bsz, seq, N_BRANCH, D_MODEL) + gate_b[l])
        mixed = jnp.sum(gates * branch, axis=2)
        x = x + mixed @ w_out[l]

        x = x + _conv_ffn(_rmsnorm(x, g_ffn[l]), w_up[l], ffn_conv_w[l], ffn_conv_b[l], w_down[l])
    return x
```

```python
import contextlib
import numpy as np
import concourse.bass as bass
import concourse.mybir as mybir
from concourse.bass_utils import run_bass_kernel_spmd

F32 = mybir.dt.float32
BF16 = mybir.dt.bfloat16
AF = mybir.ActivationFunctionType
ALU = mybir.AluOpType

D_MODEL = 1024
SEQ = 2048
DEPTH = 4
N_IN = 6656
D_FF = 2816
EPS = 1e-6
NT = 16
NTC = 4
ENGS = ("pe", "act", "dve", "pool", "sp")

PV_GMIX = 0
PV_GFFN = 8
PV_GATEB = 16
PV_ACW = 48
PV_ACB = 110
PV_ALNG = 112
PV_ALNB = 114
PV_QKG = 116
PV_SINK = 122
PV_FCW = 126
PV_FCB = 258
NPV = 304

CM_WIN = 0
CM_DIL = 384
CM_ID = CM_DIL + 17 * 128
CM_O1024 = CM_ID + 128
CM_B64 = CM_O1024 + 128
CM_ONE = CM_B64 + 128
CM_ROT = CM_ONE + 128
CM_O256 = CM_ROT + 128
NCM = CM_O256 + 128

NA_COLS = 2688


class _Op:
    __slots__ = ("eng", "fn", "deps", "dma_sem")


class Sched:
    def __init__(self):
        self.ops = []
        self.last_w = {}
        self.readers = {}

    def add(self, eng, fn, reads=(), writes=(), dma_sem=None):
        op = _Op()
        op.eng = eng
        op.fn = fn
        op.dma_sem = dma_sem
        deps = set()
        lw = self.last_w
        rd = self.readers
        for r in reads:
            w = lw.get(r)
            if w is not None:
                deps.add(w)
        for r in writes:
            w = lw.get(r)
            if w is not None:
                deps.add(w)
            x = rd.get(r)
            if x:
                deps.update(x.values())
        idx = len(self.ops)
        deps.discard(idx)
        op.deps = deps
        self.ops.append(op)
        for r in writes:
            lw[r] = idx
            rd[r] = {}
        rkey = eng if dma_sem is None else (eng, idx)
        for r in reads:
            if r in writes:
                continue
            l = rd.get(r)
            if l is None:
                rd[r] = {rkey: idx}
            else:
                l[rkey] = idx
        return idx

    def finalize(self):
        ops = self.ops
        n = len(ops)
        needs_inc = [False] * n
        for i, op in enumerate(ops):
            for d in op.deps:
                od = ops[d]
                if od.dma_sem is None and od.eng != op.eng:
                    needs_inc[d] = True
        cnt = {e: 0 for e in ENGS}
        dcnt = {}
        sig = [None] * n
        for i, op in enumerate(ops):
            if op.dma_sem is not None:
                dcnt[op.dma_sem] = dcnt.get(op.dma_sem, 0) + 16
                sig[i] = (op.dma_sem, dcnt[op.dma_sem])
            elif needs_inc[i]:
                cnt[op.eng] += 1
                sig[i] = (op.eng, cnt[op.eng])
        eng_vc = {e: {} for e in ENGS}
        op_vc = [None] * n
        waits = [None] * n
        for i, op in enumerate(ops):
            vc = eng_vc[op.eng]
            w = {}
            for d in sorted(op.deps):
                s = sig[d]
                if s is None:
                    continue
                if vc.get(s[0], 0) >= s[1]:
                    continue
                if w.get(s[0], 0) < s[1]:
                    w[s[0]] = s[1]
                dv = op_vc[d]
                for k2, v2 in dv.items():
                    if vc.get(k2, 0) < v2:
                        vc[k2] = v2
                vc[s[0]] = s[1]
            waits[i] = w
            if sig[i] is not None:
                snap = dict(vc)
                snap[sig[i][0]] = sig[i][1]
                op_vc[i] = snap
                if op.dma_sem is None:
                    vc[sig[i][0]] = sig[i][1]
        self.sig = sig
        self.waits = waits
        self.counts = cnt
        self.dcounts = dcnt

    def run_engine(self, ename, eng, sems):
        ops = self.ops
        sig = self.sig
        waits = self.waits
        for i, op in enumerate(ops):
            if op.eng != ename:
                continue
            for key, val in waits[i].items():
                eng.wait_ge(sems[key], val)
            ins = op.fn(eng)
            s = sig[i]
            if s is not None:
                ins.then_inc(sems[s[0]], 16 if op.dma_sem is not None else 1)


def build_program(nl=DEPTH, nseq=2, l0=0):
    nc = bass.Bass("TRN2", target_bir_lowering=False)
    x_d = nc.dram_tensor("x", [nseq, SEQ, D_MODEL], F32, kind="ExternalInput").ap()
    w_in_d = nc.dram_tensor("w_in", [DEPTH, D_MODEL, N_IN], F32, kind="ExternalInput").ap()
    w_br_d = nc.dram_tensor("w_branch", [DEPTH, 4, 256, D_MODEL], F32, kind="ExternalInput").ap()
    w_out_d = nc.dram_tensor("w_out", [DEPTH, D_MODEL, D_MODEL], F32, kind="ExternalInput").ap()
    w_up_d = nc.dram_tensor("w_up", [DEPTH, D_MODEL, 2 * D_FF], F32, kind="ExternalInput").ap()
    w_dn_d = nc.dram_tensor("w_down", [DEPTH, D_FF, D_MODEL], F32, kind="ExternalInput").ap()
    pvec_d = nc.dram_tensor("pvec", [DEPTH, 128, NPV], F32, kind="ExternalInput").ap()
    nab_d = nc.dram_tensor("nab", [DEPTH, 128, 4 * NA_COLS], F32, kind="ExternalInput").ap()
    cm_d = nc.dram_tensor("cmat", [128, NCM], F32, kind="ExternalInput").ap()
    rc_d = nc.dram_tensor("ropec", [128, SEQ], F32, kind="ExternalInput").ap()
    rs_d = nc.dram_tensor("ropes", [128, SEQ], F32, kind="ExternalInput").ap()
    idf_d = nc.dram_tensor("identf", [128, 128], F32, kind="ExternalInput").ap()
    out_d = nc.dram_tensor("out", [nseq, SEQ, D_MODEL], F32, kind="ExternalOutput").ap()

    xT = nc.alloc_sbuf_tensor("xT", [128, 8, SEQ], F32)
    hT = nc.alloc_sbuf_tensor("hT", [128, 8, SEQ], BF16)
    yT = nc.alloc_sbuf_tensor("yT", [128, 8, SEQ], BF16)
    SCR_N = 20736
    SCR = nc.alloc_sbuf_tensor("scr", [128, SCR_N], BF16)
    WB = nc.alloc_sbuf_tensor("wb", [128, 2, 4096], BF16)
    WBR = nc.alloc_sbuf_tensor("wbr", [128, 2, 1024], BF16)
    ropeC = nc.alloc_sbuf_tensor("ropeC", [128, SEQ], BF16)
    ropeS = nc.alloc_sbuf_tensor("ropeS", [128, SEQ], BF16)
    CM = nc.alloc_sbuf_tensor("cm", [128, NCM], BF16)
    identF = nc.alloc_sbuf_tensor("identF", [128, 128], F32)
    PV = nc.alloc_sbuf_tensor("pv", [128, NPV], F32)
    ESK = nc.alloc_sbuf_tensor("esk", [128, 4], F32)
    DG = nc.alloc_sbuf_tensor("dg", [128, 6, 128], BF16)
    DUM = nc.alloc_sbuf_tensor("dum", [128, 16], BF16)
    HS = nc.alloc_sbuf_tensor("hs", [128, 48], BF16)
    ps = nc.alloc_psum_tensor("ps", [128, 8, 512], F32)

    S = Sched()

    def scr(off_b, n_el, dt):
        assert off_b % 4 == 0
        if dt is BF16:
            assert off_b // 2 + n_el <= SCR_N, (off_b, n_el)
            return SCR[:, off_b // 2: off_b // 2 + n_el]
        assert off_b // 2 + 2 * n_el <= SCR_N, (off_b, n_el)
        return SCR[:, off_b // 2: off_b // 2 + 2 * n_el].bitcast(F32)

    ring = [0]

    def nb():
        b = ring[0]
        ring[0] = (b + 1) % 8
        return b

    def PSR(b):
        return "ps%d" % b

    def cm(c0, n=128):
        return CM[:, c0:c0 + n]

    def pvc(c):
        return PV[:, c:c + 1]

    def mm(out, lhsT, rhs, start, stop, reads, writes):
        S.add("pe", lambda e: e.matmul(out, lhsT, rhs, start=start, stop=stop), reads=reads, writes=writes)

    def act(out, in_, func, reads, writes, bias=None, scale=None):
        kw = {}
        if bias is not None:
            kw["bias"] = bias
        if scale is not None:
            kw["scale"] = scale
        S.add("act", lambda e: e.activation(out=out, in_=in_, func=func, **kw), reads=reads, writes=writes)

    def tt(out, in0, in1, op, reads, writes, eng="dve"):
        S.add(eng, lambda e: e.tensor_tensor(out=out, in0=in0, in1=in1, op=op), reads=reads, writes=writes)

    def ts(out, in0, s1, s2, op0, op1, reads, writes, eng="dve"):
        if op1 is None:
            S.add(eng, lambda e: e.tensor_scalar(out=out, in0=in0, scalar1=s1, scalar2=None, op0=op0),
                  reads=reads, writes=writes)
        else:
            S.add(eng, lambda e: e.tensor_scalar(out=out, in0=in0, scalar1=s1, scalar2=s2, op0=op0, op1=op1),
                  reads=reads, writes=writes)

    def rsqrt_eps(out, in_, reads, writes):
        act(out, in_, AF.Ln, reads, writes, bias=EPS, scale=1.0)
        act(out, out, AF.Exp, list(writes), list(writes), scale=-0.5)

    def stt(out, in0, scalar, in1, op0, op1, reads, writes, eng="dve"):
        S.add(eng, lambda e: e.scalar_tensor_tensor(out=out, in0=in0, scalar=scalar, in1=in1, op0=op0, op1=op1),
              reads=reads, writes=writes)

    def dma(eng, out, in_, sem, reads, writes):
        S.add(eng, lambda e: e.dma_start(out=out, in_=in_), reads=reads, writes=writes, dma_sem=sem)

    dma("pool", CM[:], cm_d, "c0", [], ["CM"])
    dma("pool", ropeC[:], rc_d, "c1", [], ["ropeC"])
    dma("pool", ropeS[:], rs_d, "c2", [], ["ropeS"])
    dma("sp", identF[:], idf_d, "c3", [], ["identF"])

    HT_ALL = ["hT:%d:%d" % (kc, tc) for kc in range(8) for tc in range(NTC)]

    def ht_tc(tc):
        return ["hT:%d:%d" % (kc, tc) for kc in range(8)]

    wslot = [0]

    def next_wslot():
        s = wslot[0]
        wslot[0] = 1 - s
        return s

    def wload(slot, dst, src):
        dma("pool", dst, src, "w%d" % slot, [], ["WB%d" % slot])

    dgslot = [0]

    def make_diag(col):
        s = dgslot[0]
        dgslot[0] = (s + 1) % 6
        tt(DG[:, s, :], cm(CM_ID), PV[:, col:col + 1].to_broadcast([128, 128]), ALU.mult, ["CM", "PV"], ["DG%d" % s])
        return DG[:, s, :], "DG%d" % s

    def load_x(s):
        for T in range(NT):
            sl = T % 2
            st = scr(sl * 4096, 1024, F32)
            dma("sp", st, x_d[s, T * 128:(T + 1) * 128, :], "xs%d" % sl, [], ["XST%d" % sl])
            for half in range(2):
                b = nb()
                for j in range(4):
                    kc = half * 4 + j
                    mm(ps[:, b, j * 128:(j + 1) * 128], st[:, kc * 128:(kc + 1) * 128], identF[:], True, True,
                       ["XST%d" % sl, "identF"], [PSR(b)])
                eng = "dve" if half == 0 else "act"
                dst = xT[:, half * 4:(half + 1) * 4, T * 128:(T + 1) * 128]
                src = ps[:, b, :].rearrange("p (j t) -> p j t", j=4)
                wr = ["xT:%d:%d" % (half * 4 + j, T // 4) for j in range(4)]
                if eng == "dve":
                    S.add("dve", lambda e, dst=dst, src=src: e.tensor_copy(out=dst, in_=src), reads=[PSR(b)], writes=wr)
                else:
                    act(dst, src, AF.Copy, [PSR(b)], wr)

    def store_x(s):
        for T in range(NT):
            sl = T % 2
            st = scr(sl * 4096, 1024, F32)
            for half in range(2):
                b = nb()
                for j in range(4):
                    kc = half * 4 + j
                    mm(ps[:, b, j * 128:(j + 1) * 128], xT[:, kc, T * 128:(T + 1) * 128], identF[:], True, True,
                       ["xT:%d:%d" % (kc, T // 4), "identF"], [PSR(b)])
                dst = st[:, half * 512:(half + 1) * 512]
                if half == 0:
                    S.add("dve", lambda e, dst=dst, b=b: e.tensor_copy(out=dst, in_=ps[:, b, :]), reads=[PSR(b)],
                          writes=["XST%d" % sl])
                else:
                    act(dst, ps[:, b, :], AF.Copy, [PSR(b), "XST%d" % sl], ["XST%d" % sl])
            dma("sp", out_d[s, T * 128:(T + 1) * 128, :], st, "xo%d" % sl, ["XST%d" % sl], ["OUT%d" % sl])

    def rmsnorm(gcol):
        for tc in range(NTC):
            b = nb()
            tsl = slice(tc * 512, (tc + 1) * 512)
            for kc in range(8):
                q = kc % 3
                sq = scr(q * 1024, 512, BF16)
                act(sq, xT[:, kc, tsl], AF.Square, ["xT:%d:%d" % (kc, tc)], ["SQ%d" % q])
                mm(ps[:, b, :], cm(CM_O1024), sq, kc == 0, kc == 7, ["SQ%d" % q, "CM"], [PSR(b)])
            r = tc % 2
            rstd = scr(3072 + r * 2048, 512, F32)
            rsqrt_eps(rstd, ps[:, b, :], [PSR(b)], ["RS%d" % r])
            for kc in range(8):
                stt(hT[:, kc, tsl], xT[:, kc, tsl], pvc(gcol + kc), rstd, ALU.mult, ALU.mult,
                    ["xT:%d:%d" % (kc, tc), "PV", "RS%d" % r], ["hT:%d:%d" % (kc, tc)])

    def conformer(l):
        slot = next_wslot()
        W = WB[:, slot, :].rearrange("p (k n) -> p k n", k=8)
        wload(slot, W, w_in_d[l, :, 4096:4608].rearrange("(k p) n -> p k n", p=128))
        UP = scr(0, 2 * 2080, BF16).rearrange("p (c t) -> p c t", c=2)
        o_sig = 8320
        o_cvf = o_sig + 4096
        o_cvb = o_cvf + 8192
        o_sqb = o_cvb + 4096
        o_st = o_sqb + 4096
        o_t = o_st + 4096
        for cc in range(2):
            S.add("dve", lambda e, cc=cc: e.memset(UP[:, cc, 0:15], 0.0), writes=["UPpad%d" % cc])
            S.add("dve", lambda e, cc=cc: e.memset(UP[:, cc, 15 + SEQ:30 + SEQ], 0.0), writes=["UPpad%d" % cc])
        for cc in range(2):
            for tc in range(NTC):
                tsl = slice(tc * 512, (tc + 1) * 512)
                ba, bg = nb(), nb()
                for kc in range(8):
                    mm(ps[:, ba, :], W[:, kc, cc * 128:(cc + 1) * 128], hT[:, kc, tsl], kc == 0, kc == 7,
                       ["WB%d" % slot, "hT:%d:%d" % (kc, tc)], [PSR(ba)])
                for kc in range(8):
                    mm(ps[:, bg, :], W[:, kc, 256 + cc * 128:256 + (cc + 1) * 128], hT[:, kc, tsl], kc == 0, kc == 7,
                       ["WB%d" % slot, "hT:%d:%d" % (kc, tc)], [PSR(bg)])
                q = tc % 2
                sig = scr(o_sig + q * 2048, 512, F32)
                act(sig, ps[:, bg, :], AF.Sigmoid, [PSR(bg)], ["SIG%d" % q])
                tt(UP[:, cc, 15 + tc * 512:15 + (tc + 1) * 512], ps[:, ba, :], sig, ALU.mult,
                   [PSR(ba), "SIG%d" % q], ["UP:%d:%d" % (cc, tc)])
        up_all = ["UP:%d:%d" % (cc, tc) for cc in range(2) for tc in range(NTC)] + ["UPpad0", "UPpad1"]
        for cc in range(2):
            for k in range(31):
                dg, dres = make_diag(PV_ACW + cc * 31 + k)
                for tc in range(NTC):
                    b = cc * 4 + tc
                    mm(ps[:, b, :], dg, UP[:, cc, tc * 512 + k: tc * 512 + k + 512], k == 0, k == 30,
                       [dres] + up_all, [PSR(b)])
        ring[0] = 0
        for tc in range(NTC):
            tsl = slice(tc * 512, (tc + 1) * 512)
            q = tc % 2
            for cc in range(2):
                b = cc * 4 + tc
                cvf = scr(o_cvf + (q * 2 + cc) * 2048, 512, F32)
                cvb = scr(o_cvb + (q * 2 + cc) * 1024, 512, BF16)
                sqb = scr(o_sqb + (q * 2 + cc) * 1024, 512, BF16)
                act(cvf, ps[:, b, :], AF.Identity, [PSR(b), "PV"], ["CVF%d%d" % (q, cc)], bias=pvc(PV_ACB + cc))
                act(cvb, ps[:, b, :], AF.Identity, [PSR(b), "PV"], ["CVB%d%d" % (q, cc)], bias=pvc(PV_ACB + cc))
                act(sqb, ps[:, b, :], AF.Square, [PSR(b), "PV"], ["SQB%d%d" % (q, cc)], bias=pvc(PV_ACB + cc))
            b1 = tc
            b2 = 4 + tc
            for cc in range(2):
                cvb = scr(o_cvb + (q * 2 + cc) * 1024, 512, BF16)
                mm(ps[:, b1, :], cm(CM_O256), cvb, cc == 0, cc == 1, ["CM", "CVB%d%d" % (q, cc)], [PSR(b1)])
            for cc in range(2):
                sqb = scr(o_sqb + (q * 2 + cc) * 1024, 512, BF16)
                mm(ps[:, b2, :], cm(CM_O256), sqb, cc == 0, cc == 1, ["CM", "SQB%d%d" % (q, cc)], [PSR(b2)])
            msq = scr(o_st, 512, F32)
            rstd = scr(o_st + 2048, 512, F32)
            act(msq, ps[:, b1, :], AF.Square, [PSR(b1)], ["MSQ"])
            tt(msq, ps[:, b2, :], msq, ALU.subtract, [PSR(b2), "MSQ"], ["MSQ"])
            rsqrt_eps(rstd, msq, ["MSQ"], ["LRS"])
            for cc in range(2):
                cvf = scr(o_cvf + (q * 2 + cc) * 2048, 512, F32)
                t = scr(o_t + cc * 2048, 512, F32)
                tt(t, cvf, ps[:, b1, :], ALU.subtract, ["CVF%d%d" % (q, cc), PSR(b1)], ["LT%d" % cc])
                tt(t, t, rstd, ALU.mult, ["LT%d" % cc, "LRS"], ["LT%d" % cc])
                act(yT[:, cc, tsl], t, AF.Silu, ["LT%d" % cc, "PV"], ["yT:%d:%d" % (cc, tc)],
                    bias=pvc(PV_ALNB + cc), scale=pvc(PV_ALNG + cc))

    O_QK = 0
    O_V = 16384
    O_TMP = 24576
    qkT = scr(O_QK, 4 * SEQ, BF16).rearrange("p (c t) -> p c t", c=4)
    Vt = scr(O_V, 16 * 256, BF16).rearrange("p (t c) -> p t c", t=16)

    def qk_norm_all(W, wres, specs, rope):
        units = [(wc0, dst, gcol, tc) for (wc0, dst, gcol) in specs for tc in range(NTC)]
        o_sq, o_rs, o_qn, o_t1, o_t2 = O_TMP, O_TMP + 3072, O_TMP + 7168, O_TMP + 9216, O_TMP + 11264
        st = {}

        def s1(u):
            wc0, dst, gcol, tc = units[u]
            tsl = slice(tc * 512, (tc + 1) * 512)
            b = nb()
            for kc in range(8):
                mm(ps[:, b, :], W[:, kc, wc0:wc0 + 128], hT[:, kc, tsl], kc == 0, kc == 7,
                   [wres, "hT:%d:%d" % (kc, tc)], [PSR(b)])
            q = u % 3
            sq = scr(o_sq + q * 1024, 512, BF16)
            act(sq, ps[:, b, :], AF.Square, [PSR(b)], ["QSQ%d" % q])
            st[u] = (b, sq, q)

        def s2(u):
            wc0, dst, gcol, tc = units[u]
            tsl = slice(tc * 512, (tc + 1) * 512)
            b, sq, q = st[u]
            b2 = nb()
            mm(ps[:, b2, :], cm(CM_B64), sq, True, True, ["CM", "QSQ%d" % q], [PSR(b2)])
            r = u % 2
            rs = scr(o_rs + r * 2048, 512, F32)
            rsqrt_eps(rs, ps[:, b2, :], [PSR(b2)], ["QRS%d" % r])
            if not rope:
                stt(qkT[:, dst, tsl], ps[:, b, :], pvc(gcol), rs, ALU.mult, ALU.mult,
                    [PSR(b), "PV", "QRS%d" % r], ["qk:%d:%d" % (dst, tc)])
            else:
                qn = scr(o_qn + r * 1024, 512, BF16)
                stt(qn, ps[:, b, :], pvc(gcol), rs, ALU.mult, ALU.mult, [PSR(b), "PV", "QRS%d" % r], ["QN%d" % r])
                st[u] = (qn, r)

        def s3(u):
            wc0, dst, gcol, tc = units[u]
            tsl = slice(tc * 512, (tc + 1) * 512)
            qn, r = st[u]
            b3 = nb()
            mm(ps[:, b3, :], cm(CM_ROT), qn, True, True, ["CM", "QN%d" % r], [PSR(b3)])
            t1 = scr(o_t1, 512, F32)
            t2 = scr(o_t2, 512, F32)
            tt(t2, qn, ropeC[:, tsl], ALU.mult, ["QN%d" % r, "ropeC"], ["RT2"], eng="pool")
            tt(t1, ps[:, b3, :], ropeS[:, tsl], ALU.mult, [PSR(b3), "ropeS"], ["RT1"])
            tt(qkT[:, dst, tsl], t1, t2, ALU.add, ["RT1", "RT2"], ["qk:%d:%d" % (dst, tc)])

        n = len(units)
        for u in range(n + 2):
            if u < n:
                s1(u)
            if 1 <= u <= n:
                s2(u - 1)
            if rope and 2 <= u <= n + 1:
                s3(u - 2)

    def v_proj(WVv, wres, ncols):
        per_bank = 512 // ncols
        for T0 in range(0, NT, per_bank):
            b = nb()
            for j in range(per_bank):
                T = T0 + j
                for kc in range(8):
                    mm(ps[:, b, j * ncols:(j + 1) * ncols], hT[:, kc, T * 128:(T + 1) * 128], WVv[:, kc, 0:ncols],
                       kc == 0, kc == 7, [wres, "hT:%d:%d" % (kc, T // 4)], [PSR(b)])
            dst = Vt[:, T0:T0 + per_bank, 0:ncols]
            src = ps[:, b, :].rearrange("p (j c) -> p j c", j=per_bank)
            act(dst, src, AF.Copy, [PSR(b)], ["V:%d" % T for T in range(T0, T0 + per_bank)])

    O_EM = O_TMP
    O_PR = O_TMP + 6656
    O_PM = O_PR + 4096
    O_R = O_PM + 4096
    NPS = 4
    LAG = 3

    def attention(kind, branch, l):
        gi = [0]
        R = {"na": 2, "dil": 8, "swa": 1}[kind]

        def em_load(h, part):
            c0 = h * NA_COLS
            if part == "I":
                dst, src, res, sem = scr(O_EM + (h % 2) * 1280, 640, BF16), nab_d[l, :, c0:c0 + 640], "EMI%d" % (h % 2), "emi%d" % (h % 2)
            elif part == "E0":
                dst, src, res, sem = scr(O_EM + 2560, 1024, BF16), nab_d[l, :, c0 + 640:c0 + 1664], "EME0", "eme0"
            else:
                dst, src, res, sem = scr(O_EM + 4608, 1024, BF16), nab_d[l, :, c0 + 1664:c0 + 2688], "EME3", "eme3"
            dma("pool", dst, src, sem, [], [res])
            act(dst, dst, AF.Exp, [res], [res])

        if kind == "na":
            em_load(0, "I")
            em_load(0, "E0")
            em_load(0, "E3")
        for cq in range(2):
            for hp in range(2):
                if kind == "na":
                    hh = cq * 2 + hp
                    if hh + 1 < 4:
                        em_load(hh + 1, "I")
                    if hh > 0:
                        em_load(hh, "E3")
                hs = slice(hp * 64, hp * 64 + 64)
                ck = 2 if kind == "swa" else 2 + cq
                vc0 = 0 if kind == "swa" else cq * 128
                h_sw = hp * 2 + cq
                pend = []
                for g in range(NTC):
                    par = (cq * 2 + hp) * NTC + g
                    ob, db = (0, 1) if par % 2 == 0 else (2, 3)
                    items = []
                    if kind == "na":
                        h = cq * 2 + hp
                        emI = scr(O_EM + (h % 2) * 1280, 640, BF16)
                        emE0 = scr(O_EM + 2560, 1024, BF16)
                        emE3 = scr(O_EM + 4608, 1024, BF16)
                        resI = "EMI%d" % (h % 2)
                        if g == 0:
                            for T in range(4):
                                items.append((T, 0, 1, emE0, T * 256, "EME0"))
                            qint = (2, 3)
                        elif g == 3:
                            qint = (12, 13)
                        else:
                            qint = tuple(range(4 * g, 4 * g + 4))
                        for T in range(NT):
                            qs = [qt for qt in qint if abs(T - qt) <= 2]
                            if qs:
                                items.append((T, qs[0], qs[-1], emI, (2 - (T - qs[0])) * 128, resI))
                        if g == 3:
                            for T in range(12, 16):
                                items.append((T, 14, 15, emE3, (T - 12) * 256, "EME3"))
                    else:
                        base = CM_DIL if kind == "dil" else CM_WIN
                        for T in range(NT):
                            qlo = max(4 * g, T - R)
                            qhi = min(4 * g + 3, T + R)
                            if qlo > qhi:
                                continue
                            items.append((T, qlo, qhi, CM, base + (R - (T - qlo)) * 128, "CM"))
                    pslotacc = par % 2
                    pacc = WBR[:, pslotacc, 0:512]
                    pares = "WBR%d" % pslotacc
                    S.add("pool", lambda e, pacc=pacc: e.memset(pacc, 0.0), writes=[pares])
                    for idx, (T, qlo, qhi, mten, mc, mres) in enumerate(items):
                        n = qhi - qlo + 1
                        w = n * 128
                        sb = 4 + gi[0] % 4
                        pslot = gi[0] % NPS
                        gi[0] += 1
                        qres = ["qk:%d:%d" % (cq, g)]
                        mm(ps[:, sb, 0:w], qkT[hs, ck, T * 128:(T + 1) * 128], qkT[hs, cq, qlo * 128:(qhi + 1) * 128],
                           True, True, ["qk:%d:%d" % (ck, T // 4)] + qres, [PSR(sb)])
                        pr = scr(O_PR + pslot * 1024, 512, BF16)
                        pm = scr(O_PM + pslot * 1024, 512, BF16)
                        act(pr[:, 0:w], ps[:, sb, 0:w], AF.Exp, [PSR(sb)], ["PR%d" % pslot], scale=0.125)
                        tt(pm[:, 0:w], pr[:, 0:w], mten[:, mc:mc + w], ALU.mult, ["PR%d" % pslot, mres], ["PM%d" % pslot])
                        oc = (qlo - 4 * g) * 128
                        tt(pacc[:, oc:oc + w], pacc[:, oc:oc + w], pm[:, 0:w], ALU.add, [pares, "PM%d" % pslot], [pares], eng="pool")

                        def pv_stage(T=T, pm=pm, pslot=pslot, w=w, oc=oc, ob=ob, db=db, vc0=vc0,
                                     first=(idx == 0), last=(idx == len(items) - 1)):
                            mm(ps[:, ob, oc:oc + w], Vt[:, T, vc0:vc0 + 128], pm[:, 0:w], first, last,
                               ["V:%d" % T, "PM%d" % pslot], [PSR(ob)])
                        pend.append(pv_stage)
                        while len(pend) > LAG:
                            pend.pop(0)()

                    def norm_stage(g=g, ob=ob, db=db, hs=hs, cq=cq, hp=hp, h_sw=h_sw, pacc=pacc, pares=pares):
                        rq = 0
                        r = scr(O_R, 512, F32)
                        mm(ps[:, db, :], cm(CM_ONE), pacc, True, True, ["CM", pares], [PSR(db)])
                        if kind == "swa":
                            act(r[hs, :], ps[hs, db, :], AF.Ln, [PSR(db), "ESK"], ["NR%d" % rq], bias=ESK[hs, h_sw:h_sw + 1])
                        else:
                            act(r[hs, :], ps[hs, db, :], AF.Ln, [PSR(db)], ["NR%d" % rq])
                        act(r[hs, :], r[hs, :], AF.Exp, ["NR%d" % rq], ["NR%d" % rq], scale=-1.0)
                        tt(yT[hs, branch * 2 + cq, g * 512:(g + 1) * 512], ps[hs, ob, :], r[hs, :], ALU.mult,
                           [PSR(ob), "NR%d" % rq], ["yT:%d:%d:%d" % (branch * 2 + cq, g, hp)])
                    pend.append(norm_stage)
                    if kind == "na" and g == 0 and cq * 2 + hp + 1 < 4:
                        em_load(cq * 2 + hp + 1, "E0")
                while pend:
                    pend.pop(0)()

    def branch_loads(kind, l, qbase):
        slot = next_wslot()
        W = WB[:, slot, :].rearrange("p (k n) -> p k n", k=8)
        if kind != "swa":
            wload(slot, W, w_in_d[l, :, qbase:qbase + 512].rearrange("(k p) n -> p k n", p=128))
        else:
            for bb in range(2):
                for a in range(2):
                    hh = a * 2 + bb
                    src = w_in_d[l, :, qbase + hh * 64:qbase + (hh + 1) * 64].rearrange("(k p) d -> p k d", p=128)
                    dst = W[:, :, bb * 128 + a * 64:bb * 128 + (a + 1) * 64]
                    wload(slot, dst, src)
            wload(slot, W[:, :, 256:384], w_in_d[l, :, qbase + 256:qbase + 384].rearrange("(k p) n -> p k n", p=128))
        slot2 = next_wslot()
        W2 = WB[:, slot2, :].rearrange("p (k n) -> p k n", k=8)
        if kind != "swa":
            wload(slot2, W2[:, :, 0:256], w_in_d[l, :, qbase + 512:qbase + 768].rearrange("(k p) n -> p k n", p=128))
        else:
            wload(slot2, W2[:, :, 0:128], w_in_d[l, :, qbase + 384:qbase + 512].rearrange("(k p) n -> p k n", p=128))
        return (slot, slot2)

    def branch_attn(kind, branch, l, slots, gq, gk, rope, next_loads=None):
        slot, slot2 = slots
        W = WB[:, slot, :].rearrange("p (k n) -> p k n", k=8)
        W2 = WB[:, slot2, :].rearrange("p (k n) -> p k n", k=8)
        nvc = 128 if kind == "swa" else 256
        specs = [(0, 0, gq), (128, 1, gq), (256, 2, gk)]
        if kind != "swa":
            specs.append((384, 3, gk))
        qk_norm_all(W, "WB%d" % slot, specs, rope)
        v_proj(W2, "WB%d" % slot2, nvc)
        barrier()
        nxt = next_loads() if next_loads is not None else None
        attention(kind, branch, l)
        return nxt

    O_MIX = 0
    O_G = 32768
    O_ACC = O_G + 2048
    O_MT = O_ACC + 2048
    mixT = scr(O_MIX, 8 * SEQ, BF16).rearrange("p (c t) -> p c t", c=8)

    def merge(l):
        YT_ALL = None
        for dc in range(8):
            slot = next_wslot()
            W = WB[:, slot, :].rearrange("p (k n d) -> p k n d", k=8, n=4)
            for n in range(4):
                wload(slot, W[:, :, n, :], w_in_d[l, :, n * 1024 + dc * 128:n * 1024 + (dc + 1) * 128].rearrange("(k p) d -> p k d", p=128))
            bs = dc % 2
            Wb = WBR[:, bs, :].rearrange("p (k n d) -> p k n d", k=2, n=4)
            for n in range(3):
                dma("pool", Wb[:, :, n, :], w_br_d[l, n, :, dc * 128:(dc + 1) * 128].rearrange("(k p) d -> p k d", p=128),
                    "wbr%d" % bs, [], ["WBR%d" % bs])
            for half in range(2):
                dma("pool", Wb[half * 64:(half + 1) * 64, :, 3, :],
                    w_br_d[l, 3, half * 128:(half + 1) * 128, dc * 128:(dc + 1) * 128].rearrange("(k p) d -> p k d", p=64),
                    "wbr%d" % bs, [], ["WBR%d" % bs])
            for tc in range(NTC):
                tsl = slice(tc * 512, (tc + 1) * 512)
                aq = 0
                acc = scr(O_ACC, 512, F32)
                for n in range(4):
                    bg, bb = nb(), nb()
                    for kc in range(8):
                        mm(ps[:, bg, :], W[:, kc, n, :], hT[:, kc, tsl], kc == 0, kc == 7,
                           ["WB%d" % slot, "hT:%d:%d" % (kc, tc)], [PSR(bg)])
                    for kc in range(2):
                        ych = n * 2 + kc
                        if n == 0:
                            yres = ["yT:%d:%d" % (ych, tc)]
                        else:
                            yres = ["yT:%d:%d:%d" % (ych, tc, hp) for hp in range(2)]
                        mm(ps[:, bb, :], Wb[:, kc, n, :], yT[:, ych, tsl], kc == 0, kc == 1,
                           ["WBR%d" % bs] + yres, [PSR(bb)])
                    gq = (tc * 4 + n) % 2
                    G = scr(O_G + gq * 1024, 512, BF16)
                    act(G, ps[:, bg, :], AF.Sigmoid, [PSR(bg), "PV"], ["G%d" % gq], bias=pvc(PV_GATEB + n * 8 + dc))
                    if n == 0:
                        tt(acc, ps[:, bb, :], G, ALU.mult, [PSR(bb), "G%d" % gq], ["ACC%d" % aq])
                    else:
                        mt = scr(O_MT, 512, F32)
                        tt(mt, ps[:, bb, :], G, ALU.mult, [PSR(bb), "G%d" % gq], ["MT"])
                        if n < 3:
                            tt(acc, acc, mt, ALU.add, ["ACC%d" % aq, "MT"], ["ACC%d" % aq])
                        else:
                            tt(mixT[:, dc, tsl], acc, mt, ALU.add, ["ACC%d" % aq, "MT"], ["mix:%d:%d" % (dc, tc)])
        for dg in range(2):
            slot = next_wslot()
            W = WB[:, slot, :].rearrange("p (k n) -> p k n", k=8)
            wload(slot, W, w_out_d[l, :, dg * 512:(dg + 1) * 512].rearrange("(k p) n -> p k n", p=128))
            for j in range(4):
                do = dg * 4 + j
                for tc in range(NTC):
                    tsl = slice(tc * 512, (tc + 1) * 512)
                    b = nb()
                    for kc in range(8):
                        mm(ps[:, b, :], W[:, kc, j * 128:(j + 1) * 128], mixT[:, kc, tsl], kc == 0, kc == 7,
                           ["WB%d" % slot, "mix:%d:%d" % (kc, tc)], [PSR(b)])
                    tt(xT[:, do, tsl], xT[:, do, tsl], ps[:, b, :], ALU.add, ["xT:%d:%d" % (do, tc), PSR(b)],
                       ["xT:%d:%d" % (do, tc)])

    O_U = 12288
    O_SG = O_U + 3 * 2080
    yT_flat = yT[:].rearrange("p c t -> p (c t)")

    def actT(j):
        if j < 16:
            return yT_flat[:, j * 1024:(j + 1) * 1024]
        return scr((j - 16) * 2048, 1024, BF16)

    def ffn(l):
        rmsnorm(PV_GFFN)
        barrier()
        ucount = [0]
        for th in range(2):
            t0 = th * 1024
            halo_tok = 1024 if th == 0 else 1023
            halo_col = 1025 if th == 0 else 0
            zero_col = 0 if th == 0 else 1025
            for u in range(3):
                U = scr(O_U + u * 2080, 1026, BF16)
                S.add("dve", lambda e, U=U, zc=zero_col: e.memset(U[:, zc:zc + 1], 0.0), writes=["U%d" % u])
            pendB = []
            sg_by_j = {}
            for jp in range(11):
                slot = next_wslot()
                W = WB[:, slot, :].rearrange("p (k a n) -> p k a n", k=8, a=2)
                for a in range(2):
                    wload(slot, W[:, :, a, :], w_up_d[l, :, a * D_FF + jp * 256: a * D_FF + (jp + 1) * 256].rearrange("(k p) n -> p k n", p=128))
                for jj in range(2):
                    j = jp * 2 + jj
                    for a in range(2):
                        u = ucount[0] % 3
                        ucount[0] += 1
                        U = scr(O_U + u * 2080, 1026, BF16)
                        ures = "U%d" % u
                        ch = a * 22 + j
                        wv = W[:, :, a, jj * 128:(jj + 1) * 128]
                        dgs = [make_diag(PV_FCW + ch * 3 + k) for k in range(3)]
                        for t2 in range(2):
                            b = nb()
                            tok = t0 + t2 * 512
                            for kc in range(8):
                                mm(ps[:, b, :], wv[:, kc, :], hT[:, kc, tok:tok + 512], kc == 0, kc == 7,
                                   ["WB%d" % slot, "hT:%d:%d" % (kc, tok // 512)], [PSR(b)])
                            act(U[:, 1 + t2 * 512:1 + (t2 + 1) * 512], ps[:, b, :], AF.Copy, [PSR(b)], [ures])
                        if th == 0:
                            b = nb()
                            for kc in range(8):
                                mm(ps[:, b, 0:1], wv[:, kc, :], hT[:, kc, 1024:1025], kc == 0, kc == 7,
                                   ["WB%d" % slot, "hT:%d:%d" % (kc, 2)], [PSR(b)])
                            S.add("dve", lambda e, U=U, b=b: e.tensor_copy(out=U[:, 1025:1026], in_=ps[:, b, 0:1]),
                                  reads=[PSR(b)], writes=[ures])
                            S.add("dve", lambda e, U=U, ch=ch: e.tensor_copy(out=HS[:, ch:ch + 1], in_=U[:, 1024:1025]),
                                  reads=[ures], writes=["HS%d" % ch])
                        else:
                            S.add("dve", lambda e, U=U, ch=ch: e.tensor_copy(out=U[:, 0:1], in_=HS[:, ch:ch + 1]),
                                  reads=["HS%d" % ch], writes=[ures])

                        def stageB(U=U, ures=ures, ch=ch, a=a, j=j, dgs=dgs):
                            for t2 in range(2):
                                b = nb()
                                for k in range(3):
                                    mm(ps[:, b, :], dgs[k][0], U[:, t2 * 512 + k:t2 * 512 + k + 512], k == 0, k == 2,
                                       [dgs[k][1], ures], [PSR(b)])
                                if a == 0:
                                    sq = t2
                                    sg = scr(O_SG + sq * 1024, 512, BF16)
                                    act(sg, ps[:, b, :], AF.Silu, [PSR(b), "PV"], ["SG%d" % sq], bias=pvc(PV_FCB + ch))
                                else:
                                    sg = scr(O_SG + t2 * 1024, 512, BF16)
                                    stt(actT(j)[:, t2 * 512:(t2 + 1) * 512], ps[:, b, :], pvc(PV_FCB + ch), sg, ALU.add, ALU.mult,
                                        [PSR(b), "PV", "SG%d" % t2], ["act:%d:%d" % (j, t2)])
                        pendB.append(stageB)
                        while len(pendB) > 1:
                            pendB.pop(0)()
            while pendB:
                pendB.pop(0)()
            for do in range(8):
                slot = next_wslot()
                W = WB[:, slot, 0:22 * 128].rearrange("p (k n) -> p k n", k=22)
                for hh in range(2):
                    wload(slot, W[:, hh * 11:(hh + 1) * 11, :],
                          w_dn_d[l, hh * 1408:(hh + 1) * 1408, do * 128:(do + 1) * 128].rearrange("(k p) n -> p k n", p=128))
                for t2 in range(2):
                    b = nb()
                    tok = t0 + t2 * 512
                    for j in range(22):
                        mm(ps[:, b, :], W[:, j, :], actT(j)[:, t2 * 512:(t2 + 1) * 512], j == 0, j == 21,
                           ["WB%d" % slot, "act:%d:%d" % (j, t2)], [PSR(b)])
                    tt(xT[:, do, tok:tok + 512], xT[:, do, tok:tok + 512], ps[:, b, :], ALU.add,
                       ["xT:%d:%d" % (do, tok // 512), PSR(b)], ["xT:%d:%d" % (do, tok // 512)])

    SCR_ALL = "SCRALL"


    scr_names_A0 = ["SQ0", "SQ1", "SQ2", "RS0", "RS1", "XST0", "XST1"]
    scr_names_conf = ["UPpad0", "UPpad1", "SIG0", "SIG1", "MSQ", "LRS", "LT0", "LT1"] + \
        ["UP:%d:%d" % (c, t) for c in range(2) for t in range(NTC)] + \
        ["CVF%d%d" % (q, c) for q in range(2) for c in range(2)] + ["CVB%d%d" % (q, c) for q in range(2) for c in range(2)] + \
        ["SQB%d%d" % (q, c) for q in range(2) for c in range(2)]
    scr_names_attn = ["qk:%d:%d" % (c, t) for c in range(4) for t in range(NTC)] + ["V:%d" % T for T in range(NT)] + \
        ["QSQ0", "QSQ1", "QSQ2", "QRS0", "QRS1", "QN0", "QN1", "RT1", "RT2", "EMI0", "EMI1", "EME0", "EME3", "PR0", "PR1", "PR2", "PR3", "PM0", "PM1", "PM2", "PM3", "NR0", "NR1"]
    scr_names_merge = ["mix:%d:%d" % (c, t) for c in range(8) for t in range(NTC)] + ["G0", "G1", "ACC0", "ACC1", "MT"]
    scr_names_ffn = ["U0", "U1", "U2", "SG0", "SG1"] + ["act:%d:%d" % (j, t) for j in range(22) for t in range(2)]
    yT_names = ["yT:%d:%d" % (c, t) for c in range(2) for t in range(NTC)] + \
        ["yT:%d:%d:%d" % (c, t, hp) for c in range(2, 8) for t in range(NTC) for hp in range(2)]
    ALLSCR = scr_names_A0 + scr_names_conf + scr_names_attn + scr_names_merge + scr_names_ffn

    def barrier(extra=()):
        names = ALLSCR + list(extra)
        S.add("dve", lambda e: e.memset(DUM[:, 0:1], 0.0), reads=names, writes=names)

    for s in range(nseq):
        barrier(yT_names)
        load_x(s)
        for li in range(nl):
            l = l0 + li
            dma("sp", PV[:], pvec_d[l], "pvl", [], ["PV"])
            act(ESK[:], PV[:, PV_SINK:PV_SINK + 4], AF.Exp, ["PV"], ["ESK"])
            barrier()
            rmsnorm(PV_GMIX)
            barrier()
            conformer(l)
            barrier()
            sl_na = branch_loads("na", l, 4608)
            sl_dil = branch_attn("na", 1, l, sl_na, PV_QKG + 0, PV_QKG + 1, False,
                                 next_loads=lambda l=l: branch_loads("dil", l, 5376))
            barrier()
            sl_swa = branch_attn("dil", 2, l, sl_dil, PV_QKG + 2, PV_QKG + 3, True,
                                 next_loads=lambda l=l: branch_loads("swa", l, 6144))
            barrier()
            branch_attn("swa", 3, l, sl_swa, PV_QKG + 4, PV_QKG + 5, True)
            barrier()
            merge(l)
            barrier(yT_names)
            ffn(l)
            barrier(yT_names)
        store_x(s)
    S.add("sp", lambda e: None, reads=["OUT0", "OUT1"], writes=[])

    S.finalize()
    with contextlib.ExitStack() as st:
        sems = {}
        for e in ENGS:
            sems[e] = st.enter_context(nc.semaphore("s_" + e))
        for k in S.dcounts:
            sems[k] = st.enter_context(nc.semaphore("d_" + k))
        block = st.enter_context(nc.Block())
        block.sync(lambda e: S.run_engine("sp", e, sems))
        block.scalar(lambda e: S.run_engine("act", e, sems))
        block.vector(lambda e: S.run_engine("dve", e, sems))
        block.gpsimd(lambda e: S.run_engine("pool", e, sems))
        block.tensor(lambda e: S.run_engine("pe", e, sems))
    return nc, S


def _const_mats():
    kk = np.arange(128)[:, None]
    qq = np.arange(128)[None, :]
    cmat = np.zeros((128, NCM), np.float32)
    cmat[:, CM_WIN:CM_WIN + 128] = (kk <= qq)
    cmat[:, CM_WIN + 128:CM_WIN + 256] = 1.0
    cmat[:, CM_WIN + 256:CM_WIN + 384] = (kk >= qq)
    for d in range(-8, 9):
        o = 128 * d + kk - qq
        m = (np.abs(o) <= 64).astype(np.float32)
        m += ((o % 4 == 0) & (np.abs(o) <= 256)).astype(np.float32)
        m += ((o % 16 == 0) & (np.abs(o) <= 1024)).astype(np.float32)
        cmat[:, CM_DIL + (8 - d) * 128: CM_DIL + (9 - d) * 128] = m
    cmat[:, CM_ID:CM_ID + 128] = np.eye(128, dtype=np.float32)
    cmat[:, CM_O1024:CM_O1024 + 128] = 1.0 / 1024.0
    blk = np.zeros((128, 128), np.float32)
    blk[0:64, 0:64] = 1.0 / 64.0
    blk[64:128, 64:128] = 1.0 / 64.0
    cmat[:, CM_B64:CM_B64 + 128] = blk
    cmat[:, CM_ONE:CM_ONE + 128] = 1.0
    rot = np.zeros((128, 128), np.float32)
    for hb in range(2):
        for d in range(8):
            rot[hb * 64 + d + 8, hb * 64 + d] = -1.0
            rot[hb * 64 + d, hb * 64 + d + 8] = 1.0
    cmat[:, CM_ROT:CM_ROT + 128] = rot
    cmat[:, CM_O256:CM_O256 + 128] = 1.0 / 256.0
    pos = np.arange(SEQ, dtype=np.float32)
    inv = (np.float32(500000.0) ** (-np.arange(0, 16, 2, dtype=np.float32) / np.float32(16))).astype(np.float32)
    ang = (pos[:, None] * inv[None, :]).astype(np.float32)
    cosT = np.cos(ang).astype(np.float32).T
    sinT = np.sin(ang).astype(np.float32).T
    rc = np.ones((128, SEQ), np.float32)
    rs = np.zeros((128, SEQ), np.float32)
    for hb in range(2):
        for d in range(16):
            rc[hb * 64 + d] = cosT[d % 8]
            rs[hb * 64 + d] = sinT[d % 8]
    return cmat, rc, rs, np.eye(128, dtype=np.float32)


def _na_index():
    blocks = [(2 + d, 2) for d in (2, 1, 0, -1, -2)]
    blocks += [(T, qt) for T in range(4) for qt in (0, 1)]
    blocks += [(T, qt) for T in range(12, 16) for qt in (14, 15)]
    dr = np.zeros((128, NA_COLS), np.int64)
    dc = np.zeros((128, NA_COLS), np.int64)
    valid = np.zeros((128, NA_COLS), bool)
    kk = np.arange(128)[:, None]
    qq = np.arange(128)[None, :]
    for i, (T, qt) in enumerate(blocks):
        col = i * 128
        qtok = qt * 128 + qq
        r = qtok // 64
        c = qtok % 64
        rs = np.clip(r - 4, 0, 24)
        cs = np.clip(c - 8, 0, 48)
        ktok = T * 128 + kk
        kr = ktok // 64
        kc = ktok % 64
        v = (kr >= rs) & (kr < rs + 8) & (kc >= cs) & (kc < cs + 16)
        dr[:, col:col + 128] = np.where(v, kr - r + 7, 0)
        dc[:, col:col + 128] = np.where(v, kc - c + 15, 0)
        valid[:, col:col + 128] = v
    return dr, dc, valid


def _host_layout(inp):
    f = np.float32
    L = DEPTH
    pv = np.zeros((L, 128, NPV), f)
    pv[:, :, PV_GMIX:PV_GMIX + 8] = inp["g_mix"].reshape(L, 8, 128).transpose(0, 2, 1)
    pv[:, :, PV_GFFN:PV_GFFN + 8] = inp["g_ffn"].reshape(L, 8, 128).transpose(0, 2, 1)
    pv[:, :, PV_GATEB:PV_GATEB + 32] = inp["gate_b"].reshape(L, 4, 8, 128).transpose(0, 3, 1, 2).reshape(L, 128, 32)
    pv[:, :, PV_ACW:PV_ACW + 62] = inp["a_conv_w"].reshape(L, 31, 2, 128).transpose(0, 3, 2, 1).reshape(L, 128, 62)
    pv[:, :, PV_ACB:PV_ACB + 2] = inp["a_conv_b"].reshape(L, 2, 128).transpose(0, 2, 1)
    pv[:, :, PV_ALNG:PV_ALNG + 2] = inp["a_ln_g"].reshape(L, 2, 128).transpose(0, 2, 1)
    pv[:, :, PV_ALNB:PV_ALNB + 2] = inp["a_ln_b"].reshape(L, 2, 128).transpose(0, 2, 1)
    for i, k in enumerate(["na_qn", "na_kn", "dil_qn", "dil_kn", "swa_qn", "swa_kn"]):
        pv[:, :, PV_QKG + i] = np.tile(inp[k], (1, 2))
    pv[:, :, PV_SINK:PV_SINK + 4] = inp["swa_sink"][:, None, :]
    pv[:, :, PV_FCW:PV_FCW + 132] = inp["ffn_conv_w"].reshape(L, 3, 44, 128).transpose(0, 3, 2, 1).reshape(L, 128, 132)
    pv[:, :, PV_FCB:PV_FCB + 44] = inp["ffn_conv_b"].reshape(L, 44, 128).transpose(0, 2, 1)
    dr, dc, valid = _na_index()
    rpb = inp["na_rpb"]
    g = rpb[:, :, dr, dc]
    g = np.where(valid[None, None], g, f(-30000.0)).astype(f)
    nab = np.ascontiguousarray(g.transpose(0, 2, 1, 3).reshape(L, 128, 4 * NA_COLS))
    return pv, nab


_PROG_CACHE = {}


def _get_prog(nl, nseq, l0):
    key = (nl, nseq, l0)
    if key not in _PROG_CACHE:
        _PROG_CACHE[key] = build_program(nl, nseq, l0)[0]
    return _PROG_CACHE[key]


def run_layers(inp, x, nl, l0, ncores, nseq):
    cmat, rc, rs, idf = _const_mats()
    pv, nab = _host_layout(inp)
    nc = _get_prog(nl, nseq, l0)
    c32 = lambda a: np.ascontiguousarray(np.asarray(a, dtype=np.float32))
    shared = {
        "w_in": c32(inp["w_in"]), "w_branch": c32(inp["w_branch"]), "w_out": c32(inp["w_out"]),
        "w_up": c32(inp["w_up"]), "w_down": c32(inp["w_down"]), "pvec": pv, "nab": nab,
        "cmat": cmat, "ropec": rc, "ropes": rs, "identf": idf,
    }
    in_maps = []
    for c in range(ncores):
        m = dict(shared)
        m["x"] = np.ascontiguousarray(x[c * nseq:(c + 1) * nseq])
        in_maps.append(m)
    res = run_bass_kernel_spmd(nc, in_maps, core_ids=list(range(ncores)))
    return np.concatenate([np.asarray(r["out"]) for r in res.results], axis=0)


def kernel(**inputs):
    inp = {k: np.asarray(v) for k, v in inputs.items()}
    x = np.ascontiguousarray(inp["x"].astype(np.float32))
    out = run_layers(inp, x, DEPTH, 0, 8, 2)
    return out.astype(np.float32)
```

```python
import contextlib
import numpy as np
import concourse.bass as bass
import concourse.mybir as mybir
from concourse.bass_utils import run_bass_kernel_spmd

F32 = mybir.dt.float32
BF16 = mybir.dt.bfloat16
AF = mybir.ActivationFunctionType
ALU = mybir.AluOpType

D_MODEL = 1024
SEQ = 2048
DEPTH = 4
N_IN = 6656
D_FF = 2816
EPS = 1e-6
NT = 16
NTC = 4
ENGS = ("pe", "act", "dve", "pool", "sp")

PV_GMIX = 0
PV_GFFN = 8
PV_GATEB = 16
PV_ACW = 48
PV_ACB = 110
PV_ALNG = 112
PV_ALNB = 114
PV_QKG = 116
PV_SINK = 122
PV_FCW = 126
PV_FCB = 258
NPV = 304

CM_WIN = 0
CM_DIL = 384
CM_ID = CM_DIL + 17 * 128
CM_O1024 = CM_ID + 128
CM_B64 = CM_O1024 + 128
CM_ONE = CM_B64 + 128
CM_ROT = CM_ONE + 128
CM_O256 = CM_ROT + 128
NCM = CM_O256 + 128

NA_COLS = 2688


class _Op:
    __slots__ = ("eng", "fn", "deps", "dma_sem")


class Sched:
    def __init__(self):
        self.ops = []
        self.last_w = {}
        self.readers = {}

    def add(self, eng, fn, reads=(), writes=(), dma_sem=None):
        op = _Op()
        op.eng = eng
        op.fn = fn
        op.dma_sem = dma_sem
        deps = set()
        lw = self.last_w
        rd = self.readers
        for r in reads:
            w = lw.get(r)
            if w is not None:
                deps.add(w)
        for r in writes:
            w = lw.get(r)
            if w is not None:
                deps.add(w)
            x = rd.get(r)
            if x:
                deps.update(x.values())
        idx = len(self.ops)
        deps.discard(idx)
        op.deps = deps
        self.ops.append(op)
        for r in writes:
            lw[r] = idx
            rd[r] = {}
        rkey = eng if dma_sem is None else (eng, idx)
        for r in reads:
            if r in writes:
                continue
            l = rd.get(r)
            if l is None:
                rd[r] = {rkey: idx}
            else:
                l[rkey] = idx
        return idx

    def finalize(self):
        ops = self.ops
        n = len(ops)
        needs_inc = [False] * n
        for i, op in enumerate(ops):
            for d in op.deps:
                od = ops[d]
                if od.dma_sem is None and od.eng != op.eng:
                    needs_inc[d] = True
        cnt = {e: 0 for e in ENGS}
        dcnt = {}
        sig = [None] * n
        for i, op in enumerate(ops):
            if op.dma_sem is not None:
                dcnt[op.dma_sem] = dcnt.get(op.dma_sem, 0) + 16
                sig[i] = (op.dma_sem, dcnt[op.dma_sem])
            elif needs_inc[i]:
                cnt[op.eng] += 1
                sig[i] = (op.eng, cnt[op.eng])
        eng_vc = {e: {} for e in ENGS}
        op_vc = [None] * n
        waits = [None] * n
        for i, op in enumerate(ops):
            vc = eng_vc[op.eng]
            w = {}
            for d in sorted(op.deps):
                s = sig[d]
                if s is None:
                    continue
                if vc.get(s[0], 0) >= s[1]:
                    continue
                if w.get(s[0], 0) < s[1]:
                    w[s[0]] = s[1]
                dv = op_vc[d]
                for k2, v2 in dv.items():
                    if vc.get(k2, 0) < v2:
                        vc[k2] = v2
                vc[s[0]] = s[1]
            waits[i] = w
            if sig[i] is not None:
                snap = dict(vc)
                snap[sig[i][0]] = sig[i][1]
                op_vc[i] = snap
                if op.dma_sem is None:
                    vc[sig[i][0]] = sig[i][1]
        self.sig = sig
        self.waits = waits
        self.counts = cnt
        self.dcounts = dcnt

    def run_engine(self, ename, eng, sems):
        ops = self.ops
        sig = self.sig
        waits = self.waits
        for i, op in enumerate(ops):
            if op.eng != ename:
                continue
            for key, val in waits[i].items():
                eng.wait_ge(sems[key], val)
            ins = op.fn(eng)
            s = sig[i]
            if s is not None:
                ins.then_inc(sems[s[0]], 16 if op.dma_sem is not None else 1)


def build_program(nl=DEPTH, nseq=2, l0=0):
    nc = bass.Bass("TRN2", target_bir_lowering=False)
    x_d = nc.dram_tensor("x", [nseq, SEQ, D_MODEL], F32, kind="ExternalInput").ap()
    w_in_d = nc.dram_tensor("w_in", [DEPTH, D_MODEL, N_IN], F32, kind="ExternalInput").ap()
    w_br_d = nc.dram_tensor("w_branch", [DEPTH, 4, 256, D_MODEL], F32, kind="ExternalInput").ap()
    w_out_d = nc.dram_tensor("w_out", [DEPTH, D_MODEL, D_MODEL], F32, kind="ExternalInput").ap()
    w_up_d = nc.dram_tensor("w_up", [DEPTH, D_MODEL, 2 * D_FF], F32, kind="ExternalInput").ap()
    w_dn_d = nc.dram_tensor("w_down", [DEPTH, D_FF, D_MODEL], F32, kind="ExternalInput").ap()
    pvec_d = nc.dram_tensor("pvec", [DEPTH, 128, NPV], F32, kind="ExternalInput").ap()
    nab_d = nc.dram_tensor("nab", [DEPTH, 128, 4 * NA_COLS], F32, kind="ExternalInput").ap()
    cm_d = nc.dram_tensor("cmat", [128, NCM], F32, kind="ExternalInput").ap()
    rc_d = nc.dram_tensor("ropec", [128, SEQ], F32, kind="ExternalInput").ap()
    rs_d = nc.dram_tensor("ropes", [128, SEQ], F32, kind="ExternalInput").ap()
    idf_d = nc.dram_tensor("identf", [128, 128], F32, kind="ExternalInput").ap()
    out_d = nc.dram_tensor("out", [nseq, SEQ, D_MODEL], F32, kind="ExternalOutput").ap()

    xT = nc.alloc_sbuf_tensor("xT", [128, 8, SEQ], F32)
    hT = nc.alloc_sbuf_tensor("hT", [128, 8, SEQ], BF16)
    yT = nc.alloc_sbuf_tensor("yT", [128, 8, SEQ], BF16)
    SCR_N = 20736
    SCR = nc.alloc_sbuf_tensor("scr", [128, SCR_N], BF16)
    WB = nc.alloc_sbuf_tensor("wb", [128, 2, 4096], BF16)
    WBR = nc.alloc_sbuf_tensor("wbr", [128, 2, 1024], BF16)
    ropeC = nc.alloc_sbuf_tensor("ropeC", [128, SEQ], BF16)
    ropeS = nc.alloc_sbuf_tensor("ropeS", [128, SEQ], BF16)
    CM = nc.alloc_sbuf_tensor("cm", [128, NCM], BF16)
    identF = nc.alloc_sbuf_tensor("identF", [128, 128], F32)
    PV = nc.alloc_sbuf_tensor("pv", [128, NPV], F32)
    ESK = nc.alloc_sbuf_tensor("esk", [128, 4], F32)
    DG = nc.alloc_sbuf_tensor("dg", [128, 6, 128], BF16)
    DUM = nc.alloc_sbuf_tensor("dum", [128, 16], BF16)
    HS = nc.alloc_sbuf_tensor("hs", [128, 48], BF16)
    ps = nc.alloc_psum_tensor("ps", [128, 8, 512], F32)

    S = Sched()

    def scr(off_b, n_el, dt):
        assert off_b % 4 == 0
        if dt is BF16:
            assert off_b // 2 + n_el <= SCR_N, (off_b, n_el)
            return SCR[:, off_b // 2: off_b // 2 + n_el]
        assert off_b // 2 + 2 * n_el <= SCR_N, (off_b, n_el)
        return SCR[:, off_b // 2: off_b // 2 + 2 * n_el].bitcast(F32)

    ring = [0]

    def nb():
        b = ring[0]
        ring[0] = (b + 1) % 8
        return b

    def PSR(b):
        return "ps%d" % b

    def cm(c0, n=128):
        return CM[:, c0:c0 + n]

    def pvc(c):
        return PV[:, c:c + 1]

    def mm(out, lhsT, rhs, start, stop, reads, writes):
        S.add("pe", lambda e: e.matmul(out, lhsT, rhs, start=start, stop=stop), reads=reads, writes=writes)

    def act(out, in_, func, reads, writes, bias=None, scale=None):
        kw = {}
        if bias is not None:
            kw["bias"] = bias
        if scale is not None:
            kw["scale"] = scale
        S.add("act", lambda e: e.activation(out=out, in_=in_, func=func, **kw), reads=reads, writes=writes)

    def tt(out, in0, in1, op, reads, writes, eng="dve"):
        S.add(eng, lambda e: e.tensor_tensor(out=out, in0=in0, in1=in1, op=op), reads=reads, writes=writes)

    def ts(out, in0, s1, s2, op0, op1, reads, writes, eng="dve"):
        if op1 is None:
            S.add(eng, lambda e: e.tensor_scalar(out=out, in0=in0, scalar1=s1, scalar2=None, op0=op0),
                  reads=reads, writes=writes)
        else:
            S.add(eng, lambda e: e.tensor_scalar(out=out, in0=in0, scalar1=s1, scalar2=s2, op0=op0, op1=op1),
                  reads=reads, writes=writes)

    def rsqrt_eps(out, in_, reads, writes):
        act(out, in_, AF.Ln, reads, writes, bias=EPS, scale=1.0)
        act(out, out, AF.Exp, list(writes), list(writes), scale=-0.5)

    def stt(out, in0, scalar, in1, op0, op1, reads, writes, eng="dve"):
        S.add(eng, lambda e: e.scalar_tensor_tensor(out=out, in0=in0, scalar=scalar, in1=in1, op0=op0, op1=op1),
              reads=reads, writes=writes)

    def dma(eng, out, in_, sem, reads, writes):
        S.add(eng, lambda e: e.dma_start(out=out, in_=in_), reads=reads, writes=writes, dma_sem=sem)

    dma("pool", CM[:], cm_d, "c0", [], ["CM"])
    dma("pool", ropeC[:], rc_d, "c1", [], ["ropeC"])
    dma("pool", ropeS[:], rs_d, "c2", [], ["ropeS"])
    dma("sp", identF[:], idf_d, "c3", [], ["identF"])

    HT_ALL = ["hT:%d:%d" % (kc, tc) for kc in range(8) for tc in range(NTC)]

    def ht_tc(tc):
        return ["hT:%d:%d" % (kc, tc) for kc in range(8)]

    wslot = [0]

    def next_wslot():
        s = wslot[0]
        wslot[0] = 1 - s
        return s

    def wload(slot, dst, src):
        dma("pool", dst, src, "w%d" % slot, [], ["WB%d" % slot])

    dgslot = [0]

    def make_diag(col):
        s = dgslot[0]
        dgslot[0] = (s + 1) % 6
        tt(DG[:, s, :], cm(CM_ID), PV[:, col:col + 1].to_broadcast([128, 128]), ALU.mult, ["CM", "PV"], ["DG%d" % s])
        return DG[:, s, :], "DG%d" % s

    def load_x(s):
        for T in range(NT):
            sl = T % 2
            st = scr(sl * 4096, 1024, F32)
            dma("sp", st, x_d[s, T * 128:(T + 1) * 128, :], "xs%d" % sl, [], ["XST%d" % sl])
            for half in range(2):
                b = nb()
                for j in range(4):
                    kc = half * 4 + j
                    mm(ps[:, b, j * 128:(j + 1) * 128], st[:, kc * 128:(kc + 1) * 128], identF[:], True, True,
                       ["XST%d" % sl, "identF"], [PSR(b)])
                eng = "dve" if half == 0 else "act"
                dst = xT[:, half * 4:(half + 1) * 4, T * 128:(T + 1) * 128]
                src = ps[:, b, :].rearrange("p (j t) -> p j t", j=4)
                wr = ["xT:%d:%d" % (half * 4 + j, T // 4) for j in range(4)]
                if eng == "dve":
                    S.add("dve", lambda e, dst=dst, src=src: e.tensor_copy(out=dst, in_=src), reads=[PSR(b)], writes=wr)
                else:
                    act(dst, src, AF.Copy, [PSR(b)], wr)

    def store_x(s):
        for T in range(NT):
            sl = T % 2
            st = scr(sl * 4096, 1024, F32)
            for half in range(2):
                b = nb()
                for j in range(4):
                    kc = half * 4 + j
                    mm(ps[:, b, j * 128:(j + 1) * 128], xT[:, kc, T * 128:(T + 1) * 128], identF[:], True, True,
                       ["xT:%d:%d" % (kc, T // 4), "identF"], [PSR(b)])
                dst = st[:, half * 512:(half + 1) * 512]
                if half == 0:
                    S.add("dve", lambda e, dst=dst, b=b: e.tensor_copy(out=dst, in_=ps[:, b, :]), reads=[PSR(b)],
                          writes=["XST%d" % sl])
                else:
                    act(dst, ps[:, b, :], AF.Copy, [PSR(b), "XST%d" % sl], ["XST%d" % sl])
            dma("sp", out_d[s, T * 128:(T + 1) * 128, :], st, "xo%d" % sl, ["XST%d" % sl], ["OUT%d" % sl])

    def rmsnorm(gcol):
        for tc in range(NTC):
            b = nb()
            tsl = slice(tc * 512, (tc + 1) * 512)
            for kc in range(8):
                q = kc % 3
                sq = scr(q * 1024, 512, BF16)
                act(sq, xT[:, kc, tsl], AF.Square, ["xT:%d:%d" % (kc, tc)], ["SQ%d" % q])
                mm(ps[:, b, :], cm(CM_O1024), sq, kc == 0, kc == 7, ["SQ%d" % q, "CM"], [PSR(b)])
            r = tc % 2
            rstd = scr(3072 + r * 2048, 512, F32)
            rsqrt_eps(rstd, ps[:, b, :], [PSR(b)], ["RS%d" % r])
            for kc in range(8):
                stt(hT[:, kc, tsl], xT[:, kc, tsl], pvc(gcol + kc), rstd, ALU.mult, ALU.mult,
                    ["xT:%d:%d" % (kc, tc), "PV", "RS%d" % r], ["hT:%d:%d" % (kc, tc)])

    def conformer(l):
        slot = next_wslot()
        W = WB[:, slot, :].rearrange("p (k n) -> p k n", k=8)
        wload(slot, W, w_in_d[l, :, 4096:4608].rearrange("(k p) n -> p k n", p=128))
        UP = scr(0, 2 * 2080, BF16).rearrange("p (c t) -> p c t", c=2)
        o_sig = 8320
        o_cvf = o_sig + 4096
        o_cvb = o_cvf + 8192
        o_sqb = o_cvb + 4096
        o_st = o_sqb + 4096
        o_t = o_st + 4096
        for cc in range(2):
            S.add("dve", lambda e, cc=cc: e.memset(UP[:, cc, 0:15], 0.0), writes=["UPpad%d" % cc])
            S.add("dve", lambda e, cc=cc: e.memset(UP[:, cc, 15 + SEQ:30 + SEQ], 0.0), writes=["UPpad%d" % cc])
        for cc in range(2):
            for tc in range(NTC):
                tsl = slice(tc * 512, (tc + 1) * 512)
                ba, bg = nb(), nb()
                for kc in range(8):
                    mm(ps[:, ba, :], W[:, kc, cc * 128:(cc + 1) * 128], hT[:, kc, tsl], kc == 0, kc == 7,
                       ["WB%d" % slot, "hT:%d:%d" % (kc, tc)], [PSR(ba)])
                for kc in range(8):
                    mm(ps[:, bg, :], W[:, kc, 256 + cc * 128:256 + (cc + 1) * 128], hT[:, kc, tsl], kc == 0, kc == 7,
                       ["WB%d" % slot, "hT:%d:%d" % (kc, tc)], [PSR(bg)])
                q = tc % 2
                sig = scr(o_sig + q * 2048, 512, F32)
                act(sig, ps[:, bg, :], AF.Sigmoid, [PSR(bg)], ["SIG%d" % q])
                tt(UP[:, cc, 15 + tc * 512:15 + (tc + 1) * 512], ps[:, ba, :], sig, ALU.mult,
                   [PSR(ba), "SIG%d" % q], ["UP:%d:%d" % (cc, tc)])
        up_all = ["UP:%d:%d" % (cc, tc) for cc in range(2) for tc in range(NTC)] + ["UPpad0", "UPpad1"]
        for cc in range(2):
            for k in range(31):
                dg, dres = make_diag(PV_ACW + cc * 31 + k)
                for tc in range(NTC):
                    b = cc * 4 + tc
                    mm(ps[:, b, :], dg, UP[:, cc, tc * 512 + k: tc * 512 + k + 512], k == 0, k == 30,
                       [dres] + up_all, [PSR(b)])
        ring[0] = 0
        for tc in range(NTC):
            tsl = slice(tc * 512, (tc + 1) * 512)
            q = tc % 2
            for cc in range(2):
                b = cc * 4 + tc
                cvf = scr(o_cvf + (q * 2 + cc) * 2048, 512, F32)
                cvb = scr(o_cvb + (q * 2 + cc) * 1024, 512, BF16)
                sqb = scr(o_sqb + (q * 2 + cc) * 1024, 512, BF16)
                act(cvf, ps[:, b, :], AF.Identity, [PSR(b), "PV"], ["CVF%d%d" % (q, cc)], bias=pvc(PV_ACB + cc))
                act(cvb, ps[:, b, :], AF.Identity, [PSR(b), "PV"], ["CVB%d%d" % (q, cc)], bias=pvc(PV_ACB + cc))
                act(sqb, ps[:, b, :], AF.Square, [PSR(b), "PV"], ["SQB%d%d" % (q, cc)], bias=pvc(PV_ACB + cc))
            b1 = tc
            b2 = 4 + tc
            for cc in range(2):
                cvb = scr(o_cvb + (q * 2 + cc) * 1024, 512, BF16)
                mm(ps[:, b1, :], cm(CM_O256), cvb, cc == 0, cc == 1, ["CM", "CVB%d%d" % (q, cc)], [PSR(b1)])
            for cc in range(2):
                sqb = scr(o_sqb + (q * 2 + cc) * 1024, 512, BF16)
                mm(ps[:, b2, :], cm(CM_O256), sqb, cc == 0, cc == 1, ["CM", "SQB%d%d" % (q, cc)], [PSR(b2)])
            msq = scr(o_st, 512, F32)
            rstd = scr(o_st + 2048, 512, F32)
            act(msq, ps[:, b1, :], AF.Square, [PSR(b1)], ["MSQ"])
            tt(msq, ps[:, b2, :], msq, ALU.subtract, [PSR(b2), "MSQ"], ["MSQ"])
            rsqrt_eps(rstd, msq, ["MSQ"], ["LRS"])
            for cc in range(2):
                cvf = scr(o_cvf + (q * 2 + cc) * 2048, 512, F32)
                t = scr(o_t + cc * 2048, 512, F32)
                tt(t, cvf, ps[:, b1, :], ALU.subtract, ["CVF%d%d" % (q, cc), PSR(b1)], ["LT%d" % cc])
                tt(t, t, rstd, ALU.mult, ["LT%d" % cc, "LRS"], ["LT%d" % cc])
                act(yT[:, cc, tsl], t, AF.Silu, ["LT%d" % cc, "PV"], ["yT:%d:%d" % (cc, tc)],
                    bias=pvc(PV_ALNB + cc), scale=pvc(PV_ALNG + cc))

    O_QK = 0
    O_V = 16384
    O_TMP = 24576
    qkT = scr(O_QK, 4 * SEQ, BF16).rearrange("p (c t) -> p c t", c=4)
    Vt = scr(O_V, 16 * 256, BF16).rearrange("p (t c) -> p t c", t=16)

    def qk_norm_all(W, wres, specs, rope):
        units = [(wc0, dst, gcol, tc) for (wc0, dst, gcol) in specs for tc in range(NTC)]
        o_sq, o_rs, o_qn, o_t1, o_t2 = O_TMP, O_TMP + 3072, O_TMP + 7168, O_TMP + 9216, O_TMP + 11264
        st = {}

        def s1(u):
            wc0, dst, gcol, tc = units[u]
            tsl = slice(tc * 512, (tc + 1) * 512)
            b = nb()
            for kc in range(8):
                mm(ps[:, b, :], W[:, kc, wc0:wc0 + 128], hT[:, kc, tsl], kc == 0, kc == 7,
                   [wres, "hT:%d:%d" % (kc, tc)], [PSR(b)])
            q = u % 3
            sq = scr(o_sq + q * 1024, 512, BF16)
            act(sq, ps[:, b, :], AF.Square, [PSR(b)], ["QSQ%d" % q])
            st[u] = (b, sq, q)

        def s2(u):
            wc0, dst, gcol, tc = units[u]
            tsl = slice(tc * 512, (tc + 1) * 512)
            b, sq, q = st[u]
            b2 = nb()
            mm(ps[:, b2, :], cm(CM_B64), sq, True, True, ["CM", "QSQ%d" % q], [PSR(b2)])
            r = u % 2
            rs = scr(o_rs + r * 2048, 512, F32)
            rsqrt_eps(rs, ps[:, b2, :], [PSR(b2)], ["QRS%d" % r])
            if not rope:
                stt(qkT[:, dst, tsl], ps[:, b, :], pvc(gcol), rs, ALU.mult, ALU.mult,
                    [PSR(b), "PV", "QRS%d" % r], ["qk:%d:%d" % (dst, tc)])
            else:
                qn = scr(o_qn + r * 1024, 512, BF16)
                stt(qn, ps[:, b, :], pvc(gcol), rs, ALU.mult, ALU.mult, [PSR(b), "PV", "QRS%d" % r], ["QN%d" % r])
                st[u] = (qn, r)

        def s3(u):
            wc0, dst, gcol, tc = units[u]
            tsl = slice(tc * 512, (tc + 1) * 512)
            qn, r = st[u]
            b3 = nb()
            mm(ps[:, b3, :], cm(CM_ROT), qn, True, True, ["CM", "QN%d" % r], [PSR(b3)])
            t1 = scr(o_t1, 512, F32)
            t2 = scr(o_t2, 512, F32)
            tt(t2, qn, ropeC[:, tsl], ALU.mult, ["QN%d" % r, "ropeC"], ["RT2"], eng="pool")
            tt(t1, ps[:, b3, :], ropeS[:, tsl], ALU.mult, [PSR(b3), "ropeS"], ["RT1"])
            tt(qkT[:, dst, tsl], t1, t2, ALU.add, ["RT1", "RT2"], ["qk:%d:%d" % (dst, tc)])

        n = len(units)
        for u in range(n + 2):
            if u < n:
                s1(u)
            if 1 <= u <= n:
                s2(u - 1)
            if rope and 2 <= u <= n + 1:
                s3(u - 2)

    def v_proj(WVv, wres, ncols):
        per_bank = 512 // ncols
        for T0 in range(0, NT, per_bank):
            b = nb()
            for j in range(per_bank):
                T = T0 + j
                for kc in range(8):
                    mm(ps[:, b, j * ncols:(j + 1) * ncols], hT[:, kc, T * 128:(T + 1) * 128], WVv[:, kc, 0:ncols],
                       kc == 0, kc == 7, [wres, "hT:%d:%d" % (kc, T // 4)], [PSR(b)])
            dst = Vt[:, T0:T0 + per_bank, 0:ncols]
            src = ps[:, b, :].rearrange("p (j c) -> p j c", j=per_bank)
            act(dst, src, AF.Copy, [PSR(b)], ["V:%d" % T for T in range(T0, T0 + per_bank)])

    O_EM = O_TMP
    O_PR = O_TMP + 6656
    O_PM = O_PR + 4096
    O_R = O_PM + 4096
    NPS = 4
    LAG = 3

    def attention(kind, branch, l):
        gi = [0]
        R = {"na": 2, "dil": 8, "swa": 1}[kind]

        def em_load(h, part):
            c0 = h * NA_COLS
            if part == "I":
                dst, src, res, sem = scr(O_EM + (h % 2) * 1280, 640, BF16), nab_d[l, :, c0:c0 + 640], "EMI%d" % (h % 2), "emi%d" % (h % 2)
            elif part == "E0":
                dst, src, res, sem = scr(O_EM + 2560, 1024, BF16), nab_d[l, :, c0 + 640:c0 + 1664], "EME0", "eme0"
            else:
                dst, src, res, sem = scr(O_EM + 4608, 1024, BF16), nab_d[l, :, c0 + 1664:c0 + 2688], "EME3", "eme3"
            dma("pool", dst, src, sem, [], [res])
            act(dst, dst, AF.Exp, [res], [res])

        if kind == "na":
            em_load(0, "I")
            em_load(0, "E0")
            em_load(0, "E3")
        for cq in range(2):
            for hp in range(2):
                if kind == "na":
                    hh = cq * 2 + hp
                    if hh + 1 < 4:
                        em_load(hh + 1, "I")
                    if hh > 0:
                        em_load(hh, "E3")
                hs = slice(hp * 64, hp * 64 + 64)
                ck = 2 if kind == "swa" else 2 + cq
                vc0 = 0 if kind == "swa" else cq * 128
                h_sw = hp * 2 + cq
                pend = []
                for g in range(NTC):
                    par = (cq * 2 + hp) * NTC + g
                    ob, db = (0, 1) if par % 2 == 0 else (2, 3)
                    items = []
                    if kind == "na":
                        h = cq * 2 + hp
                        emI = scr(O_EM + (h % 2) * 1280, 640, BF16)
                        emE0 = scr(O_EM + 2560, 1024, BF16)
                        emE3 = scr(O_EM + 4608, 1024, BF16)
                        resI = "EMI%d" % (h % 2)
                        if g == 0:
                            for T in range(4):
                                items.append((T, 0, 1, emE0, T * 256, "EME0"))
                            qint = (2, 3)
                        elif g == 3:
                            qint = (12, 13)
                        else:
                            qint = tuple(range(4 * g, 4 * g + 4))
                        for T in range(NT):
                            qs = [qt for qt in qint if abs(T - qt) <= 2]
                            if qs:
                                items.append((T, qs[0], qs[-1], emI, (2 - (T - qs[0])) * 128, resI))
                        if g == 3:
                            for T in range(12, 16):
                                items.append((T, 14, 15, emE3, (T - 12) * 256, "EME3"))
                    else:
                        base = CM_DIL if kind == "dil" else CM_WIN
                        for T in range(NT):
                            qlo = max(4 * g, T - R)
                            qhi = min(4 * g + 3, T + R)
                            if qlo > qhi:
                                continue
                            items.append((T, qlo, qhi, CM, base + (R - (T - qlo)) * 128, "CM"))
                    for idx, (T, qlo, qhi, mten, mc, mres) in enumerate(items):
                        n = qhi - qlo + 1
                        w = n * 128
                        sb = 4 + gi[0] % 4
                        pslot = gi[0] % NPS
                        gi[0] += 1
                        qres = ["qk:%d:%d" % (cq, g)]
                        mm(ps[:, sb, 0:w], qkT[hs, ck, T * 128:(T + 1) * 128], qkT[hs, cq, qlo * 128:(qhi + 1) * 128],
                           True, True, ["qk:%d:%d" % (ck, T // 4)] + qres, [PSR(sb)])
                        pr = scr(O_PR + pslot * 1024, 512, BF16)
                        pm = scr(O_PM + pslot * 1024, 512, BF16)
                        act(pr[:, 0:w], ps[:, sb, 0:w], AF.Exp, [PSR(sb)], ["PR%d" % pslot], scale=0.125)
                        tt(pm[:, 0:w], pr[:, 0:w], mten[:, mc:mc + w], ALU.mult, ["PR%d" % pslot, mres], ["PM%d" % pslot])
                        oc = (qlo - 4 * g) * 128

                        def pv_stage(T=T, pm=pm, pslot=pslot, w=w, oc=oc, ob=ob, db=db, vc0=vc0,
                                     first=(idx == 0), last=(idx == len(items) - 1)):
                            mm(ps[:, ob, oc:oc + w], Vt[:, T, vc0:vc0 + 128], pm[:, 0:w], first, last,
                               ["V:%d" % T, "PM%d" % pslot], [PSR(ob)])
                            mm(ps[:, db, oc:oc + w], cm(CM_ONE), pm[:, 0:w], first, last,
                               ["CM", "PM%d" % pslot], [PSR(db)])
                        pend.append(pv_stage)
                        while len(pend) > LAG:
                            pend.pop(0)()

                    def norm_stage(g=g, ob=ob, db=db, hs=hs, cq=cq, hp=hp, h_sw=h_sw):
                        rq = 0
                        r = scr(O_R, 512, F32)
                        if kind == "swa":
                            act(r[hs, :], ps[hs, db, :], AF.Ln, [PSR(db), "ESK"], ["NR%d" % rq], bias=ESK[hs, h_sw:h_sw + 1])
                        else:
                            act(r[hs, :], ps[hs, db, :], AF.Ln, [PSR(db)], ["NR%d" % rq])
                        act(r[hs, :], r[hs, :], AF.Exp, ["NR%d" % rq], ["NR%d" % rq], scale=-1.0)
                        tt(yT[hs, branch * 2 + cq, g * 512:(g + 1) * 512], ps[hs, ob, :], r[hs, :], ALU.mult,
                           [PSR(ob), "NR%d" % rq], ["yT:%d:%d:%d" % (branch * 2 + cq, g, hp)])
                    pend.append(norm_stage)
                    if kind == "na" and g == 0 and cq * 2 + hp + 1 < 4:
                        em_load(cq * 2 + hp + 1, "E0")
                while pend:
                    pend.pop(0)()

    def branch_loads(kind, l, qbase):
        slot = next_wslot()
        W = WB[:, slot, :].rearrange("p (k n) -> p k n", k=8)
        if kind != "swa":
            wload(slot, W, w_in_d[l, :, qbase:qbase + 512].rearrange("(k p) n -> p k n", p=128))
        else:
            for bb in range(2):
                for a in range(2):
                    hh = a * 2 + bb
                    src = w_in_d[l, :, qbase + hh * 64:qbase + (hh + 1) * 64].rearrange("(k p) d -> p k d", p=128)
                    dst = W[:, :, bb * 128 + a * 64:bb * 128 + (a + 1) * 64]
                    wload(slot, dst, src)
            wload(slot, W[:, :, 256:384], w_in_d[l, :, qbase + 256:qbase + 384].rearrange("(k p) n -> p k n", p=128))
        slot2 = next_wslot()
        W2 = WB[:, slot2, :].rearrange("p (k n) -> p k n", k=8)
        if kind != "swa":
            wload(slot2, W2[:, :, 0:256], w_in_d[l, :, qbase + 512:qbase + 768].rearrange("(k p) n -> p k n", p=128))
        else:
            wload(slot2, W2[:, :, 0:128], w_in_d[l, :, qbase + 384:qbase + 512].rearrange("(k p) n -> p k n", p=128))
        return (slot, slot2)

    def branch_attn(kind, branch, l, slots, gq, gk, rope, next_loads=None):
        slot, slot2 = slots
        W = WB[:, slot, :].rearrange("p (k n) -> p k n", k=8)
        W2 = WB[:, slot2, :].rearrange("p (k n) -> p k n", k=8)
        nvc = 128 if kind == "swa" else 256
        specs = [(0, 0, gq), (128, 1, gq), (256, 2, gk)]
        if kind != "swa":
            specs.append((384, 3, gk))
        qk_norm_all(W, "WB%d" % slot, specs, rope)
        v_proj(W2, "WB%d" % slot2, nvc)
        barrier()
        nxt = next_loads() if next_loads is not None else None
        attention(kind, branch, l)
        return nxt

    O_MIX = 0
    O_G = 32768
    O_ACC = O_G + 2048
    O_MT = O_ACC + 2048
    mixT = scr(O_MIX, 8 * SEQ, BF16).rearrange("p (c t) -> p c t", c=8)

    def merge(l):
        YT_ALL = None
        for dc in range(8):
            slot = next_wslot()
            W = WB[:, slot, :].rearrange("p (k n d) -> p k n d", k=8, n=4)
            for n in range(4):
                wload(slot, W[:, :, n, :], w_in_d[l, :, n * 1024 + dc * 128:n * 1024 + (dc + 1) * 128].rearrange("(k p) d -> p k d", p=128))
            bs = dc % 2
            Wb = WBR[:, bs, :].rearrange("p (k n d) -> p k n d", k=2, n=4)
            for n in range(3):
                dma("pool", Wb[:, :, n, :], w_br_d[l, n, :, dc * 128:(dc + 1) * 128].rearrange("(k p) d -> p k d", p=128),
                    "wbr%d" % bs, [], ["WBR%d" % bs])
            for half in range(2):
                dma("pool", Wb[half * 64:(half + 1) * 64, :, 3, :],
                    w_br_d[l, 3, half * 128:(half + 1) * 128, dc * 128:(dc + 1) * 128].rearrange("(k p) d -> p k d", p=64),
                    "wbr%d" % bs, [], ["WBR%d" % bs])
            for tc in range(NTC):
                tsl = slice(tc * 512, (tc + 1) * 512)
                aq = 0
                acc = scr(O_ACC, 512, F32)
                for n in range(4):
                    bg, bb = nb(), nb()
                    for kc in range(8):
                        mm(ps[:, bg, :], W[:, kc, n, :], hT[:, kc, tsl], kc == 0, kc == 7,
                           ["WB%d" % slot, "hT:%d:%d" % (kc, tc)], [PSR(bg)])
                    for kc in range(2):
                        ych = n * 2 + kc
                        if n == 0:
                            yres = ["yT:%d:%d" % (ych, tc)]
                        else:
                            yres = ["yT:%d:%d:%d" % (ych, tc, hp) for hp in range(2)]
                        mm(ps[:, bb, :], Wb[:, kc, n, :], yT[:, ych, tsl], kc == 0, kc == 1,
                           ["WBR%d" % bs] + yres, [PSR(bb)])
                    gq = (tc * 4 + n) % 2
                    G = scr(O_G + gq * 1024, 512, BF16)
                    act(G, ps[:, bg, :], AF.Sigmoid, [PSR(bg), "PV"], ["G%d" % gq], bias=pvc(PV_GATEB + n * 8 + dc))
                    if n == 0:
                        tt(acc, ps[:, bb, :], G, ALU.mult, [PSR(bb), "G%d" % gq], ["ACC%d" % aq])
                    else:
                        mt = scr(O_MT, 512, F32)
                        tt(mt, ps[:, bb, :], G, ALU.mult, [PSR(bb), "G%d" % gq], ["MT"])
                        if n < 3:
                            tt(acc, acc, mt, ALU.add, ["ACC%d" % aq, "MT"], ["ACC%d" % aq])
                        else:
                            tt(mixT[:, dc, tsl], acc, mt, ALU.add, ["ACC%d" % aq, "MT"], ["mix:%d:%d" % (dc, tc)])
        for dg in range(2):
            slot = next_wslot()
            W = WB[:, slot, :].rearrange("p (k n) -> p k n", k=8)
            wload(slot, W, w_out_d[l, :, dg * 512:(dg + 1) * 512].rearrange("(k p) n -> p k n", p=128))
            for j in range(4):
                do = dg * 4 + j
                for tc in range(NTC):
                    tsl = slice(tc * 512, (tc + 1) * 512)
                    b = nb()
                    for kc in range(8):
                        mm(ps[:, b, :], W[:, kc, j * 128:(j + 1) * 128], mixT[:, kc, tsl], kc == 0, kc == 7,
                           ["WB%d" % slot, "mix:%d:%d" % (kc, tc)], [PSR(b)])
                    tt(xT[:, do, tsl], xT[:, do, tsl], ps[:, b, :], ALU.add, ["xT:%d:%d" % (do, tc), PSR(b)],
                       ["xT:%d:%d" % (do, tc)])

    O_U = 12288
    O_SG = O_U + 3 * 2080
    yT_flat = yT[:].rearrange("p c t -> p (c t)")

    def actT(j):
        if j < 16:
            return yT_flat[:, j * 1024:(j + 1) * 1024]
        return scr((j - 16) * 2048, 1024, BF16)

    def ffn(l):
        rmsnorm(PV_GFFN)
        barrier()
        ucount = [0]
        for th in range(2):
            t0 = th * 1024
            halo_tok = 1024 if th == 0 else 1023
            halo_col = 1025 if th == 0 else 0
            zero_col = 0 if th == 0 else 1025
            for u in range(3):
                U = scr(O_U + u * 2080, 1026, BF16)
                S.add("dve", lambda e, U=U, zc=zero_col: e.memset(U[:, zc:zc + 1], 0.0), writes=["U%d" % u])
            pendB = []
            sg_by_j = {}
            for jp in range(11):
                slot = next_wslot()
                W = WB[:, slot, :].rearrange("p (k a n) -> p k a n", k=8, a=2)
                for a in range(2):
                    wload(slot, W[:, :, a, :], w_up_d[l, :, a * D_FF + jp * 256: a * D_FF + (jp + 1) * 256].rearrange("(k p) n -> p k n", p=128))
                for jj in range(2):
                    j = jp * 2 + jj
                    for a in range(2):
                        u = ucount[0] % 3
                        ucount[0] += 1
                        U = scr(O_U + u * 2080, 1026, BF16)
                        ures = "U%d" % u
                        ch = a * 22 + j
                        wv = W[:, :, a, jj * 128:(jj + 1) * 128]
                        dgs = [make_diag(PV_FCW + ch * 3 + k) for k in range(3)]
                        for t2 in range(2):
                            b = nb()
                            tok = t0 + t2 * 512
                            for kc in range(8):
                                mm(ps[:, b, :], wv[:, kc, :], hT[:, kc, tok:tok + 512], kc == 0, kc == 7,
                                   ["WB%d" % slot, "hT:%d:%d" % (kc, tok // 512)], [PSR(b)])
                            act(U[:, 1 + t2 * 512:1 + (t2 + 1) * 512], ps[:, b, :], AF.Copy, [PSR(b)], [ures])
                        if th == 0:
                            b = nb()
                            for kc in range(8):
                                mm(ps[:, b, 0:1], wv[:, kc, :], hT[:, kc, 1024:1025], kc == 0, kc == 7,
                                   ["WB%d" % slot, "hT:%d:%d" % (kc, 2)], [PSR(b)])
                            S.add("dve", lambda e, U=U, b=b: e.tensor_copy(out=U[:, 1025:1026], in_=ps[:, b, 0:1]),
                                  reads=[PSR(b)], writes=[ures])
                            S.add("dve", lambda e, U=U, ch=ch: e.tensor_copy(out=HS[:, ch:ch + 1], in_=U[:, 1024:1025]),
                                  reads=[ures], writes=["HS%d" % ch])
                        else:
                            S.add("dve", lambda e, U=U, ch=ch: e.tensor_copy(out=U[:, 0:1], in_=HS[:, ch:ch + 1]),
                                  reads=["HS%d" % ch], writes=[ures])

                        def stageB(U=U, ures=ures, ch=ch, a=a, j=j, dgs=dgs):
                            for t2 in range(2):
                                b = nb()
                                for k in range(3):
                                    mm(ps[:, b, :], dgs[k][0], U[:, t2 * 512 + k:t2 * 512 + k + 512], k == 0, k == 2,
                                       [dgs[k][1], ures], [PSR(b)])
                                if a == 0:
                                    sq = t2
                                    sg = scr(O_SG + sq * 1024, 512, BF16)
                                    act(sg, ps[:, b, :], AF.Silu, [PSR(b), "PV"], ["SG%d" % sq], bias=pvc(PV_FCB + ch))
                                else:
                                    sg = scr(O_SG + t2 * 1024, 512, BF16)
                                    stt(actT(j)[:, t2 * 512:(t2 + 1) * 512], ps[:, b, :], pvc(PV_FCB + ch), sg, ALU.add, ALU.mult,
                                        [PSR(b), "PV", "SG%d" % t2], ["act:%d:%d" % (j, t2)])
                        pendB.append(stageB)
                        while len(pendB) > 1:
                            pendB.pop(0)()
            while pendB:
                pendB.pop(0)()
            for do in range(8):
                slot = next_wslot()
                W = WB[:, slot, 0:22 * 128].rearrange("p (k n) -> p k n", k=22)
                for hh in range(2):
                    wload(slot, W[:, hh * 11:(hh + 1) * 11, :],
                          w_dn_d[l, hh * 1408:(hh + 1) * 1408, do * 128:(do + 1) * 128].rearrange("(k p) n -> p k n", p=128))
                for t2 in range(2):
                    b = nb()
                    tok = t0 + t2 * 512
                    for j in range(22):
                        mm(ps[:, b, :], W[:, j, :], actT(j)[:, t2 * 512:(t2 + 1) * 512], j == 0, j == 21,
                           ["WB%d" % slot, "act:%d:%d" % (j, t2)], [PSR(b)])
                    tt(xT[:, do, tok:tok + 512], xT[:, do, tok:tok + 512], ps[:, b, :], ALU.add,
                       ["xT:%d:%d" % (do, tok // 512), PSR(b)], ["xT:%d:%d" % (do, tok // 512)])

    SCR_ALL = "SCRALL"


    scr_names_A0 = ["SQ0", "SQ1", "SQ2", "RS0", "RS1", "XST0", "XST1"]
    scr_names_conf = ["UPpad0", "UPpad1", "SIG0", "SIG1", "MSQ", "LRS", "LT0", "LT1"] + \
        ["UP:%d:%d" % (c, t) for c in range(2) for t in range(NTC)] + \
        ["CVF%d%d" % (q, c) for q in range(2) for c in range(2)] + ["CVB%d%d" % (q, c) for q in range(2) for c in range(2)] + \
        ["SQB%d%d" % (q, c) for q in range(2) for c in range(2)]
    scr_names_attn = ["qk:%d:%d" % (c, t) for c in range(4) for t in range(NTC)] + ["V:%d" % T for T in range(NT)] + \
        ["QSQ0", "QSQ1", "QSQ2", "QRS0", "QRS1", "QN0", "QN1", "RT1", "RT2", "EMI0", "EMI1", "EME0", "EME3", "PR0", "PR1", "PR2", "PR3", "PM0", "PM1", "PM2", "PM3", "NR0", "NR1"]
    scr_names_merge = ["mix:%d:%d" % (c, t) for c in range(8) for t in range(NTC)] + ["G0", "G1", "ACC0", "ACC1", "MT"]
    scr_names_ffn = ["U0", "U1", "U2", "SG0", "SG1"] + ["act:%d:%d" % (j, t) for j in range(22) for t in range(2)]
    yT_names = ["yT:%d:%d" % (c, t) for c in range(2) for t in range(NTC)] + \
        ["yT:%d:%d:%d" % (c, t, hp) for c in range(2, 8) for t in range(NTC) for hp in range(2)]
    ALLSCR = scr_names_A0 + scr_names_conf + scr_names_attn + scr_names_merge + scr_names_ffn

    def barrier(extra=()):
        names = ALLSCR + list(extra)
        S.add("dve", lambda e: e.memset(DUM[:, 0:1], 0.0), reads=names, writes=names)

    for s in range(nseq):
        barrier(yT_names)
        load_x(s)
        for li in range(nl):
            l = l0 + li
            dma("sp", PV[:], pvec_d[l], "pvl", [], ["PV"])
            act(ESK[:], PV[:, PV_SINK:PV_SINK + 4], AF.Exp, ["PV"], ["ESK"])
            barrier()
            rmsnorm(PV_GMIX)
            barrier()
            conformer(l)
            barrier()
            sl_na = branch_loads("na", l, 4608)
            sl_dil = branch_attn("na", 1, l, sl_na, PV_QKG + 0, PV_QKG + 1, False,
                                 next_loads=lambda l=l: branch_loads("dil", l, 5376))
            barrier()
            sl_swa = branch_attn("dil", 2, l, sl_dil, PV_QKG + 2, PV_QKG + 3, True,
                                 next_loads=lambda l=l: branch_loads("swa", l, 6144))
            barrier()
            branch_attn("swa", 3, l, sl_swa, PV_QKG + 4, PV_QKG + 5, True)
            barrier()
            merge(l)
            barrier(yT_names)
            ffn(l)
            barrier(yT_names)
        store_x(s)
    S.add("sp", lambda e: None, reads=["OUT0", "OUT1"], writes=[])

    S.finalize()
    with contextlib.ExitStack() as st:
        sems = {}
        for e in ENGS:
            sems[e] = st.enter_context(nc.semaphore("s_" + e))
        for k in S.dcounts:
            sems[k] = st.enter_context(nc.semaphore("d_" + k))
        block = st.enter_context(nc.Block())
        block.sync(lambda e: S.run_engine("sp", e, sems))
        block.scalar(lambda e: S.run_engine("act", e, sems))
        block.vector(lambda e: S.run_engine("dve", e, sems))
        block.gpsimd(lambda e: S.run_engine("pool", e, sems))
        block.tensor(lambda e: S.run_engine("pe", e, sems))
    return nc, S


def _const_mats():
    kk = np.arange(128)[:, None]
    qq = np.arange(128)[None, :]
    cmat = np.zeros((128, NCM), np.float32)
    cmat[:, CM_WIN:CM_WIN + 128] = (kk <= qq)
    cmat[:, CM_WIN + 128:CM_WIN + 256] = 1.0
    cmat[:, CM_WIN + 256:CM_WIN + 384] = (kk >= qq)
    for d in range(-8, 9):
        o = 128 * d + kk - qq
        m = (np.abs(o) <= 64).astype(np.float32)
        m += ((o % 4 == 0) & (np.abs(o) <= 256)).astype(np.float32)
        m += ((o % 16 == 0) & (np.abs(o) <= 1024)).astype(np.float32)
        cmat[:, CM_DIL + (8 - d) * 128: CM_DIL + (9 - d) * 128] = m
    cmat[:, CM_ID:CM_ID + 128] = np.eye(128, dtype=np.float32)
    cmat[:, CM_O1024:CM_O1024 + 128] = 1.0 / 1024.0
    blk = np.zeros((128, 128), np.float32)
    blk[0:64, 0:64] = 1.0 / 64.0
    blk[64:128, 64:128] = 1.0 / 64.0
    cmat[:, CM_B64:CM_B64 + 128] = blk
    cmat[:, CM_ONE:CM_ONE + 128] = 1.0
    rot = np.zeros((128, 128), np.float32)
    for hb in range(2):
        for d in range(8):
            rot[hb * 64 + d + 8, hb * 64 + d] = -1.0
            rot[hb * 64 + d, hb * 64 + d + 8] = 1.0
    cmat[:, CM_ROT:CM_ROT + 128] = rot
    cmat[:, CM_O256:CM_O256 + 128] = 1.0 / 256.0
    pos = np.arange(SEQ, dtype=np.float32)
    inv = (np.float32(500000.0) ** (-np.arange(0, 16, 2, dtype=np.float32) / np.float32(16))).astype(np.float32)
    ang = (pos[:, None] * inv[None, :]).astype(np.float32)
    cosT = np.cos(ang).astype(np.float32).T
    sinT = np.sin(ang).astype(np.float32).T
    rc = np.ones((128, SEQ), np.float32)
    rs = np.zeros((128, SEQ), np.float32)
    for hb in range(2):
        for d in range(16):
            rc[hb * 64 + d] = cosT[d % 8]
            rs[hb * 64 + d] = sinT[d % 8]
    return cmat, rc, rs, np.eye(128, dtype=np.float32)


def _na_index():
    blocks = [(2 + d, 2) for d in (2, 1, 0, -1, -2)]
    blocks += [(T, qt) for T in range(4) for qt in (0, 1)]
    blocks += [(T, qt) for T in range(12, 16) for qt in (14, 15)]
    dr = np.zeros((128, NA_COLS), np.int64)
    dc = np.zeros((128, NA_COLS), np.int64)
    valid = np.zeros((128, NA_COLS), bool)
    kk = np.arange(128)[:, None]
    qq = np.arange(128)[None, :]
    for i, (T, qt) in enumerate(blocks):
        col = i * 128
        qtok = qt * 128 + qq
        r = qtok // 64
        c = qtok % 64
        rs = np.clip(r - 4, 0, 24)
        cs = np.clip(c - 8, 0, 48)
        ktok = T * 128 + kk
        kr = ktok // 64
        kc = ktok % 64
        v = (kr >= rs) & (kr < rs + 8) & (kc >= cs) & (kc < cs + 16)
        dr[:, col:col + 128] = np.where(v, kr - r + 7, 0)
        dc[:, col:col + 128] = np.where(v, kc - c + 15, 0)
        valid[:, col:col + 128] = v
    return dr, dc, valid


def _host_layout(inp):
    f = np.float32
    L = DEPTH
    pv = np.zeros((L, 128, NPV), f)
    pv[:, :, PV_GMIX:PV_GMIX + 8] = inp["g_mix"].reshape(L, 8, 128).transpose(0, 2, 1)
    pv[:, :, PV_GFFN:PV_GFFN + 8] = inp["g_ffn"].reshape(L, 8, 128).transpose(0, 2, 1)
    pv[:, :, PV_GATEB:PV_GATEB + 32] = inp["gate_b"].reshape(L, 4, 8, 128).transpose(0, 3, 1, 2).reshape(L, 128, 32)
    pv[:, :, PV_ACW:PV_ACW + 62] = inp["a_conv_w"].reshape(L, 31, 2, 128).transpose(0, 3, 2, 1).reshape(L, 128, 62)
    pv[:, :, PV_ACB:PV_ACB + 2] = inp["a_conv_b"].reshape(L, 2, 128).transpose(0, 2, 1)
    pv[:, :, PV_ALNG:PV_ALNG + 2] = inp["a_ln_g"].reshape(L, 2, 128).transpose(0, 2, 1)
    pv[:, :, PV_ALNB:PV_ALNB + 2] = inp["a_ln_b"].reshape(L, 2, 128).transpose(0, 2, 1)
    for i, k in enumerate(["na_qn", "na_kn", "dil_qn", "dil_kn", "swa_qn", "swa_kn"]):
        pv[:, :, PV_QKG + i] = np.tile(inp[k], (1, 2))
    pv[:, :, PV_SINK:PV_SINK + 4] = inp["swa_sink"][:, None, :]
    pv[:, :, PV_FCW:PV_FCW + 132] = inp["ffn_conv_w"].reshape(L, 3, 44, 128).transpose(0, 3, 2, 1).reshape(L, 128, 132)
    pv[:, :, PV_FCB:PV_FCB + 44] = inp["ffn_conv_b"].reshape(L, 44, 128).transpose(0, 2, 1)
    dr, dc, valid = _na_index()
    rpb = inp["na_rpb"]
    g = rpb[:, :, dr, dc]
    g = np.where(valid[None, None], g, f(-30000.0)).astype(f)
    nab = np.ascontiguousarray(g.transpose(0, 2, 1, 3).reshape(L, 128, 4 * NA_COLS))
    return pv, nab


_PROG_CACHE = {}


def _get_prog(nl, nseq, l0):
    key = (nl, nseq, l0)
    if key not in _PROG_CACHE:
        _PROG_CACHE[key] = build_program(nl, nseq, l0)[0]
    return _PROG_CACHE[key]


def run_layers(inp, x, nl, l0, ncores, nseq):
    cmat, rc, rs, idf = _const_mats()
    pv, nab = _host_layout(inp)
    nc = _get_prog(nl, nseq, l0)
    c32 = lambda a: np.ascontiguousarray(np.asarray(a, dtype=np.float32))
    shared = {
        "w_in": c32(inp["w_in"]), "w_branch": c32(inp["w_branch"]), "w_out": c32(inp["w_out"]),
        "w_up": c32(inp["w_up"]), "w_down": c32(inp["w_down"]), "pvec": pv, "nab": nab,
        "cmat": cmat, "ropec": rc, "ropes": rs, "identf": idf,
    }
    in_maps = []
    for c in range(ncores):
        m = dict(shared)
        m["x"] = np.ascontiguousarray(x[c * nseq:(c + 1) * nseq])
        in_maps.append(m)
    res = run_bass_kernel_spmd(nc, in_maps, core_ids=list(range(ncores)))
    return np.concatenate([np.asarray(r["out"]) for r in res.results], axis=0)


def kernel(**inputs):
    inp = {k: np.asarray(v) for k, v in inputs.items()}
    x = np.ascontiguousarray(inp["x"].astype(np.float32))
    out = run_layers(inp, x, DEPTH, 0, 8, 2)
    return out.astype(np.float32)
```

```python
import contextlib
import numpy as np
import concourse.bass as bass
import concourse.mybir as mybir
from concourse.bass_utils import run_bass_kernel_spmd

F32 = mybir.dt.float32
BF16 = mybir.dt.bfloat16
AF = mybir.ActivationFunctionType
ALU = mybir.AluOpType

D_MODEL = 1024
SEQ = 2048
DEPTH = 4
N_IN = 6656
D_FF = 2816
EPS = 1e-6
NT = 16
NTC = 4
ENGS = ("pe", "act", "dve", "pool", "sp")

PV_GMIX = 0
PV_GFFN = 8
PV_GATEB = 16
PV_ACW = 48
PV_ACB = 110
PV_ALNG = 112
PV_ALNB = 114
PV_QKG = 116
PV_SINK = 122
PV_FCW = 126
PV_FCB = 258
NPV = 304

CM_WIN = 0
CM_DIL = 384
CM_ID = CM_DIL + 17 * 128
CM_O1024 = CM_ID + 128
CM_B64 = CM_O1024 + 128
CM_ONE = CM_B64 + 128
CM_ROT = CM_ONE + 128
CM_O256 = CM_ROT + 128
NCM = CM_O256 + 128

NA_COLS = 2688


class _Op:
    __slots__ = ("eng", "fn", "deps", "dma_sem")


class Sched:
    def __init__(self):
        self.ops = []
        self.last_w = {}
        self.readers = {}

    def add(self, eng, fn, reads=(), writes=(), dma_sem=None):
        op = _Op()
        op.eng = eng
        op.fn = fn
        op.dma_sem = dma_sem
        deps = set()
        lw = self.last_w
        rd = self.readers
        for r in reads:
            w = lw.get(r)
            if w is not None:
                deps.add(w)
        for r in writes:
            w = lw.get(r)
            if w is not None:
                deps.add(w)
            x = rd.get(r)
            if x:
                deps.update(x.values())
        idx = len(self.ops)
        deps.discard(idx)
        op.deps = deps
        self.ops.append(op)
        for r in writes:
            lw[r] = idx
            rd[r] = {}
        rkey = eng if dma_sem is None else (eng, idx)
        for r in reads:
            if r in writes:
                continue
            l = rd.get(r)
            if l is None:
                rd[r] = {rkey: idx}
            else:
                l[rkey] = idx
        return idx

    def finalize(self):
        ops = self.ops
        n = len(ops)
        needs_inc = [False] * n
        for i, op in enumerate(ops):
            for d in op.deps:
                od = ops[d]
                if od.dma_sem is None and od.eng != op.eng:
                    needs_inc[d] = True
        cnt = {e: 0 for e in ENGS}
        dcnt = {}
        sig = [None] * n
        for i, op in enumerate(ops):
            if op.dma_sem is not None:
                dcnt[op.dma_sem] = dcnt.get(op.dma_sem, 0) + 16
                sig[i] = (op.dma_sem, dcnt[op.dma_sem])
            elif needs_inc[i]:
                cnt[op.eng] += 1
                sig[i] = (op.eng, cnt[op.eng])
        eng_vc = {e: {} for e in ENGS}
        op_vc = [None] * n
        waits = [None] * n
        for i, op in enumerate(ops):
            vc = eng_vc[op.eng]
            w = {}
            for d in sorted(op.deps):
                s = sig[d]
                if s is None:
                    continue
                if vc.get(s[0], 0) >= s[1]:
                    continue
                if w.get(s[0], 0) < s[1]:
                    w[s[0]] = s[1]
                dv = op_vc[d]
                for k2, v2 in dv.items():
                    if vc.get(k2, 0) < v2:
                        vc[k2] = v2
                vc[s[0]] = s[1]
            waits[i] = w
            if sig[i] is not None:
                snap = dict(vc)
                snap[sig[i][0]] = sig[i][1]
                op_vc[i] = snap
                if op.dma_sem is None:
                    vc[sig[i][0]] = sig[i][1]
        self.sig = sig
        self.waits = waits
        self.counts = cnt
        self.dcounts = dcnt

    def run_engine(self, ename, eng, sems):
        ops = self.ops
        sig = self.sig
        waits = self.waits
        for i, op in enumerate(ops):
            if op.eng != ename:
                continue
            for key, val in waits[i].items():
                eng.wait_ge(sems[key], val)
            ins = op.fn(eng)
            s = sig[i]
            if s is not None:
                ins.then_inc(sems[s[0]], 16 if op.dma_sem is not None else 1)


def build_program(nl=DEPTH, nseq=2, l0=0):
    nc = bass.Bass("TRN2", target_bir_lowering=False)
    x_d = nc.dram_tensor("x", [nseq, SEQ, D_MODEL], F32, kind="ExternalInput").ap()
    w_in_d = nc.dram_tensor("w_in", [DEPTH, D_MODEL, N_IN], F32, kind="ExternalInput").ap()
    w_br_d = nc.dram_tensor("w_branch", [DEPTH, 4, 256, D_MODEL], F32, kind="ExternalInput").ap()
    w_out_d = nc.dram_tensor("w_out", [DEPTH, D_MODEL, D_MODEL], F32, kind="ExternalInput").ap()
    w_up_d = nc.dram_tensor("w_up", [DEPTH, D_MODEL, 2 * D_FF], F32, kind="ExternalInput").ap()
    w_dn_d = nc.dram_tensor("w_down", [DEPTH, D_FF, D_MODEL], F32, kind="ExternalInput").ap()
    pvec_d = nc.dram_tensor("pvec", [DEPTH, 128, NPV], F32, kind="ExternalInput").ap()
    nab_d = nc.dram_tensor("nab", [DEPTH, 128, 4 * NA_COLS], F32, kind="ExternalInput").ap()
    cm_d = nc.dram_tensor("cmat", [128, NCM], F32, kind="ExternalInput").ap()
    rc_d = nc.dram_tensor("ropec", [128, SEQ], F32, kind="ExternalInput").ap()
    rs_d = nc.dram_tensor("ropes", [128, SEQ], F32, kind="ExternalInput").ap()
    idf_d = nc.dram_tensor("identf", [128, 128], F32, kind="ExternalInput").ap()
    out_d = nc.dram_tensor("out", [nseq, SEQ, D_MODEL], F32, kind="ExternalOutput").ap()

    xT = nc.alloc_sbuf_tensor("xT", [128, 8, SEQ], F32)
    hT = nc.alloc_sbuf_tensor("hT", [128, 8, SEQ], BF16)
    yT = nc.alloc_sbuf_tensor("yT", [128, 8, SEQ], BF16)
    SCR_N = 20736
    SCR = nc.alloc_sbuf_tensor("scr", [128, SCR_N], BF16)
    WB = nc.alloc_sbuf_tensor("wb", [128, 2, 4096], BF16)
    WBR = nc.alloc_sbuf_tensor("wbr", [128, 2, 1024], BF16)
    ropeC = nc.alloc_sbuf_tensor("ropeC", [128, SEQ], BF16)
    ropeS = nc.alloc_sbuf_tensor("ropeS", [128, SEQ], BF16)
    CM = nc.alloc_sbuf_tensor("cm", [128, NCM], BF16)
    identF = nc.alloc_sbuf_tensor("identF", [128, 128], F32)
    PV = nc.alloc_sbuf_tensor("pv", [128, NPV], F32)
    ESK = nc.alloc_sbuf_tensor("esk", [128, 4], F32)
    DG = nc.alloc_sbuf_tensor("dg", [128, 6, 128], BF16)
    DUM = nc.alloc_sbuf_tensor("dum", [128, 16], BF16)
    HS = nc.alloc_sbuf_tensor("hs", [128, 48], BF16)
    ps = nc.alloc_psum_tensor("ps", [128, 8, 512], F32)

    S = Sched()

    def scr(off_b, n_el, dt):
        assert off_b % 4 == 0
        if dt is BF16:
            assert off_b // 2 + n_el <= SCR_N, (off_b, n_el)
            return SCR[:, off_b // 2: off_b // 2 + n_el]
        assert off_b // 2 + 2 * n_el <= SCR_N, (off_b, n_el)
        return SCR[:, off_b // 2: off_b // 2 + 2 * n_el].bitcast(F32)

    ring = [0]

    def nb():
        b = ring[0]
        ring[0] = (b + 1) % 8
        return b

    def PSR(b):
        return "ps%d" % b

    def cm(c0, n=128):
        return CM[:, c0:c0 + n]

    def pvc(c):
        return PV[:, c:c + 1]

    def mm(out, lhsT, rhs, start, stop, reads, writes):
        S.add("pe", lambda e: e.matmul(out, lhsT, rhs, start=start, stop=stop), reads=reads, writes=writes)

    def act(out, in_, func, reads, writes, bias=None, scale=None):
        kw = {}
        if bias is not None:
            kw["bias"] = bias
        if scale is not None:
            kw["scale"] = scale
        S.add("act", lambda e: e.activation(out=out, in_=in_, func=func, **kw), reads=reads, writes=writes)

    def tt(out, in0, in1, op, reads, writes, eng="dve"):
        S.add(eng, lambda e: e.tensor_tensor(out=out, in0=in0, in1=in1, op=op), reads=reads, writes=writes)

    def ts(out, in0, s1, s2, op0, op1, reads, writes, eng="dve"):
        if op1 is None:
            S.add(eng, lambda e: e.tensor_scalar(out=out, in0=in0, scalar1=s1, scalar2=None, op0=op0),
                  reads=reads, writes=writes)
        else:
            S.add(eng, lambda e: e.tensor_scalar(out=out, in0=in0, scalar1=s1, scalar2=s2, op0=op0, op1=op1),
                  reads=reads, writes=writes)

    def rsqrt_eps(out, in_, reads, writes):
        act(out, in_, AF.Ln, reads, writes, bias=EPS, scale=1.0)
        act(out, out, AF.Exp, list(writes), list(writes), scale=-0.5)

    def stt(out, in0, scalar, in1, op0, op1, reads, writes, eng="dve"):
        S.add(eng, lambda e: e.scalar_tensor_tensor(out=out, in0=in0, scalar=scalar, in1=in1, op0=op0, op1=op1),
              reads=reads, writes=writes)

    def dma(eng, out, in_, sem, reads, writes):
        S.add(eng, lambda e: e.dma_start(out=out, in_=in_), reads=reads, writes=writes, dma_sem=sem)

    dma("pool", CM[:], cm_d, "c0", [], ["CM"])
    dma("pool", ropeC[:], rc_d, "c1", [], ["ropeC"])
    dma("pool", ropeS[:], rs_d, "c2", [], ["ropeS"])
    dma("sp", identF[:], idf_d, "c3", [], ["identF"])

    HT_ALL = ["hT:%d:%d" % (kc, tc) for kc in range(8) for tc in range(NTC)]

    def ht_tc(tc):
        return ["hT:%d:%d" % (kc, tc) for kc in range(8)]

    wslot = [0]

    def next_wslot():
        s = wslot[0]
        wslot[0] = 1 - s
        return s

    def wload(slot, dst, src):
        dma("pool", dst, src, "w%d" % slot, [], ["WB%d" % slot])

    dgslot = [0]

    def make_diag(col):
        s = dgslot[0]
        dgslot[0] = (s + 1) % 6
        tt(DG[:, s, :], cm(CM_ID), PV[:, col:col + 1].to_broadcast([128, 128]), ALU.mult, ["CM", "PV"], ["DG%d" % s])
        return DG[:, s, :], "DG%d" % s

    def load_x(s):
        for T in range(NT):
            sl = T % 2
            st = scr(sl * 4096, 1024, F32)
            dma("sp", st, x_d[s, T * 128:(T + 1) * 128, :], "xs%d" % sl, [], ["XST%d" % sl])
            for half in range(2):
                b = nb()
                for j in range(4):
                    kc = half * 4 + j
                    mm(ps[:, b, j * 128:(j + 1) * 128], st[:, kc * 128:(kc + 1) * 128], identF[:], True, True,
                       ["XST%d" % sl, "identF"], [PSR(b)])
                eng = "dve" if half == 0 else "act"
                dst = xT[:, half * 4:(half + 1) * 4, T * 128:(T + 1) * 128]
                src = ps[:, b, :].rearrange("p (j t) -> p j t", j=4)
                wr = ["xT:%d:%d" % (half * 4 + j, T // 4) for j in range(4)]
                if eng == "dve":
                    S.add("dve", lambda e, dst=dst, src=src: e.tensor_copy(out=dst, in_=src), reads=[PSR(b)], writes=wr)
                else:
                    act(dst, src, AF.Copy, [PSR(b)], wr)

    def store_x(s):
        for T in range(NT):
            sl = T % 2
            st = scr(sl * 4096, 1024, F32)
            for half in range(2):
                b = nb()
                for j in range(4):
                    kc = half * 4 + j
                    mm(ps[:, b, j * 128:(j + 1) * 128], xT[:, kc, T * 128:(T + 1) * 128], identF[:], True, True,
                       ["xT:%d:%d" % (kc, T // 4), "identF"], [PSR(b)])
                dst = st[:, half * 512:(half + 1) * 512]
                if half == 0:
                    S.add("dve", lambda e, dst=dst, b=b: e.tensor_copy(out=dst, in_=ps[:, b, :]), reads=[PSR(b)],
                          writes=["XST%d" % sl])
                else:
                    act(dst, ps[:, b, :], AF.Copy, [PSR(b), "XST%d" % sl], ["XST%d" % sl])
            dma("sp", out_d[s, T * 128:(T + 1) * 128, :], st, "xo%d" % sl, ["XST%d" % sl], ["OUT%d" % sl])

    def rmsnorm(gcol):
        for tc in range(NTC):
            b = nb()
            tsl = slice(tc * 512, (tc + 1) * 512)
            for kc in range(8):
                q = kc % 3
                sq = scr(q * 1024, 512, BF16)
                act(sq, xT[:, kc, tsl], AF.Square, ["xT:%d:%d" % (kc, tc)], ["SQ%d" % q])
                mm(ps[:, b, :], cm(CM_O1024), sq, kc == 0, kc == 7, ["SQ%d" % q, "CM"], [PSR(b)])
            r = tc % 2
            rstd = scr(3072 + r * 2048, 512, F32)
            rsqrt_eps(rstd, ps[:, b, :], [PSR(b)], ["RS%d" % r])
            for kc in range(8):
                stt(hT[:, kc, tsl], xT[:, kc, tsl], pvc(gcol + kc), rstd, ALU.mult, ALU.mult,
                    ["xT:%d:%d" % (kc, tc), "PV", "RS%d" % r], ["hT:%d:%d" % (kc, tc)])

    def conformer(l):
        slot = next_wslot()
        W = WB[:, slot, :].rearrange("p (k n) -> p k n", k=8)
        wload(slot, W, w_in_d[l, :, 4096:4608].rearrange("(k p) n -> p k n", p=128))
        UP = scr(0, 2 * 2080, BF16).rearrange("p (c t) -> p c t", c=2)
        o_sig = 8320
        o_cvf = o_sig + 4096
        o_cvb = o_cvf + 8192
        o_sqb = o_cvb + 4096
        o_st = o_sqb + 4096
        o_t = o_st + 4096
        for cc in range(2):
            S.add("dve", lambda e, cc=cc: e.memset(UP[:, cc, 0:15], 0.0), writes=["UPpad%d" % cc])
            S.add("dve", lambda e, cc=cc: e.memset(UP[:, cc, 15 + SEQ:30 + SEQ], 0.0), writes=["UPpad%d" % cc])
        for cc in range(2):
            for tc in range(NTC):
                tsl = slice(tc * 512, (tc + 1) * 512)
                ba, bg = nb(), nb()
                for kc in range(8):
                    mm(ps[:, ba, :], W[:, kc, cc * 128:(cc + 1) * 128], hT[:, kc, tsl], kc == 0, kc == 7,
                       ["WB%d" % slot, "hT:%d:%d" % (kc, tc)], [PSR(ba)])
                for kc in range(8):
                    mm(ps[:, bg, :], W[:, kc, 256 + cc * 128:256 + (cc + 1) * 128], hT[:, kc, tsl], kc == 0, kc == 7,
                       ["WB%d" % slot, "hT:%d:%d" % (kc, tc)], [PSR(bg)])
                q = tc % 2
                sig = scr(o_sig + q * 2048, 512, F32)
                act(sig, ps[:, bg, :], AF.Sigmoid, [PSR(bg)], ["SIG%d" % q])
                tt(UP[:, cc, 15 + tc * 512:15 + (tc + 1) * 512], ps[:, ba, :], sig, ALU.mult,
                   [PSR(ba), "SIG%d" % q], ["UP:%d:%d" % (cc, tc)])
        up_all = ["UP:%d:%d" % (cc, tc) for cc in range(2) for tc in range(NTC)] + ["UPpad0", "UPpad1"]
        for cc in range(2):
            for k in range(31):
                dg, dres = make_diag(PV_ACW + cc * 31 + k)
                for tc in range(NTC):
                    b = cc * 4 + tc
                    mm(ps[:, b, :], dg, UP[:, cc, tc * 512 + k: tc * 512 + k + 512], k == 0, k == 30,
                       [dres] + up_all, [PSR(b)])
        ring[0] = 0
        for tc in range(NTC):
            tsl = slice(tc * 512, (tc + 1) * 512)
            q = tc % 2
            for cc in range(2):
                b = cc * 4 + tc
                cvf = scr(o_cvf + (q * 2 + cc) * 2048, 512, F32)
                cvb = scr(o_cvb + (q * 2 + cc) * 1024, 512, BF16)
                sqb = scr(o_sqb + (q * 2 + cc) * 1024, 512, BF16)
                act(cvf, ps[:, b, :], AF.Identity, [PSR(b), "PV"], ["CVF%d%d" % (q, cc)], bias=pvc(PV_ACB + cc))
                act(cvb, ps[:, b, :], AF.Identity, [PSR(b), "PV"], ["CVB%d%d" % (q, cc)], bias=pvc(PV_ACB + cc))
                act(sqb, ps[:, b, :], AF.Square, [PSR(b), "PV"], ["SQB%d%d" % (q, cc)], bias=pvc(PV_ACB + cc))
            b1 = tc
            b2 = 4 + tc
            for cc in range(2):
                cvb = scr(o_cvb + (q * 2 + cc) * 1024, 512, BF16)
                mm(ps[:, b1, :], cm(CM_O256), cvb, cc == 0, cc == 1, ["CM", "CVB%d%d" % (q, cc)], [PSR(b1)])
            for cc in range(2):
                sqb = scr(o_sqb + (q * 2 + cc) * 1024, 512, BF16)
                mm(ps[:, b2, :], cm(CM_O256), sqb, cc == 0, cc == 1, ["CM", "SQB%d%d" % (q, cc)], [PSR(b2)])
            msq = scr(o_st, 512, F32)
            rstd = scr(o_st + 2048, 512, F32)
            act(msq, ps[:, b1, :], AF.Square, [PSR(b1)], ["MSQ"])
            tt(msq, ps[:, b2, :], msq, ALU.subtract, [PSR(b2), "MSQ"], ["MSQ"])
            rsqrt_eps(rstd, msq, ["MSQ"], ["LRS"])
            for cc in range(2):
                cvf = scr(o_cvf + (q * 2 + cc) * 2048, 512, F32)
                t = scr(o_t + cc * 2048, 512, F32)
                tt(t, cvf, ps[:, b1, :], ALU.subtract, ["CVF%d%d" % (q, cc), PSR(b1)], ["LT%d" % cc])
                tt(t, t, rstd, ALU.mult, ["LT%d" % cc, "LRS"], ["LT%d" % cc])
                act(yT[:, cc, tsl], t, AF.Silu, ["LT%d" % cc, "PV"], ["yT:%d:%d" % (cc, tc)],
                    bias=pvc(PV_ALNB + cc), scale=pvc(PV_ALNG + cc))

    O_QK = 0
    O_V = 16384
    O_TMP = 24576
    qkT = scr(O_QK, 4 * SEQ, BF16).rearrange("p (c t) -> p c t", c=4)
    Vt = scr(O_V, 16 * 256, BF16).rearrange("p (t c) -> p t c", t=16)

    def qk_norm_all(W, wres, specs, rope):
        units = [(wc0, dst, gcol, tc) for (wc0, dst, gcol) in specs for tc in range(NTC)]
        o_sq, o_rs, o_qn, o_t1, o_t2 = O_TMP, O_TMP + 3072, O_TMP + 7168, O_TMP + 9216, O_TMP + 11264
        st = {}

        def s1(u):
            wc0, dst, gcol, tc = units[u]
            tsl = slice(tc * 512, (tc + 1) * 512)
            b = nb()
            for kc in range(8):
                mm(ps[:, b, :], W[:, kc, wc0:wc0 + 128], hT[:, kc, tsl], kc == 0, kc == 7,
                   [wres, "hT:%d:%d" % (kc, tc)], [PSR(b)])
            q = u % 3
            sq = scr(o_sq + q * 1024, 512, BF16)
            act(sq, ps[:, b, :], AF.Square, [PSR(b)], ["QSQ%d" % q])
            st[u] = (b, sq, q)

        def s2(u):
            wc0, dst, gcol, tc = units[u]
            tsl = slice(tc * 512, (tc + 1) * 512)
            b, sq, q = st[u]
            b2 = nb()
            mm(ps[:, b2, :], cm(CM_B64), sq, True, True, ["CM", "QSQ%d" % q], [PSR(b2)])
            r = u % 2
            rs = scr(o_rs + r * 2048, 512, F32)
            rsqrt_eps(rs, ps[:, b2, :], [PSR(b2)], ["QRS%d" % r])
            if not rope:
                stt(qkT[:, dst, tsl], ps[:, b, :], pvc(gcol), rs, ALU.mult, ALU.mult,
                    [PSR(b), "PV", "QRS%d" % r], ["qk:%d:%d" % (dst, tc)])
            else:
                qn = scr(o_qn + r * 1024, 512, BF16)
                stt(qn, ps[:, b, :], pvc(gcol), rs, ALU.mult, ALU.mult, [PSR(b), "PV", "QRS%d" % r], ["QN%d" % r])
                st[u] = (qn, r)

        def s3(u):
            wc0, dst, gcol, tc = units[u]
            tsl = slice(tc * 512, (tc + 1) * 512)
            qn, r = st[u]
            b3 = nb()
            mm(ps[:, b3, :], cm(CM_ROT), qn, True, True, ["CM", "QN%d" % r], [PSR(b3)])
            t1 = scr(o_t1, 512, F32)
            t2 = scr(o_t2, 512, F32)
            tt(t2, qn, ropeC[:, tsl], ALU.mult, ["QN%d" % r, "ropeC"], ["RT2"], eng="pool")
            tt(t1, ps[:, b3, :], ropeS[:, tsl], ALU.mult, [PSR(b3), "ropeS"], ["RT1"])
            tt(qkT[:, dst, tsl], t1, t2, ALU.add, ["RT1", "RT2"], ["qk:%d:%d" % (dst, tc)])

        n = len(units)
        for u in range(n + 2):
            if u < n:
                s1(u)
            if 1 <= u <= n:
                s2(u - 1)
            if rope and 2 <= u <= n + 1:
                s3(u - 2)

    def v_proj(WVv, wres, ncols):
        per_bank = 512 // ncols
        for T0 in range(0, NT, per_bank):
            b = nb()
            for j in range(per_bank):
                T = T0 + j
                for kc in range(8):
                    mm(ps[:, b, j * ncols:(j + 1) * ncols], hT[:, kc, T * 128:(T + 1) * 128], WVv[:, kc, 0:ncols],
                       kc == 0, kc == 7, [wres, "hT:%d:%d" % (kc, T // 4)], [PSR(b)])
            dst = Vt[:, T0:T0 + per_bank, 0:ncols]
            src = ps[:, b, :].rearrange("p (j c) -> p j c", j=per_bank)
            act(dst, src, AF.Copy, [PSR(b)], ["V:%d" % T for T in range(T0, T0 + per_bank)])

    O_EM = O_TMP
    O_PR = O_TMP + 6656
    O_PM = O_PR + 4096
    O_R = O_PM + 4096
    NPS = 4
    LAG = 3

    def attention(kind, branch, l):
        gi = [0]
        R = {"na": 2, "dil": 8, "swa": 1}[kind]

        def em_load(h, part):
            c0 = h * NA_COLS
            if part == "I":
                dst, src, res, sem = scr(O_EM + (h % 2) * 1280, 640, BF16), nab_d[l, :, c0:c0 + 640], "EMI%d" % (h % 2), "emi%d" % (h % 2)
            elif part == "E0":
                dst, src, res, sem = scr(O_EM + 2560, 1024, BF16), nab_d[l, :, c0 + 640:c0 + 1664], "EME0", "eme0"
            else:
                dst, src, res, sem = scr(O_EM + 4608, 1024, BF16), nab_d[l, :, c0 + 1664:c0 + 2688], "EME3", "eme3"
            dma("pool", dst, src, sem, [], [res])
            act(dst, dst, AF.Exp, [res], [res])

        if kind == "na":
            em_load(0, "I")
            em_load(0, "E0")
            em_load(0, "E3")
        for cq in range(2):
            for hp in range(2):
                if kind == "na":
                    hh = cq * 2 + hp
                    if hh + 1 < 4:
                        em_load(hh + 1, "I")
                    if hh > 0:
                        em_load(hh, "E3")
                hs = slice(hp * 64, hp * 64 + 64)
                ck = 2 if kind == "swa" else 2 + cq
                vc0 = 0 if kind == "swa" else cq * 128
                h_sw = hp * 2 + cq
                pend = []
                for g in range(NTC):
                    par = (cq * 2 + hp) * NTC + g
                    ob, db = (0, 1) if par % 2 == 0 else (2, 3)
                    items = []
                    if kind == "na":
                        h = cq * 2 + hp
                        emI = scr(O_EM + (h % 2) * 1280, 640, BF16)
                        emE0 = scr(O_EM + 2560, 1024, BF16)
                        emE3 = scr(O_EM + 4608, 1024, BF16)
                        resI = "EMI%d" % (h % 2)
                        if g == 0:
                            for T in range(4):
                                items.append((T, 0, 1, emE0, T * 256, "EME0"))
                            qint = (2, 3)
                        elif g == 3:
                            qint = (12, 13)
                        else:
                            qint = tuple(range(4 * g, 4 * g + 4))
                        for T in range(NT):
                            qs = [qt for qt in qint if abs(T - qt) <= 2]
                            if qs:
                                items.append((T, qs[0], qs[-1], emI, (2 - (T - qs[0])) * 128, resI))
                        if g == 3:
                            for T in range(12, 16):
                                items.append((T, 14, 15, emE3, (T - 12) * 256, "EME3"))
                    else:
                        base = CM_DIL if kind == "dil" else CM_WIN
                        for T in range(NT):
                            qlo = max(4 * g, T - R)
                            qhi = min(4 * g + 3, T + R)
                            if qlo > qhi:
                                continue
                            items.append((T, qlo, qhi, CM, base + (R - (T - qlo)) * 128, "CM"))
                    for idx, (T, qlo, qhi, mten, mc, mres) in enumerate(items):
                        n = qhi - qlo + 1
                        w = n * 128
                        sb = 4 + gi[0] % 4
                        pslot = gi[0] % NPS
                        gi[0] += 1
                        qres = ["qk:%d:%d" % (cq, g)]
                        mm(ps[:, sb, 0:w], qkT[hs, ck, T * 128:(T + 1) * 128], qkT[hs, cq, qlo * 128:(qhi + 1) * 128],
                           True, True, ["qk:%d:%d" % (ck, T // 4)] + qres, [PSR(sb)])
                        pr = scr(O_PR + pslot * 1024, 512, BF16)
                        pm = scr(O_PM + pslot * 1024, 512, BF16)
                        act(pr[:, 0:w], ps[:, sb, 0:w], AF.Exp, [PSR(sb)], ["PR%d" % pslot], scale=0.125)
                        tt(pm[:, 0:w], pr[:, 0:w], mten[:, mc:mc + w], ALU.mult, ["PR%d" % pslot, mres], ["PM%d" % pslot])
                        oc = (qlo - 4 * g) * 128

                        def pv_stage(T=T, pm=pm, pslot=pslot, w=w, oc=oc, ob=ob, db=db, vc0=vc0,
                                     first=(idx == 0), last=(idx == len(items) - 1)):
                            mm(ps[:, ob, oc:oc + w], Vt[:, T, vc0:vc0 + 128], pm[:, 0:w], first, last,
                               ["V:%d" % T, "PM%d" % pslot], [PSR(ob)])
                            mm(ps[:, db, oc:oc + w], cm(CM_ONE), pm[:, 0:w], first, last,
                               ["CM", "PM%d" % pslot], [PSR(db)])
                        pend.append(pv_stage)
                        while len(pend) > LAG:
                            pend.pop(0)()

                    def norm_stage(g=g, ob=ob, db=db, hs=hs, cq=cq, hp=hp, h_sw=h_sw):
                        rq = 0
                        r = scr(O_R, 512, F32)
                        if kind == "swa":
                            act(r[hs, :], ps[hs, db, :], AF.Ln, [PSR(db), "ESK"], ["NR%d" % rq], bias=ESK[hs, h_sw:h_sw + 1])
                        else:
                            act(r[hs, :], ps[hs, db, :], AF.Ln, [PSR(db)], ["NR%d" % rq])
                        act(r[hs, :], r[hs, :], AF.Exp, ["NR%d" % rq], ["NR%d" % rq], scale=-1.0)
                        tt(yT[hs, branch * 2 + cq, g * 512:(g + 1) * 512], ps[hs, ob, :], r[hs, :], ALU.mult,
                           [PSR(ob), "NR%d" % rq], ["yT:%d:%d:%d" % (branch * 2 + cq, g, hp)])
                    pend.append(norm_stage)
                    if kind == "na" and g == 0 and cq * 2 + hp + 1 < 4:
                        em_load(cq * 2 + hp + 1, "E0")
                while pend:
                    pend.pop(0)()

    def branch_loads(kind, l, qbase):
        slot = next_wslot()
        W = WB[:, slot, :].rearrange("p (k n) -> p k n", k=8)
        if kind != "swa":
            wload(slot, W, w_in_d[l, :, qbase:qbase + 512].rearrange("(k p) n -> p k n", p=128))
        else:
            for bb in range(2):
                for a in range(2):
                    hh = a * 2 + bb
                    src = w_in_d[l, :, qbase + hh * 64:qbase + (hh + 1) * 64].rearrange("(k p) d -> p k d", p=128)
                    dst = W[:, :, bb * 128 + a * 64:bb * 128 + (a + 1) * 64]
                    wload(slot, dst, src)
            wload(slot, W[:, :, 256:384], w_in_d[l, :, qbase + 256:qbase + 384].rearrange("(k p) n -> p k n", p=128))
        slot2 = next_wslot()
        W2 = WB[:, slot2, :].rearrange("p (k n) -> p k n", k=8)
        if kind != "swa":
            wload(slot2, W2[:, :, 0:256], w_in_d[l, :, qbase + 512:qbase + 768].rearrange("(k p) n -> p k n", p=128))
        else:
            wload(slot2, W2[:, :, 0:128], w_in_d[l, :, qbase + 384:qbase + 512].rearrange("(k p) n -> p k n", p=128))
        return (slot, slot2)

    def branch_attn(kind, branch, l, slots, gq, gk, rope, next_loads=None):
        slot, slot2 = slots
        W = WB[:, slot, :].rearrange("p (k n) -> p k n", k=8)
        W2 = WB[:, slot2, :].rearrange("p (k n) -> p k n", k=8)
        nvc = 128 if kind == "swa" else 256
        specs = [(0, 0, gq), (128, 1, gq), (256, 2, gk)]
        if kind != "swa":
            specs.append((384, 3, gk))
        qk_norm_all(W, "WB%d" % slot, specs, rope)
        v_proj(W2, "WB%d" % slot2, nvc)
        barrier()
        nxt = next_loads() if next_loads is not None else None
        attention(kind, branch, l)
        return nxt

    O_MIX = 0
    O_G = 32768
    O_ACC = O_G + 2048
    O_MT = O_ACC + 2048
    mixT = scr(O_MIX, 8 * SEQ, BF16).rearrange("p (c t) -> p c t", c=8)

    def merge(l):
        YT_ALL = None
        for dc in range(8):
            slot = next_wslot()
            W = WB[:, slot, :].rearrange("p (k n d) -> p k n d", k=8, n=4)
            for n in range(4):
                wload(slot, W[:, :, n, :], w_in_d[l, :, n * 1024 + dc * 128:n * 1024 + (dc + 1) * 128].rearrange("(k p) d -> p k d", p=128))
            bs = dc % 2
            Wb = WBR[:, bs, :].rearrange("p (k n d) -> p k n d", k=2, n=4)
            for n in range(3):
                dma("pool", Wb[:, :, n, :], w_br_d[l, n, :, dc * 128:(dc + 1) * 128].rearrange("(k p) d -> p k d", p=128),
                    "wbr%d" % bs, [], ["WBR%d" % bs])
            for half in range(2):
                dma("pool", Wb[half * 64:(half + 1) * 64, :, 3, :],
                    w_br_d[l, 3, half * 128:(half + 1) * 128, dc * 128:(dc + 1) * 128].rearrange("(k p) d -> p k d", p=64),
                    "wbr%d" % bs, [], ["WBR%d" % bs])
            for tc in range(NTC):
                tsl = slice(tc * 512, (tc + 1) * 512)
                aq = 0
                acc = scr(O_ACC, 512, F32)
                for n in range(4):
                    bg, bb = nb(), nb()
                    for kc in range(8):
                        mm(ps[:, bg, :], W[:, kc, n, :], hT[:, kc, tsl], kc == 0, kc == 7,
                           ["WB%d" % slot, "hT:%d:%d" % (kc, tc)], [PSR(bg)])
                    for kc in range(2):
                        ych = n * 2 + kc
                        if n == 0:
                            yres = ["yT:%d:%d" % (ych, tc)]
                        else:
                            yres = ["yT:%d:%d:%d" % (ych, tc, hp) for hp in range(2)]
                        mm(ps[:, bb, :], Wb[:, kc, n, :], yT[:, ych, tsl], kc == 0, kc == 1,
                           ["WBR%d" % bs] + yres, [PSR(bb)])
                    gq = (tc * 4 + n) % 2
                    G = scr(O_G + gq * 1024, 512, BF16)
                    act(G, ps[:, bg, :], AF.Sigmoid, [PSR(bg), "PV"], ["G%d" % gq], bias=pvc(PV_GATEB + n * 8 + dc))
                    if n == 0:
                        tt(acc, ps[:, bb, :], G, ALU.mult, [PSR(bb), "G%d" % gq], ["ACC%d" % aq])
                    else:
                        mt = scr(O_MT, 512, F32)
                        tt(mt, ps[:, bb, :], G, ALU.mult, [PSR(bb), "G%d" % gq], ["MT"])
                        if n < 3:
                            tt(acc, acc, mt, ALU.add, ["ACC%d" % aq, "MT"], ["ACC%d" % aq])
                        else:
                            tt(mixT[:, dc, tsl], acc, mt, ALU.add, ["ACC%d" % aq, "MT"], ["mix:%d:%d" % (dc, tc)])
        for dg in range(2):
            slot = next_wslot()
            W = WB[:, slot, :].rearrange("p (k n) -> p k n", k=8)
            wload(slot, W, w_out_d[l, :, dg * 512:(dg + 1) * 512].rearrange("(k p) n -> p k n", p=128))
            for j in range(4):
                do = dg * 4 + j
                for tc in range(NTC):
                    tsl = slice(tc * 512, (tc + 1) * 512)
                    b = nb()
                    for kc in range(8):
                        mm(ps[:, b, :], W[:, kc, j * 128:(j + 1) * 128], mixT[:, kc, tsl], kc == 0, kc == 7,
                           ["WB%d" % slot, "mix:%d:%d" % (kc, tc)], [PSR(b)])
                    tt(xT[:, do, tsl], xT[:, do, tsl], ps[:, b, :], ALU.add, ["xT:%d:%d" % (do, tc), PSR(b)],
                       ["xT:%d:%d" % (do, tc)])

    O_U = 12288
    O_SG = O_U + 3 * 2080
    yT_flat = yT[:].rearrange("p c t -> p (c t)")

    def actT(j):
        if j < 16:
            return yT_flat[:, j * 1024:(j + 1) * 1024]
        return scr((j - 16) * 2048, 1024, BF16)

    def ffn(l):
        rmsnorm(PV_GFFN)
        barrier()
        ucount = [0]
        for th in range(2):
            t0 = th * 1024
            halo_tok = 1024 if th == 0 else 1023
            halo_col = 1025 if th == 0 else 0
            zero_col = 0 if th == 0 else 1025
            for u in range(3):
                U = scr(O_U + u * 2080, 1026, BF16)
                S.add("dve", lambda e, U=U, zc=zero_col: e.memset(U[:, zc:zc + 1], 0.0), writes=["U%d" % u])
            pendB = []
            sg_by_j = {}
            for jp in range(11):
                slot = next_wslot()
                W = WB[:, slot, :].rearrange("p (k a n) -> p k a n", k=8, a=2)
                for a in range(2):
                    wload(slot, W[:, :, a, :], w_up_d[l, :, a * D_FF + jp * 256: a * D_FF + (jp + 1) * 256].rearrange("(k p) n -> p k n", p=128))
                for jj in range(2):
                    j = jp * 2 + jj
                    for a in range(2):
                        u = ucount[0] % 3
                        ucount[0] += 1
                        U = scr(O_U + u * 2080, 1026, BF16)
                        ures = "U%d" % u
                        ch = a * 22 + j
                        wv = W[:, :, a, jj * 128:(jj + 1) * 128]
                        for t2 in range(2):
                            b = nb()
                            tok = t0 + t2 * 512
                            for kc in range(8):
                                mm(ps[:, b, :], wv[:, kc, :], hT[:, kc, tok:tok + 512], kc == 0, kc == 7,
                                   ["WB%d" % slot, "hT:%d:%d" % (kc, tok // 512)], [PSR(b)])
                            act(U[:, 1 + t2 * 512:1 + (t2 + 1) * 512], ps[:, b, :], AF.Copy, [PSR(b)], [ures])
                        if th == 0:
                            b = nb()
                            for kc in range(8):
                                mm(ps[:, b, 0:1], wv[:, kc, :], hT[:, kc, 1024:1025], kc == 0, kc == 7,
                                   ["WB%d" % slot, "hT:%d:%d" % (kc, 2)], [PSR(b)])
                            S.add("dve", lambda e, U=U, b=b: e.tensor_copy(out=U[:, 1025:1026], in_=ps[:, b, 0:1]),
                                  reads=[PSR(b)], writes=[ures])
                            S.add("dve", lambda e, U=U, ch=ch: e.tensor_copy(out=HS[:, ch:ch + 1], in_=U[:, 1024:1025]),
                                  reads=[ures], writes=["HS%d" % ch])
                        else:
                            S.add("dve", lambda e, U=U, ch=ch: e.tensor_copy(out=U[:, 0:1], in_=HS[:, ch:ch + 1]),
                                  reads=["HS%d" % ch], writes=[ures])

                        def stageB(U=U, ures=ures, ch=ch, a=a, j=j):
                            for t2 in range(2):
                                b = nb()
                                c0 = PV_FCW + ch * 3
                                tt(ps[:, b, :], U[:, t2 * 512:t2 * 512 + 512], PV[:, c0:c0 + 1].to_broadcast([128, 512]), ALU.mult,
                                   [ures, "PV"], [PSR(b)])
                                for k in (1, 2):
                                    stt(ps[:, b, :], U[:, t2 * 512 + k:t2 * 512 + k + 512], pvc(c0 + k), ps[:, b, :], ALU.mult, ALU.add,
                                        [ures, "PV", PSR(b)], [PSR(b)])
                                if a == 0:
                                    sq = t2
                                    sg = scr(O_SG + sq * 1024, 512, BF16)
                                    act(sg, ps[:, b, :], AF.Silu, [PSR(b), "PV"], ["SG%d" % sq], bias=pvc(PV_FCB + ch))
                                else:
                                    sg = scr(O_SG + t2 * 1024, 512, BF16)
                                    stt(actT(j)[:, t2 * 512:(t2 + 1) * 512], ps[:, b, :], pvc(PV_FCB + ch), sg, ALU.add, ALU.mult,
                                        [PSR(b), "PV", "SG%d" % t2], ["act:%d:%d" % (j, t2)])
                        pendB.append(stageB)
                        while len(pendB) > 1:
                            pendB.pop(0)()
            while pendB:
                pendB.pop(0)()
            for do in range(8):
                slot = next_wslot()
                W = WB[:, slot, 0:22 * 128].rearrange("p (k n) -> p k n", k=22)
                for hh in range(2):
                    wload(slot, W[:, hh * 11:(hh + 1) * 11, :],
                          w_dn_d[l, hh * 1408:(hh + 1) * 1408, do * 128:(do + 1) * 128].rearrange("(k p) n -> p k n", p=128))
                for t2 in range(2):
                    b = nb()
                    tok = t0 + t2 * 512
                    for j in range(22):
                        mm(ps[:, b, :], W[:, j, :], actT(j)[:, t2 * 512:(t2 + 1) * 512], j == 0, j == 21,
                           ["WB%d" % slot, "act:%d:%d" % (j, t2)], [PSR(b)])
                    tt(xT[:, do, tok:tok + 512], xT[:, do, tok:tok + 512], ps[:, b, :], ALU.add,
                       ["xT:%d:%d" % (do, tok // 512), PSR(b)], ["xT:%d:%d" % (do, tok // 512)])

    SCR_ALL = "SCRALL"


    scr_names_A0 = ["SQ0", "SQ1", "SQ2", "RS0", "RS1", "XST0", "XST1"]
    scr_names_conf = ["UPpad0", "UPpad1", "SIG0", "SIG1", "MSQ", "LRS", "LT0", "LT1"] + \
        ["UP:%d:%d" % (c, t) for c in range(2) for t in range(NTC)] + \
        ["CVF%d%d" % (q, c) for q in range(2) for c in range(2)] + ["CVB%d%d" % (q, c) for q in range(2) for c in range(2)] + \
        ["SQB%d%d" % (q, c) for q in range(2) for c in range(2)]
    scr_names_attn = ["qk:%d:%d" % (c, t) for c in range(4) for t in range(NTC)] + ["V:%d" % T for T in range(NT)] + \
        ["QSQ0", "QSQ1", "QSQ2", "QRS0", "QRS1", "QN0", "QN1", "RT1", "RT2", "EMI0", "EMI1", "EME0", "EME3", "PR0", "PR1", "PR2", "PR3", "PM0", "PM1", "PM2", "PM3", "NR0", "NR1"]
    scr_names_merge = ["mix:%d:%d" % (c, t) for c in range(8) for t in range(NTC)] + ["G0", "G1", "ACC0", "ACC1", "MT"]
    scr_names_ffn = ["U0", "U1", "U2", "SG0", "SG1"] + ["act:%d:%d" % (j, t) for j in range(22) for t in range(2)]
    yT_names = ["yT:%d:%d" % (c, t) for c in range(2) for t in range(NTC)] + \
        ["yT:%d:%d:%d" % (c, t, hp) for c in range(2, 8) for t in range(NTC) for hp in range(2)]
    ALLSCR = scr_names_A0 + scr_names_conf + scr_names_attn + scr_names_merge + scr_names_ffn

    def barrier(extra=()):
        names = ALLSCR + list(extra)
        S.add("dve", lambda e: e.memset(DUM[:, 0:1], 0.0), reads=names, writes=names)

    for s in range(nseq):
        barrier(yT_names)
        load_x(s)
        for li in range(nl):
            l = l0 + li
            dma("sp", PV[:], pvec_d[l], "pvl", [], ["PV"])
            act(ESK[:], PV[:, PV_SINK:PV_SINK + 4], AF.Exp, ["PV"], ["ESK"])
            barrier()
            rmsnorm(PV_GMIX)
            barrier()
            conformer(l)
            barrier()
            sl_na = branch_loads("na", l, 4608)
            sl_dil = branch_attn("na", 1, l, sl_na, PV_QKG + 0, PV_QKG + 1, False,
                                 next_loads=lambda l=l: branch_loads("dil", l, 5376))
            barrier()
            sl_swa = branch_attn("dil", 2, l, sl_dil, PV_QKG + 2, PV_QKG + 3, True,
                                 next_loads=lambda l=l: branch_loads("swa", l, 6144))
            barrier()
            branch_attn("swa", 3, l, sl_swa, PV_QKG + 4, PV_QKG + 5, True)
            barrier()
            merge(l)
            barrier(yT_names)
            ffn(l)
            barrier(yT_names)
        store_x(s)
    S.add("sp", lambda e: None, reads=["OUT0", "OUT1"], writes=[])

    S.finalize()
    with contextlib.ExitStack() as st:
        sems = {}
        for e in ENGS:
            sems[e] = st.enter_context(nc.semaphore("s_" + e))
        for k in S.dcounts:
            sems[k] = st.enter_context(nc.semaphore("d_" + k))
        block = st.enter_context(nc.Block())
        block.sync(lambda e: S.run_engine("sp", e, sems))
        block.scalar(lambda e: S.run_engine("act", e, sems))
        block.vector(lambda e: S.run_engine("dve", e, sems))
        block.gpsimd(lambda e: S.run_engine("pool", e, sems))
        block.tensor(lambda e: S.run_engine("pe", e, sems))
    return nc, S


def _const_mats():
    kk = np.arange(128)[:, None]
    qq = np.arange(128)[None, :]
    cmat = np.zeros((128, NCM), np.float32)
    cmat[:, CM_WIN:CM_WIN + 128] = (kk <= qq)
    cmat[:, CM_WIN + 128:CM_WIN + 256] = 1.0
    cmat[:, CM_WIN + 256:CM_WIN + 384] = (kk >= qq)
    for d in range(-8, 9):
        o = 128 * d + kk - qq
        m = (np.abs(o) <= 64).astype(np.float32)
        m += ((o % 4 == 0) & (np.abs(o) <= 256)).astype(np.float32)
        m += ((o % 16 == 0) & (np.abs(o) <= 1024)).astype(np.float32)
        cmat[:, CM_DIL + (8 - d) * 128: CM_DIL + (9 - d) * 128] = m
    cmat[:, CM_ID:CM_ID + 128] = np.eye(128, dtype=np.float32)
    cmat[:, CM_O1024:CM_O1024 + 128] = 1.0 / 1024.0
    blk = np.zeros((128, 128), np.float32)
    blk[0:64, 0:64] = 1.0 / 64.0
    blk[64:128, 64:128] = 1.0 / 64.0
    cmat[:, CM_B64:CM_B64 + 128] = blk
    cmat[:, CM_ONE:CM_ONE + 128] = 1.0
    rot = np.zeros((128, 128), np.float32)
    for hb in range(2):
        for d in range(8):
            rot[hb * 64 + d + 8, hb * 64 + d] = -1.0
            rot[hb * 64 + d, hb * 64 + d + 8] = 1.0
    cmat[:, CM_ROT:CM_ROT + 128] = rot
    cmat[:, CM_O256:CM_O256 + 128] = 1.0 / 256.0
    pos = np.arange(SEQ, dtype=np.float32)
    inv = (np.float32(500000.0) ** (-np.arange(0, 16, 2, dtype=np.float32) / np.float32(16))).astype(np.float32)
    ang = (pos[:, None] * inv[None, :]).astype(np.float32)
    cosT = np.cos(ang).astype(np.float32).T
    sinT = np.sin(ang).astype(np.float32).T
    rc = np.ones((128, SEQ), np.float32)
    rs = np.zeros((128, SEQ), np.float32)
    for hb in range(2):
        for d in range(16):
            rc[hb * 64 + d] = cosT[d % 8]
            rs[hb * 64 + d] = sinT[d % 8]
    return cmat, rc, rs, np.eye(128, dtype=np.float32)


def _na_index():
    blocks = [(2 + d, 2) for d in (2, 1, 0, -1, -2)]
    blocks += [(T, qt) for T in range(4) for qt in (0, 1)]
    blocks += [(T, qt) for T in range(12, 16) for qt in (14, 15)]
    dr = np.zeros((128, NA_COLS), np.int64)
    dc = np.zeros((128, NA_COLS), np.int64)
    valid = np.zeros((128, NA_COLS), bool)
    kk = np.arange(128)[:, None]
    qq = np.arange(128)[None, :]
    for i, (T, qt) in enumerate(blocks):
        col = i * 128
        qtok = qt * 128 + qq
        r = qtok // 64
        c = qtok % 64
        rs = np.clip(r - 4, 0, 24)
        cs = np.clip(c - 8, 0, 48)
        ktok = T * 128 + kk
        kr = ktok // 64
        kc = ktok % 64
        v = (kr >= rs) & (kr < rs + 8) & (kc >= cs) & (kc < cs + 16)
        dr[:, col:col + 128] = np.where(v, kr - r + 7, 0)
        dc[:, col:col + 128] = np.where(v, kc - c + 15, 0)
        valid[:, col:col + 128] = v
    return dr, dc, valid


def _host_layout(inp):
    f = np.float32
    L = DEPTH
    pv = np.zeros((L, 128, NPV), f)
    pv[:, :, PV_GMIX:PV_GMIX + 8] = inp["g_mix"].reshape(L, 8, 128).transpose(0, 2, 1)
    pv[:, :, PV_GFFN:PV_GFFN + 8] = inp["g_ffn"].reshape(L, 8, 128).transpose(0, 2, 1)
    pv[:, :, PV_GATEB:PV_GATEB + 32] = inp["gate_b"].reshape(L, 4, 8, 128).transpose(0, 3, 1, 2).reshape(L, 128, 32)
    pv[:, :, PV_ACW:PV_ACW + 62] = inp["a_conv_w"].reshape(L, 31, 2, 128).transpose(0, 3, 2, 1).reshape(L, 128, 62)
    pv[:, :, PV_ACB:PV_ACB + 2] = inp["a_conv_b"].reshape(L, 2, 128).transpose(0, 2, 1)
    pv[:, :, PV_ALNG:PV_ALNG + 2] = inp["a_ln_g"].reshape(L, 2, 128).transpose(0, 2, 1)
    pv[:, :, PV_ALNB:PV_ALNB + 2] = inp["a_ln_b"].reshape(L, 2, 128).transpose(0, 2, 1)
    for i, k in enumerate(["na_qn", "na_kn", "dil_qn", "dil_kn", "swa_qn", "swa_kn"]):
        pv[:, :, PV_QKG + i] = np.tile(inp[k], (1, 2))
    pv[:, :, PV_SINK:PV_SINK + 4] = inp["swa_sink"][:, None, :]
    pv[:, :, PV_FCW:PV_FCW + 132] = inp["ffn_conv_w"].reshape(L, 3, 44, 128).transpose(0, 3, 2, 1).reshape(L, 128, 132)
    pv[:, :, PV_FCB:PV_FCB + 44] = inp["ffn_conv_b"].reshape(L, 44, 128).transpose(0, 2, 1)
    dr, dc, valid = _na_index()
    rpb = inp["na_rpb"]
    g = rpb[:, :, dr, dc]
    g = np.where(valid[None, None], g, f(-30000.0)).astype(f)
    nab = np.ascontiguousarray(g.transpose(0, 2, 1, 3).reshape(L, 128, 4 * NA_COLS))
    return pv, nab


_PROG_CACHE = {}


def _get_prog(nl, nseq, l0):
    key = (nl, nseq, l0)
    if key not in _PROG_CACHE:
        _PROG_CACHE[key] = build_program(nl, nseq, l0)[0]
    return _PROG_CACHE[key]


def run_layers(inp, x, nl, l0, ncores, nseq):
    cmat, rc, rs, idf = _const_mats()
    pv, nab = _host_layout(inp)
    nc = _get_prog(nl, nseq, l0)
    c32 = lambda a: np.ascontiguousarray(np.asarray(a, dtype=np.float32))
    shared = {
        "w_in": c32(inp["w_in"]), "w_branch": c32(inp["w_branch"]), "w_out": c32(inp["w_out"]),
        "w_up": c32(inp["w_up"]), "w_down": c32(inp["w_down"]), "pvec": pv, "nab": nab,
        "cmat": cmat, "ropec": rc, "ropes": rs, "identf": idf,
    }
    in_maps = []
    for c in range(ncores):
        m = dict(shared)
        m["x"] = np.ascontiguousarray(x[c * nseq:(c + 1) * nseq])
        in_maps.append(m)
    res = run_bass_kernel_spmd(nc, in_maps, core_ids=list(range(ncores)))
    return np.concatenate([np.asarray(r["out"]) for r in res.results], axis=0)


def kernel(**inputs):
    inp = {k: np.asarray(v) for k, v in inputs.items()}
    x = np.ascontiguousarray(inp["x"].astype(np.float32))
    out = run_layers(inp, x, DEPTH, 0, 8, 2)
    return out.astype(np.float32)
```

```python
import contextlib
import numpy as np
import concourse.bass as bass
import concourse.mybir as mybir
from concourse.bass_utils import run_bass_kernel_spmd

F32 = mybir.dt.float32
BF16 = mybir.dt.bfloat16
AF = mybir.ActivationFunctionType
ALU = mybir.AluOpType

D_MODEL = 1024
SEQ = 2048
DEPTH = 4
N_IN = 6656
D_FF = 2816
EPS = 1e-6
NT = 16
NTC = 4
ENGS = ("pe", "act", "dve", "pool", "sp")

PV_GMIX = 0
PV_GFFN = 8
PV_GATEB = 16
PV_ACW = 48
PV_ACB = 110
PV_ALNG = 112
PV_ALNB = 114
PV_QKG = 116
PV_SINK = 122
PV_FCW = 126
PV_FCB = 258
NPV = 304

CM_WIN = 0
CM_DIL = 384
CM_ID = CM_DIL + 17 * 128
CM_O1024 = CM_ID + 128
CM_B64 = CM_O1024 + 128
CM_ONE = CM_B64 + 128
CM_ROT = CM_ONE + 128
CM_O256 = CM_ROT + 128
NCM = CM_O256 + 128

NA_COLS = 2688


class _Op:
    __slots__ = ("eng", "fn", "deps", "dma_sem")


class Sched:
    def __init__(self):
        self.ops = []
        self.last_w = {}
        self.readers = {}

    def add(self, eng, fn, reads=(), writes=(), dma_sem=None):
        op = _Op()
        op.eng = eng
        op.fn = fn
        op.dma_sem = dma_sem
        deps = set()
        lw = self.last_w
        rd = self.readers
        for r in reads:
            w = lw.get(r)
            if w is not None:
                deps.add(w)
        for r in writes:
            w = lw.get(r)
            if w is not None:
                deps.add(w)
            x = rd.get(r)
            if x:
                deps.update(x.values())
        idx = len(self.ops)
        deps.discard(idx)
        op.deps = deps
        self.ops.append(op)
        for r in writes:
            lw[r] = idx
            rd[r] = {}
        rkey = eng if dma_sem is None else (eng, idx)
        for r in reads:
            if r in writes:
                continue
            l = rd.get(r)
            if l is None:
                rd[r] = {rkey: idx}
            else:
                l[rkey] = idx
        return idx

    def finalize(self):
        ops = self.ops
        n = len(ops)
        needs_inc = [False] * n
        for i, op in enumerate(ops):
            for d in op.deps:
                od = ops[d]
                if od.dma_sem is None and od.eng != op.eng:
                    needs_inc[d] = True
        cnt = {e: 0 for e in ENGS}
        dcnt = {}
        sig = [None] * n
        for i, op in enumerate(ops):
            if op.dma_sem is not None:
                dcnt[op.dma_sem] = dcnt.get(op.dma_sem, 0) + 16
                sig[i] = (op.dma_sem, dcnt[op.dma_sem])
            elif needs_inc[i]:
                cnt[op.eng] += 1
                sig[i] = (op.eng, cnt[op.eng])
        eng_vc = {e: {} for e in ENGS}
        op_vc = [None] * n
        waits = [None] * n
        for i, op in enumerate(ops):
            vc = eng_vc[op.eng]
            w = {}
            for d in sorted(op.deps):
                s = sig[d]
                if s is None:
                    continue
                if vc.get(s[0], 0) >= s[1]:
                    continue
                if w.get(s[0], 0) < s[1]:
                    w[s[0]] = s[1]
                dv = op_vc[d]
                for k2, v2 in dv.items():
                    if vc.get(k2, 0) < v2:
                        vc[k2] = v2
                vc[s[0]] = s[1]
            waits[i] = w
            if sig[i] is not None:
                snap = dict(vc)
                snap[sig[i][0]] = sig[i][1]
                op_vc[i] = snap
                if op.dma_sem is None:
                    vc[sig[i][0]] = sig[i][1]
        self.sig = sig
        self.waits = waits
        self.counts = cnt
        self.dcounts = dcnt

    def run_engine(self, ename, eng, sems):
        ops = self.ops
        sig = self.sig
        waits = self.waits
        for i, op in enumerate(ops):
            if op.eng != ename:
                continue
            for key, val in waits[i].items():
                eng.wait_ge(sems[key], val)
            ins = op.fn(eng)
            s = sig[i]
            if s is not None:
                ins.then_inc(sems[s[0]], 16 if op.dma_sem is not None else 1)


def build_program(nl=DEPTH, nseq=2, l0=0):
    nc = bass.Bass("TRN2", target_bir_lowering=False)
    x_d = nc.dram_tensor("x", [nseq, SEQ, D_MODEL], F32, kind="ExternalInput").ap()
    w_in_d = nc.dram_tensor("w_in", [DEPTH, D_MODEL, N_IN], F32, kind="ExternalInput").ap()
    w_br_d = nc.dram_tensor("w_branch", [DEPTH, 4, 256, D_MODEL], F32, kind="ExternalInput").ap()
    w_out_d = nc.dram_tensor("w_out", [DEPTH, D_MODEL, D_MODEL], F32, kind="ExternalInput").ap()
    w_up_d = nc.dram_tensor("w_up", [DEPTH, D_MODEL, 2 * D_FF], F32, kind="ExternalInput").ap()
    w_dn_d = nc.dram_tensor("w_down", [DEPTH, D_FF, D_MODEL], F32, kind="ExternalInput").ap()
    pvec_d = nc.dram_tensor("pvec", [DEPTH, 128, NPV], F32, kind="ExternalInput").ap()
    nab_d = nc.dram_tensor("nab", [DEPTH, 128, 4 * NA_COLS], F32, kind="ExternalInput").ap()
    cm_d = nc.dram_tensor("cmat", [128, NCM], F32, kind="ExternalInput").ap()
    rc_d = nc.dram_tensor("ropec", [128, SEQ], F32, kind="ExternalInput").ap()
    rs_d = nc.dram_tensor("ropes", [128, SEQ], F32, kind="ExternalInput").ap()
    idf_d = nc.dram_tensor("identf", [128, 128], F32, kind="ExternalInput").ap()
    out_d = nc.dram_tensor("out", [nseq, SEQ, D_MODEL], F32, kind="ExternalOutput").ap()

    xT = nc.alloc_sbuf_tensor("xT", [128, 8, SEQ], F32)
    hT = nc.alloc_sbuf_tensor("hT", [128, 8, SEQ], BF16)
    yT = nc.alloc_sbuf_tensor("yT", [128, 8, SEQ], BF16)
    SCR_N = 20736
    SCR = nc.alloc_sbuf_tensor("scr", [128, SCR_N], BF16)
    WB = nc.alloc_sbuf_tensor("wb", [128, 2, 4096], BF16)
    WBR = nc.alloc_sbuf_tensor("wbr", [128, 2, 1024], BF16)
    ropeC = nc.alloc_sbuf_tensor("ropeC", [128, SEQ], BF16)
    ropeS = nc.alloc_sbuf_tensor("ropeS", [128, SEQ], BF16)
    CM = nc.alloc_sbuf_tensor("cm", [128, NCM], BF16)
    identF = nc.alloc_sbuf_tensor("identF", [128, 128], F32)
    PV = nc.alloc_sbuf_tensor("pv", [128, NPV], F32)
    ESK = nc.alloc_sbuf_tensor("esk", [128, 4], F32)
    DG = nc.alloc_sbuf_tensor("dg", [128, 6, 128], BF16)
    DUM = nc.alloc_sbuf_tensor("dum", [128, 16], BF16)
    HS = nc.alloc_sbuf_tensor("hs", [128, 48], BF16)
    ps = nc.alloc_psum_tensor("ps", [128, 8, 512], F32)

    S = Sched()

    def scr(off_b, n_el, dt):
        assert off_b % 4 == 0
        if dt is BF16:
            assert off_b // 2 + n_el <= SCR_N, (off_b, n_el)
            return SCR[:, off_b // 2: off_b // 2 + n_el]
        assert off_b // 2 + 2 * n_el <= SCR_N, (off_b, n_el)
        return SCR[:, off_b // 2: off_b // 2 + 2 * n_el].bitcast(F32)

    yT67 = yT[:, 6:8, :].rearrange("p c t -> p (c t)")

    def ytmp(off_b, n_el, dt):
        assert off_b % 4 == 0
        if dt is BF16:
            assert off_b // 2 + n_el <= 4096
            return yT67[:, off_b // 2: off_b // 2 + n_el]
        assert off_b // 2 + 2 * n_el <= 4096
        return yT67[:, off_b // 2: off_b // 2 + 2 * n_el].bitcast(F32)

    ring = [0]

    def nb():
        b = ring[0]
        ring[0] = (b + 1) % 8
        return b

    def PSR(b):
        return "ps%d" % b

    def cm(c0, n=128):
        return CM[:, c0:c0 + n]

    def pvc(c):
        return PV[:, c:c + 1]

    def mm(out, lhsT, rhs, start, stop, reads, writes):
        S.add("pe", lambda e: e.matmul(out, lhsT, rhs, start=start, stop=stop), reads=reads, writes=writes)

    def act(out, in_, func, reads, writes, bias=None, scale=None):
        kw = {}
        if bias is not None:
            kw["bias"] = bias
        if scale is not None:
            kw["scale"] = scale
        S.add("act", lambda e: e.activation(out=out, in_=in_, func=func, **kw), reads=reads, writes=writes)

    def tt(out, in0, in1, op, reads, writes, eng="dve"):
        S.add(eng, lambda e: e.tensor_tensor(out=out, in0=in0, in1=in1, op=op), reads=reads, writes=writes)

    def ts(out, in0, s1, s2, op0, op1, reads, writes, eng="dve"):
        if op1 is None:
            S.add(eng, lambda e: e.tensor_scalar(out=out, in0=in0, scalar1=s1, scalar2=None, op0=op0),
                  reads=reads, writes=writes)
        else:
            S.add(eng, lambda e: e.tensor_scalar(out=out, in0=in0, scalar1=s1, scalar2=s2, op0=op0, op1=op1),
                  reads=reads, writes=writes)

    def rsqrt_eps(out, in_, reads, writes):
        act(out, in_, AF.Ln, reads, writes, bias=EPS, scale=1.0)
        act(out, out, AF.Exp, list(writes), list(writes), scale=-0.5)

    def stt(out, in0, scalar, in1, op0, op1, reads, writes, eng="dve"):
        S.add(eng, lambda e: e.scalar_tensor_tensor(out=out, in0=in0, scalar=scalar, in1=in1, op0=op0, op1=op1),
              reads=reads, writes=writes)

    def dma(eng, out, in_, sem, reads, writes):
        S.add(eng, lambda e: e.dma_start(out=out, in_=in_), reads=reads, writes=writes, dma_sem=sem)

    dma("pool", CM[:], cm_d, "c0", [], ["CM"])
    dma("pool", ropeC[:], rc_d, "c1", [], ["ropeC"])
    dma("pool", ropeS[:], rs_d, "c2", [], ["ropeS"])
    dma("sp", identF[:], idf_d, "c3", [], ["identF"])

    HT_ALL = ["hT:%d:%d" % (kc, tc) for kc in range(8) for tc in range(NTC)]

    def ht_tc(tc):
        return ["hT:%d:%d" % (kc, tc) for kc in range(8)]

    wslot = [0]

    def next_wslot():
        s = wslot[0]
        wslot[0] = 1 - s
        return s

    def wload(slot, dst, src):
        dma("pool", dst, src, "w%d" % slot, [], ["WB%d" % slot])

    dgslot = [0]

    def make_diag(col):
        s = dgslot[0]
        dgslot[0] = (s + 1) % 6
        tt(DG[:, s, :], cm(CM_ID), PV[:, col:col + 1].to_broadcast([128, 128]), ALU.mult, ["CM", "PV"], ["DG%d" % s])
        return DG[:, s, :], "DG%d" % s

    def load_x(s):
        for T in range(NT):
            sl = T % 2
            st = scr(sl * 4096, 1024, F32)
            dma("sp", st, x_d[s, T * 128:(T + 1) * 128, :], "xs%d" % sl, [], ["XST%d" % sl])
            for half in range(2):
                b = nb()
                for j in range(4):
                    kc = half * 4 + j
                    mm(ps[:, b, j * 128:(j + 1) * 128], st[:, kc * 128:(kc + 1) * 128], identF[:], True, True,
                       ["XST%d" % sl, "identF"], [PSR(b)])
                eng = "dve" if half == 0 else "act"
                dst = xT[:, half * 4:(half + 1) * 4, T * 128:(T + 1) * 128]
                src = ps[:, b, :].rearrange("p (j t) -> p j t", j=4)
                wr = ["xT:%d:%d" % (half * 4 + j, T // 4) for j in range(4)]
                if eng == "dve":
                    S.add("dve", lambda e, dst=dst, src=src: e.tensor_copy(out=dst, in_=src), reads=[PSR(b)], writes=wr)
                else:
                    act(dst, src, AF.Copy, [PSR(b)], wr)

    def store_x(s):
        for T in range(NT):
            sl = T % 2
            st = scr(sl * 4096, 1024, F32)
            for half in range(2):
                b = nb()
                for j in range(4):
                    kc = half * 4 + j
                    mm(ps[:, b, j * 128:(j + 1) * 128], xT[:, kc, T * 128:(T + 1) * 128], identF[:], True, True,
                       ["xT:%d:%d" % (kc, T // 4), "identF"], [PSR(b)])
                dst = st[:, half * 512:(half + 1) * 512]
                if half == 0:
                    S.add("dve", lambda e, dst=dst, b=b: e.tensor_copy(out=dst, in_=ps[:, b, :]), reads=[PSR(b)],
                          writes=["XST%d" % sl])
                else:
                    act(dst, ps[:, b, :], AF.Copy, [PSR(b), "XST%d" % sl], ["XST%d" % sl])
            dma("sp", out_d[s, T * 128:(T + 1) * 128, :], st, "xo%d" % sl, ["XST%d" % sl], ["OUT%d" % sl])

    def rmsnorm(gcol, where):
        tmpv = ytmp if where == "y" else (lambda off, n, dt: scr(20576 + off, n, dt))
        for tc in range(NTC):
            b = nb()
            tsl = slice(tc * 512, (tc + 1) * 512)
            for kc in range(8):
                q = kc % 3
                sq = tmpv(q * 1024, 512, BF16)
                act(sq, xT[:, kc, tsl], AF.Square, ["xT:%d:%d" % (kc, tc)], ["SQ%d" % q])
                mm(ps[:, b, :], cm(CM_O1024), sq, kc == 0, kc == 7, ["SQ%d" % q, "CM"], [PSR(b)])
            r = tc % 2
            rstd = tmpv(3072 + r * 2048, 512, F32)
            rsqrt_eps(rstd, ps[:, b, :], [PSR(b)], ["RS%d" % r])
            for kc in range(8):
                stt(hT[:, kc, tsl], xT[:, kc, tsl], pvc(gcol + kc), rstd, ALU.mult, ALU.mult,
                    ["xT:%d:%d" % (kc, tc), "PV", "RS%d" % r], ["hT:%d:%d" % (kc, tc)])

    def conformer(l):
        slot = next_wslot()
        W = WB[:, slot, :].rearrange("p (k n) -> p k n", k=8)
        wload(slot, W, w_in_d[l, :, 4096:4608].rearrange("(k p) n -> p k n", p=128))
        UP = scr(0, 2 * 2080, BF16).rearrange("p (c t) -> p c t", c=2)
        o_sig = 8320
        o_cvf = o_sig + 4096
        o_cvb = o_cvf + 8192
        o_sqb = o_cvb + 4096
        o_st = o_sqb + 4096
        o_t = o_st + 4096
        for cc in range(2):
            S.add("dve", lambda e, cc=cc: e.memset(UP[:, cc, 0:15], 0.0), writes=["UPpad%d" % cc])
            S.add("dve", lambda e, cc=cc: e.memset(UP[:, cc, 15 + SEQ:30 + SEQ], 0.0), writes=["UPpad%d" % cc])
        for cc in range(2):
            for tc in range(NTC):
                tsl = slice(tc * 512, (tc + 1) * 512)
                ba, bg = nb(), nb()
                for kc in range(8):
                    mm(ps[:, ba, :], W[:, kc, cc * 128:(cc + 1) * 128], hT[:, kc, tsl], kc == 0, kc == 7,
                       ["WB%d" % slot, "hT:%d:%d" % (kc, tc)], [PSR(ba)])
                for kc in range(8):
                    mm(ps[:, bg, :], W[:, kc, 256 + cc * 128:256 + (cc + 1) * 128], hT[:, kc, tsl], kc == 0, kc == 7,
                       ["WB%d" % slot, "hT:%d:%d" % (kc, tc)], [PSR(bg)])
                q = tc % 2
                sig = scr(o_sig + q * 2048, 512, F32)
                act(sig, ps[:, bg, :], AF.Sigmoid, [PSR(bg)], ["SIG%d" % q])
                tt(UP[:, cc, 15 + tc * 512:15 + (tc + 1) * 512], ps[:, ba, :], sig, ALU.mult,
                   [PSR(ba), "SIG%d" % q], ["UP:%d:%d" % (cc, tc)])
        up_all = ["UP:%d:%d" % (cc, tc) for cc in range(2) for tc in range(NTC)] + ["UPpad0", "UPpad1"]
        for cc in range(2):
            for k in range(31):
                dg, dres = make_diag(PV_ACW + cc * 31 + k)
                for tc in range(NTC):
                    b = cc * 4 + tc
                    mm(ps[:, b, :], dg, UP[:, cc, tc * 512 + k: tc * 512 + k + 512], k == 0, k == 30,
                       [dres] + up_all, [PSR(b)])
        ring[0] = 0
        for tc in range(NTC):
            tsl = slice(tc * 512, (tc + 1) * 512)
            q = tc % 2
            for cc in range(2):
                b = cc * 4 + tc
                cvf = scr(o_cvf + (q * 2 + cc) * 2048, 512, F32)
                cvb = scr(o_cvb + (q * 2 + cc) * 1024, 512, BF16)
                sqb = scr(o_sqb + (q * 2 + cc) * 1024, 512, BF16)
                act(cvf, ps[:, b, :], AF.Identity, [PSR(b), "PV"], ["CVF%d%d" % (q, cc)], bias=pvc(PV_ACB + cc))
                act(cvb, ps[:, b, :], AF.Identity, [PSR(b), "PV"], ["CVB%d%d" % (q, cc)], bias=pvc(PV_ACB + cc))
                act(sqb, ps[:, b, :], AF.Square, [PSR(b), "PV"], ["SQB%d%d" % (q, cc)], bias=pvc(PV_ACB + cc))
            b1 = tc
            b2 = 4 + tc
            for cc in range(2):
                cvb = scr(o_cvb + (q * 2 + cc) * 1024, 512, BF16)
                mm(ps[:, b1, :], cm(CM_O256), cvb, cc == 0, cc == 1, ["CM", "CVB%d%d" % (q, cc)], [PSR(b1)])
            for cc in range(2):
                sqb = scr(o_sqb + (q * 2 + cc) * 1024, 512, BF16)
                mm(ps[:, b2, :], cm(CM_O256), sqb, cc == 0, cc == 1, ["CM", "SQB%d%d" % (q, cc)], [PSR(b2)])
            msq = scr(o_st, 512, F32)
            rstd = scr(o_st + 2048, 512, F32)
            act(msq, ps[:, b1, :], AF.Square, [PSR(b1)], ["MSQ"])
            tt(msq, ps[:, b2, :], msq, ALU.subtract, [PSR(b2), "MSQ"], ["MSQ"])
            rsqrt_eps(rstd, msq, ["MSQ"], ["LRS"])
            for cc in range(2):
                cvf = scr(o_cvf + (q * 2 + cc) * 2048, 512, F32)
                t = scr(o_t + cc * 2048, 512, F32)
                tt(t, cvf, ps[:, b1, :], ALU.subtract, ["CVF%d%d" % (q, cc), PSR(b1)], ["LT%d" % cc])
                tt(t, t, rstd, ALU.mult, ["LT%d" % cc, "LRS"], ["LT%d" % cc])
                act(yT[:, cc, tsl], t, AF.Silu, ["LT%d" % cc, "PV"], ["yT:%d:%d" % (cc, tc)],
                    bias=pvc(PV_ALNB + cc), scale=pvc(PV_ALNG + cc))

    O_QK = 0
    O_V = 16384
    O_TMP = 24576
    qkT = scr(O_QK, 4 * SEQ, BF16).rearrange("p (c t) -> p c t", c=4)
    Vt = scr(O_V, 16 * 256, BF16).rearrange("p (t c) -> p t c", t=16)

    def qk_norm_all(W, wres, specs, rope):
        units = [(wc0, dst, gcol, tc) for (wc0, dst, gcol) in specs for tc in range(NTC)]
        o_sq, o_qn, o_t1, o_t2 = 0, 3072, 5120, 7168
        st = {}

        def s1(u):
            wc0, dst, gcol, tc = units[u]
            tsl = slice(tc * 512, (tc + 1) * 512)
            b = nb()
            for kc in range(8):
                mm(ps[:, b, :], W[:, kc, wc0:wc0 + 128], hT[:, kc, tsl], kc == 0, kc == 7,
                   [wres, "hT:%d:%d" % (kc, tc)], [PSR(b)])
            q = u % 3
            sq = ytmp(o_sq + q * 1024, 512, BF16)
            act(sq, ps[:, b, :], AF.Square, [PSR(b)], ["QSQ%d" % q])
            st[u] = (b, sq, q)

        def s2(u):
            wc0, dst, gcol, tc = units[u]
            tsl = slice(tc * 512, (tc + 1) * 512)
            b, sq, q = st[u]
            b2 = nb()
            mm(ps[:, b2, :], cm(CM_B64), sq, True, True, ["CM", "QSQ%d" % q], [PSR(b2)])
            r = u % 2
            rs = WBR[:, r, 0:1024].bitcast(F32)
            rsqrt_eps(rs, ps[:, b2, :], [PSR(b2)], ["WBR%d" % r])
            if not rope:
                stt(qkT[:, dst, tsl], ps[:, b, :], pvc(gcol), rs, ALU.mult, ALU.mult,
                    [PSR(b), "PV", "WBR%d" % r], ["qk:%d:%d" % (dst, tc)])
            else:
                qn = ytmp(o_qn + r * 1024, 512, BF16)
                stt(qn, ps[:, b, :], pvc(gcol), rs, ALU.mult, ALU.mult, [PSR(b), "PV", "WBR%d" % r], ["QN%d" % r])
                st[u] = (qn, r)

        def s3(u):
            wc0, dst, gcol, tc = units[u]
            tsl = slice(tc * 512, (tc + 1) * 512)
            qn, r = st[u]
            b3 = nb()
            mm(ps[:, b3, :], cm(CM_ROT), qn, True, True, ["CM", "QN%d" % r], [PSR(b3)])
            t1 = ytmp(o_t1, 512, F32)
            t2 = ytmp(o_t2, 512, BF16)
            tt(t2, qn, ropeC[:, tsl], ALU.mult, ["QN%d" % r, "ropeC"], ["RT2"], eng="pool")
            tt(t1, ps[:, b3, :], ropeS[:, tsl], ALU.mult, [PSR(b3), "ropeS"], ["RT1"])
            tt(qkT[:, dst, tsl], t1, t2, ALU.add, ["RT1", "RT2"], ["qk:%d:%d" % (dst, tc)])

        n = len(units)
        for u in range(n + 2):
            if u < n:
                s1(u)
            if 1 <= u <= n:
                s2(u - 1)
            if rope and 2 <= u <= n + 1:
                s3(u - 2)

    def v_proj(WVv, wres, ncols):
        per_bank = 512 // ncols
        for T0 in range(0, NT, per_bank):
            b = nb()
            for j in range(per_bank):
                T = T0 + j
                for kc in range(8):
                    mm(ps[:, b, j * ncols:(j + 1) * ncols], hT[:, kc, T * 128:(T + 1) * 128], WVv[:, kc, 0:ncols],
                       kc == 0, kc == 7, [wres, "hT:%d:%d" % (kc, T // 4)], [PSR(b)])
            dst = Vt[:, T0:T0 + per_bank, 0:ncols]
            src = ps[:, b, :].rearrange("p (j c) -> p j c", j=per_bank)
            act(dst, src, AF.Copy, [PSR(b)], ["V:%d" % T for T in range(T0, T0 + per_bank)])

    O_EM = O_TMP
    O_PR = O_TMP + 6656
    O_PM = O_PR + 4096
    O_R = O_PM + 4096
    NPS = 4
    LAG = 3

    def attention(kind, branch, l):
        gi = [0]
        R = {"na": 2, "dil": 8, "swa": 1}[kind]

        def em_load(h, part):
            c0 = h * NA_COLS
            if part == "I":
                dst, src, res, sem = scr(O_EM + (h % 2) * 1280, 640, BF16), nab_d[l, :, c0:c0 + 640], "EMI%d" % (h % 2), "emi%d" % (h % 2)
            elif part == "E0":
                dst, src, res, sem = scr(O_EM + 2560, 1024, BF16), nab_d[l, :, c0 + 640:c0 + 1664], "EME0", "eme0"
            else:
                dst, src, res, sem = scr(O_EM + 4608, 1024, BF16), nab_d[l, :, c0 + 1664:c0 + 2688], "EME3", "eme3"
            dma("pool", dst, src, sem, [], [res])
            act(dst, dst, AF.Exp, [res], [res])

        if kind == "na":
            em_load(0, "I")
            em_load(0, "E0")
            em_load(0, "E3")
        for cq in range(2):
            for hp in range(2):
                if kind == "na":
                    hh = cq * 2 + hp
                    if hh + 1 < 4:
                        em_load(hh + 1, "I")
                    if hh > 0:
                        em_load(hh, "E3")
                hs = slice(hp * 64, hp * 64 + 64)
                ck = 2 if kind == "swa" else 2 + cq
                vc0 = 0 if kind == "swa" else cq * 128
                h_sw = hp * 2 + cq
                pend = []
                for g in range(NTC):
                    par = (cq * 2 + hp) * NTC + g
                    ob, db = (0, 1) if par % 2 == 0 else (2, 3)
                    items = []
                    if kind == "na":
                        h = cq * 2 + hp
                        emI = scr(O_EM + (h % 2) * 1280, 640, BF16)
                        emE0 = scr(O_EM + 2560, 1024, BF16)
                        emE3 = scr(O_EM + 4608, 1024, BF16)
                        resI = "EMI%d" % (h % 2)
                        if g == 0:
                            for T in range(4):
                                items.append((T, 0, 1, emE0, T * 256, "EME0"))
                            qint = (2, 3)
                        elif g == 3:
                            qint = (12, 13)
                        else:
                            qint = tuple(range(4 * g, 4 * g + 4))
                        for T in range(NT):
                            qs = [qt for qt in qint if abs(T - qt) <= 2]
                            if qs:
                                items.append((T, qs[0], qs[-1], emI, (2 - (T - qs[0])) * 128, resI))
                        if g == 3:
                            for T in range(12, 16):
                                items.append((T, 14, 15, emE3, (T - 12) * 256, "EME3"))
                    else:
                        base = CM_DIL if kind == "dil" else CM_WIN
                        for T in range(NT):
                            qlo = max(4 * g, T - R)
                            qhi = min(4 * g + 3, T + R)
                            if qlo > qhi:
                                continue
                            items.append((T, qlo, qhi, CM, base + (R - (T - qlo)) * 128, "CM"))
                    for idx, (T, qlo, qhi, mten, mc, mres) in enumerate(items):
                        n = qhi - qlo + 1
                        w = n * 128
                        sb = 4 + gi[0] % 4
                        pslot = gi[0] % NPS
                        gi[0] += 1
                        qres = ["qk:%d:%d" % (cq, g)]
                        mm(ps[:, sb, 0:w], qkT[hs, ck, T * 128:(T + 1) * 128], qkT[hs, cq, qlo * 128:(qhi + 1) * 128],
                           True, True, ["qk:%d:%d" % (ck, T // 4)] + qres, [PSR(sb)])
                        pr = scr(O_PR + pslot * 1024, 512, BF16)
                        pm = scr(O_PM + pslot * 1024, 512, BF16)
                        act(pr[:, 0:w], ps[:, sb, 0:w], AF.Exp, [PSR(sb)], ["PR%d" % pslot], scale=0.125)
                        tt(pm[:, 0:w], pr[:, 0:w], mten[:, mc:mc + w], ALU.mult, ["PR%d" % pslot, mres], ["PM%d" % pslot])
                        oc = (qlo - 4 * g) * 128

                        def pv_stage(T=T, pm=pm, pslot=pslot, w=w, oc=oc, ob=ob, db=db, vc0=vc0,
                                     first=(idx == 0), last=(idx == len(items) - 1)):
                            mm(ps[:, ob, oc:oc + w], Vt[:, T, vc0:vc0 + 128], pm[:, 0:w], first, last,
                               ["V:%d" % T, "PM%d" % pslot], [PSR(ob)])
                            mm(ps[:, db, oc:oc + w], cm(CM_ONE), pm[:, 0:w], first, last,
                               ["CM", "PM%d" % pslot], [PSR(db)])
                        pend.append(pv_stage)
                        while len(pend) > LAG:
                            pend.pop(0)()

                    def norm_stage(g=g, ob=ob, db=db, hs=hs, cq=cq, hp=hp, h_sw=h_sw):
                        rq = 0
                        r = scr(O_R, 512, F32)
                        if kind == "swa":
                            act(r[hs, :], ps[hs, db, :], AF.Ln, [PSR(db), "ESK"], ["NR%d" % rq], bias=ESK[hs, h_sw:h_sw + 1])
                        else:
                            act(r[hs, :], ps[hs, db, :], AF.Ln, [PSR(db)], ["NR%d" % rq])
                        act(r[hs, :], r[hs, :], AF.Exp, ["NR%d" % rq], ["NR%d" % rq], scale=-1.0)
                        extra = ["QSQ0", "QSQ1", "QSQ2", "QN0", "QN1", "RT1", "RT2", "SQ0", "SQ1", "SQ2", "RS0", "RS1"] if branch == 3 else []
                        tt(yT[hs, branch * 2 + cq, g * 512:(g + 1) * 512], ps[hs, ob, :], r[hs, :], ALU.mult,
                           [PSR(ob), "NR%d" % rq], ["yT:%d:%d:%d" % (branch * 2 + cq, g, hp)] + extra)
                    pend.append(norm_stage)
                    if kind == "na" and g == 0 and cq * 2 + hp + 1 < 4:
                        em_load(cq * 2 + hp + 1, "E0")
                while pend:
                    pend.pop(0)()

    def branch_loads(kind, l, qbase):
        slot = next_wslot()
        W = WB[:, slot, :].rearrange("p (k n) -> p k n", k=8)
        if kind != "swa":
            wload(slot, W, w_in_d[l, :, qbase:qbase + 512].rearrange("(k p) n -> p k n", p=128))
        else:
            for bb in range(2):
                for a in range(2):
                    hh = a * 2 + bb
                    src = w_in_d[l, :, qbase + hh * 64:qbase + (hh + 1) * 64].rearrange("(k p) d -> p k d", p=128)
                    dst = W[:, :, bb * 128 + a * 64:bb * 128 + (a + 1) * 64]
                    wload(slot, dst, src)
            wload(slot, W[:, :, 256:384], w_in_d[l, :, qbase + 256:qbase + 384].rearrange("(k p) n -> p k n", p=128))
        slot2 = next_wslot()
        W2 = WB[:, slot2, :].rearrange("p (k n) -> p k n", k=8)
        if kind != "swa":
            wload(slot2, W2[:, :, 0:256], w_in_d[l, :, qbase + 512:qbase + 768].rearrange("(k p) n -> p k n", p=128))
        else:
            wload(slot2, W2[:, :, 0:128], w_in_d[l, :, qbase + 384:qbase + 512].rearrange("(k p) n -> p k n", p=128))
        return (slot, slot2)

    def branch_attn(kind, branch, l, slots, gq, gk, rope, next_loads=None):
        slot, slot2 = slots
        W = WB[:, slot, :].rearrange("p (k n) -> p k n", k=8)
        W2 = WB[:, slot2, :].rearrange("p (k n) -> p k n", k=8)
        nvc = 128 if kind == "swa" else 256
        specs = [(0, 0, gq), (128, 1, gq), (256, 2, gk)]
        if kind != "swa":
            specs.append((384, 3, gk))
        qk_norm_all(W, "WB%d" % slot, specs, rope)
        v_proj(W2, "WB%d" % slot2, nvc)
        nxt = next_loads() if next_loads is not None else None
        attention(kind, branch, l)
        return nxt

    O_MIX = 0
    O_G = 32768
    O_ACC = O_G + 2048
    O_MT = O_ACC + 2048
    mixT = scr(O_MIX, 8 * SEQ, BF16).rearrange("p (c t) -> p c t", c=8)

    def merge(l):
        YT_ALL = None
        for dc in range(8):
            slot = next_wslot()
            W = WB[:, slot, :].rearrange("p (k n d) -> p k n d", k=8, n=4)
            for n in range(4):
                wload(slot, W[:, :, n, :], w_in_d[l, :, n * 1024 + dc * 128:n * 1024 + (dc + 1) * 128].rearrange("(k p) d -> p k d", p=128))
            bs = dc % 2
            Wb = WBR[:, bs, :].rearrange("p (k n d) -> p k n d", k=2, n=4)
            for n in range(3):
                dma("pool", Wb[:, :, n, :], w_br_d[l, n, :, dc * 128:(dc + 1) * 128].rearrange("(k p) d -> p k d", p=128),
                    "wbr%d" % bs, [], ["WBR%d" % bs])
            for half in range(2):
                dma("pool", Wb[half * 64:(half + 1) * 64, :, 3, :],
                    w_br_d[l, 3, half * 128:(half + 1) * 128, dc * 128:(dc + 1) * 128].rearrange("(k p) d -> p k d", p=64),
                    "wbr%d" % bs, [], ["WBR%d" % bs])
            for tc in range(NTC):
                tsl = slice(tc * 512, (tc + 1) * 512)
                aq = 0
                acc = scr(O_ACC, 512, F32)
                for n in range(4):
                    bg, bb = nb(), nb()
                    for kc in range(8):
                        mm(ps[:, bg, :], W[:, kc, n, :], hT[:, kc, tsl], kc == 0, kc == 7,
                           ["WB%d" % slot, "hT:%d:%d" % (kc, tc)], [PSR(bg)])
                    for kc in range(2):
                        ych = n * 2 + kc
                        if n == 0:
                            yres = ["yT:%d:%d" % (ych, tc)]
                        else:
                            yres = ["yT:%d:%d:%d" % (ych, tc, hp) for hp in range(2)]
                        mm(ps[:, bb, :], Wb[:, kc, n, :], yT[:, ych, tsl], kc == 0, kc == 1,
                           ["WBR%d" % bs] + yres, [PSR(bb)])
                    gq = (tc * 4 + n) % 2
                    G = scr(O_G + gq * 1024, 512, BF16)
                    act(G, ps[:, bg, :], AF.Sigmoid, [PSR(bg), "PV"], ["G%d" % gq], bias=pvc(PV_GATEB + n * 8 + dc))
                    if n == 0:
                        tt(acc, ps[:, bb, :], G, ALU.mult, [PSR(bb), "G%d" % gq], ["ACC%d" % aq])
                    else:
                        mt = scr(O_MT, 512, F32)
                        tt(mt, ps[:, bb, :], G, ALU.mult, [PSR(bb), "G%d" % gq], ["MT"])
                        if n < 3:
                            tt(acc, acc, mt, ALU.add, ["ACC%d" % aq, "MT"], ["ACC%d" % aq])
                        else:
                            tt(mixT[:, dc, tsl], acc, mt, ALU.add, ["ACC%d" % aq, "MT"], ["mix:%d:%d" % (dc, tc)])
        for dg in range(2):
            slot = next_wslot()
            W = WB[:, slot, :].rearrange("p (k n) -> p k n", k=8)
            wload(slot, W, w_out_d[l, :, dg * 512:(dg + 1) * 512].rearrange("(k p) n -> p k n", p=128))
            for j in range(4):
                do = dg * 4 + j
                for tc in range(NTC):
                    tsl = slice(tc * 512, (tc + 1) * 512)
                    b = nb()
                    for kc in range(8):
                        mm(ps[:, b, :], W[:, kc, j * 128:(j + 1) * 128], mixT[:, kc, tsl], kc == 0, kc == 7,
                           ["WB%d" % slot, "mix:%d:%d" % (kc, tc)], [PSR(b)])
                    tt(xT[:, do, tsl], xT[:, do, tsl], ps[:, b, :], ALU.add, ["xT:%d:%d" % (do, tc), PSR(b)],
                       ["xT:%d:%d" % (do, tc)])

    O_U = 12288
    O_SG = O_U + 3 * 2080
    yT_flat = yT[:].rearrange("p c t -> p (c t)")

    def actT(j):
        if j < 16:
            return yT_flat[:, j * 1024:(j + 1) * 1024]
        return scr((j - 16) * 2048, 1024, BF16)

    def ffn(l):
        rmsnorm(PV_GFFN, "s")
        ucount = [0]
        for th in range(2):
            t0 = th * 1024
            halo_tok = 1024 if th == 0 else 1023
            halo_col = 1025 if th == 0 else 0
            zero_col = 0 if th == 0 else 1025
            for u in range(3):
                U = scr(O_U + u * 2080, 1026, BF16)
                S.add("dve", lambda e, U=U, zc=zero_col: e.memset(U[:, zc:zc + 1], 0.0), writes=["U%d" % u])
            pendB = []
            sg_by_j = {}
            for jp in range(11):
                slot = next_wslot()
                W = WB[:, slot, :].rearrange("p (k a n) -> p k a n", k=8, a=2)
                for a in range(2):
                    wload(slot, W[:, :, a, :], w_up_d[l, :, a * D_FF + jp * 256: a * D_FF + (jp + 1) * 256].rearrange("(k p) n -> p k n", p=128))
                for jj in range(2):
                    j = jp * 2 + jj
                    for a in range(2):
                        u = ucount[0] % 3
                        ucount[0] += 1
                        U = scr(O_U + u * 2080, 1026, BF16)
                        ures = "U%d" % u
                        ch = a * 22 + j
                        wv = W[:, :, a, jj * 128:(jj + 1) * 128]
                        dgs = [make_diag(PV_FCW + ch * 3 + k) for k in range(3)]
                        for t2 in range(2):
                            b = nb()
                            tok = t0 + t2 * 512
                            for kc in range(8):
                                mm(ps[:, b, :], wv[:, kc, :], hT[:, kc, tok:tok + 512], kc == 0, kc == 7,
                                   ["WB%d" % slot, "hT:%d:%d" % (kc, tok // 512)], [PSR(b)])
                            act(U[:, 1 + t2 * 512:1 + (t2 + 1) * 512], ps[:, b, :], AF.Copy, [PSR(b)], [ures])
                        if th == 0:
                            b = nb()
                            for kc in range(8):
                                mm(ps[:, b, 0:1], wv[:, kc, :], hT[:, kc, 1024:1025], kc == 0, kc == 7,
                                   ["WB%d" % slot, "hT:%d:%d" % (kc, 2)], [PSR(b)])
                            S.add("dve", lambda e, U=U, b=b: e.tensor_copy(out=U[:, 1025:1026], in_=ps[:, b, 0:1]),
                                  reads=[PSR(b)], writes=[ures])
                            S.add("dve", lambda e, U=U, ch=ch: e.tensor_copy(out=HS[:, ch:ch + 1], in_=U[:, 1024:1025]),
                                  reads=[ures], writes=["HS%d" % ch])
                        else:
                            S.add("dve", lambda e, U=U, ch=ch: e.tensor_copy(out=U[:, 0:1], in_=HS[:, ch:ch + 1]),
                                  reads=["HS%d" % ch], writes=[ures])

                        def stageB(U=U, ures=ures, ch=ch, a=a, j=j, dgs=dgs):
                            for t2 in range(2):
                                b = nb()
                                for k in range(3):
                                    mm(ps[:, b, :], dgs[k][0], U[:, t2 * 512 + k:t2 * 512 + k + 512], k == 0, k == 2,
                                       [dgs[k][1], ures], [PSR(b)])
                                if a == 0:
                                    sq = t2
                                    sg = scr(O_SG + sq * 1024, 512, BF16)
                                    act(sg, ps[:, b, :], AF.Silu, [PSR(b), "PV"], ["SG%d" % sq], bias=pvc(PV_FCB + ch))
                                else:
                                    sg = scr(O_SG + t2 * 1024, 512, BF16)
                                    stt(actT(j)[:, t2 * 512:(t2 + 1) * 512], ps[:, b, :], pvc(PV_FCB + ch), sg, ALU.add, ALU.mult,
                                        [PSR(b), "PV", "SG%d" % t2], ["act:%d:%d" % (j, t2)])
                        pendB.append(stageB)
                        while len(pendB) > 1:
                            pendB.pop(0)()
            while pendB:
                pendB.pop(0)()
            for do in range(8):
                slot = next_wslot()
                W = WB[:, slot, 0:22 * 128].rearrange("p (k n) -> p k n", k=22)
                for hh in range(2):
                    wload(slot, W[:, hh * 11:(hh + 1) * 11, :],
                          w_dn_d[l, hh * 1408:(hh + 1) * 1408, do * 128:(do + 1) * 128].rearrange("(k p) n -> p k n", p=128))
                for t2 in range(2):
                    b = nb()
                    tok = t0 + t2 * 512
                    for j in range(22):
                        mm(ps[:, b, :], W[:, j, :], actT(j)[:, t2 * 512:(t2 + 1) * 512], j == 0, j == 21,
                           ["WB%d" % slot, "act:%d:%d" % (j, t2)], [PSR(b)])
                    tt(xT[:, do, tok:tok + 512], xT[:, do, tok:tok + 512], ps[:, b, :], ALU.add,
                       ["xT:%d:%d" % (do, tok // 512), PSR(b)], ["xT:%d:%d" % (do, tok // 512)])

    SCR_ALL = "SCRALL"


    scr_names_A0 = ["SQ0", "SQ1", "SQ2", "RS0", "RS1", "XST0", "XST1"]
    scr_names_conf = ["UPpad0", "UPpad1", "SIG0", "SIG1", "MSQ", "LRS", "LT0", "LT1"] + \
        ["UP:%d:%d" % (c, t) for c in range(2) for t in range(NTC)] + \
        ["CVF%d%d" % (q, c) for q in range(2) for c in range(2)] + ["CVB%d%d" % (q, c) for q in range(2) for c in range(2)] + \
        ["SQB%d%d" % (q, c) for q in range(2) for c in range(2)]
    scr_names_attn = ["qk:%d:%d" % (c, t) for c in range(4) for t in range(NTC)] + ["V:%d" % T for T in range(NT)] + \
        ["QSQ0", "QSQ1", "QSQ2", "QRS0", "QRS1", "QN0", "QN1", "RT1", "RT2", "EMI0", "EMI1", "EME0", "EME3", "PR0", "PR1", "PR2", "PR3", "PM0", "PM1", "PM2", "PM3", "NR0", "NR1"]
    scr_names_merge = ["mix:%d:%d" % (c, t) for c in range(8) for t in range(NTC)] + ["G0", "G1", "ACC0", "ACC1", "MT", "WBR0", "WBR1"]
    scr_names_ffn = ["U0", "U1", "U2", "SG0", "SG1"] + ["act:%d:%d" % (j, t) for j in range(22) for t in range(2)]
    yT_names = ["yT:%d:%d" % (c, t) for c in range(2) for t in range(NTC)] + \
        ["yT:%d:%d:%d" % (c, t, hp) for c in range(2, 8) for t in range(NTC) for hp in range(2)]
    ALLSCR = scr_names_A0 + scr_names_conf + scr_names_attn + scr_names_merge + scr_names_ffn

    def barrier(extra=()):
        names = ALLSCR + list(extra)
        S.add("dve", lambda e: e.memset(DUM[:, 0:1], 0.0), reads=names, writes=names)

    for s in range(nseq):
        barrier(yT_names)
        load_x(s)
        for li in range(nl):
            l = l0 + li
            dma("sp", PV[:], pvec_d[l], "pvl", [], ["PV"])
            act(ESK[:], PV[:, PV_SINK:PV_SINK + 4], AF.Exp, ["PV"], ["ESK"])
            barrier()
            rmsnorm(PV_GMIX, "y")
            conformer(l)
            barrier()
            sl_na = branch_loads("na", l, 4608)
            sl_dil = branch_attn("na", 1, l, sl_na, PV_QKG + 0, PV_QKG + 1, False,
                                 next_loads=lambda l=l: branch_loads("dil", l, 5376))
            sl_swa = branch_attn("dil", 2, l, sl_dil, PV_QKG + 2, PV_QKG + 3, True,
                                 next_loads=lambda l=l: branch_loads("swa", l, 6144))
            branch_attn("swa", 3, l, sl_swa, PV_QKG + 4, PV_QKG + 5, True)
            barrier()
            merge(l)
            barrier(yT_names)
            ffn(l)
            barrier(yT_names)
        store_x(s)
    S.add("sp", lambda e: None, reads=["OUT0", "OUT1"], writes=[])

    S.finalize()
    with contextlib.ExitStack() as st:
        sems = {}
        for e in ENGS:
            sems[e] = st.enter_context(nc.semaphore("s_" + e))
        for k in S.dcounts:
            sems[k] = st.enter_context(nc.semaphore("d_" + k))
        block = st.enter_context(nc.Block())
        block.sync(lambda e: S.run_engine("sp", e, sems))
        block.scalar(lambda e: S.run_engine("act", e, sems))
        block.vector(lambda e: S.run_engine("dve", e, sems))
        block.gpsimd(lambda e: S.run_engine("pool", e, sems))
        block.tensor(lambda e: S.run_engine("pe", e, sems))
    return nc, S


def _const_mats():
    kk = np.arange(128)[:, None]
    qq = np.arange(128)[None, :]
    cmat = np.zeros((128, NCM), np.float32)
    cmat[:, CM_WIN:CM_WIN + 128] = (kk <= qq)
    cmat[:, CM_WIN + 128:CM_WIN + 256] = 1.0
    cmat[:, CM_WIN + 256:CM_WIN + 384] = (kk >= qq)
    for d in range(-8, 9):
        o = 128 * d + kk - qq
        m = (np.abs(o) <= 64).astype(np.float32)
        m += ((o % 4 == 0) & (np.abs(o) <= 256)).astype(np.float32)
        m += ((o % 16 == 0) & (np.abs(o) <= 1024)).astype(np.float32)
        cmat[:, CM_DIL + (8 - d) * 128: CM_DIL + (9 - d) * 128] = m
    cmat[:, CM_ID:CM_ID + 128] = np.eye(128, dtype=np.float32)
    cmat[:, CM_O1024:CM_O1024 + 128] = 1.0 / 1024.0
    blk = np.zeros((128, 128), np.float32)
    blk[0:64, 0:64] = 1.0 / 64.0
    blk[64:128, 64:128] = 1.0 / 64.0
    cmat[:, CM_B64:CM_B64 + 128] = blk
    cmat[:, CM_ONE:CM_ONE + 128] = 1.0
    rot = np.zeros((128, 128), np.float32)
    for hb in range(2):
        for d in range(8):
            rot[hb * 64 + d + 8, hb * 64 + d] = -1.0
            rot[hb * 64 + d, hb * 64 + d + 8] = 1.0
    cmat[:, CM_ROT:CM_ROT + 128] = rot
    cmat[:, CM_O256:CM_O256 + 128] = 1.0 / 256.0
    pos = np.arange(SEQ, dtype=np.float32)
    inv = (np.float32(500000.0) ** (-np.arange(0, 16, 2, dtype=np.float32) / np.float32(16))).astype(np.float32)
    ang = (pos[:, None] * inv[None, :]).astype(np.float32)
    cosT = np.cos(ang).astype(np.float32).T
    sinT = np.sin(ang).astype(np.float32).T
    rc = np.ones((128, SEQ), np.float32)
    rs = np.zeros((128, SEQ), np.float32)
    for hb in range(2):
        for d in range(16):
            rc[hb * 64 + d] = cosT[d % 8]
            rs[hb * 64 + d] = sinT[d % 8]
    return cmat, rc, rs, np.eye(128, dtype=np.float32)


def _na_index():
    blocks = [(2 + d, 2) for d in (2, 1, 0, -1, -2)]
    blocks += [(T, qt) for T in range(4) for qt in (0, 1)]
    blocks += [(T, qt) for T in range(12, 16) for qt in (14, 15)]
    dr = np.zeros((128, NA_COLS), np.int64)
    dc = np.zeros((128, NA_COLS), np.int64)
    valid = np.zeros((128, NA_COLS), bool)
    kk = np.arange(128)[:, None]
    qq = np.arange(128)[None, :]
    for i, (T, qt) in enumerate(blocks):
        col = i * 128
        qtok = qt * 128 + qq
        r = qtok // 64
        c = qtok % 64
        rs = np.clip(r - 4, 0, 24)
        cs = np.clip(c - 8, 0, 48)
        ktok = T * 128 + kk
        kr = ktok // 64
        kc = ktok % 64
        v = (kr >= rs) & (kr < rs + 8) & (kc >= cs) & (kc < cs + 16)
        dr[:, col:col + 128] = np.where(v, kr - r + 7, 0)
        dc[:, col:col + 128] = np.where(v, kc - c + 15, 0)
        valid[:, col:col + 128] = v
    return dr, dc, valid


def _host_layout(inp):
    f = np.float32
    L = DEPTH
    pv = np.zeros((L, 128, NPV), f)
    pv[:, :, PV_GMIX:PV_GMIX + 8] = inp["g_mix"].reshape(L, 8, 128).transpose(0, 2, 1)
    pv[:, :, PV_GFFN:PV_GFFN + 8] = inp["g_ffn"].reshape(L, 8, 128).transpose(0, 2, 1)
    pv[:, :, PV_GATEB:PV_GATEB + 32] = inp["gate_b"].reshape(L, 4, 8, 128).transpose(0, 3, 1, 2).reshape(L, 128, 32)
    pv[:, :, PV_ACW:PV_ACW + 62] = inp["a_conv_w"].reshape(L, 31, 2, 128).transpose(0, 3, 2, 1).reshape(L, 128, 62)
    pv[:, :, PV_ACB:PV_ACB + 2] = inp["a_conv_b"].reshape(L, 2, 128).transpose(0, 2, 1)
    pv[:, :, PV_ALNG:PV_ALNG + 2] = inp["a_ln_g"].reshape(L, 2, 128).transpose(0, 2, 1)
    pv[:, :, PV_ALNB:PV_ALNB + 2] = inp["a_ln_b"].reshape(L, 2, 128).transpose(0, 2, 1)
    for i, k in enumerate(["na_qn", "na_kn", "dil_qn", "dil_kn", "swa_qn", "swa_kn"]):
        pv[:, :, PV_QKG + i] = np.tile(inp[k], (1, 2))
    pv[:, :, PV_SINK:PV_SINK + 4] = inp["swa_sink"][:, None, :]
    pv[:, :, PV_FCW:PV_FCW + 132] = inp["ffn_conv_w"].reshape(L, 3, 44, 128).transpose(0, 3, 2, 1).reshape(L, 128, 132)
    pv[:, :, PV_FCB:PV_FCB + 44] = inp["ffn_conv_b"].reshape(L, 44, 128).transpose(0, 2, 1)
    dr, dc, valid = _na_index()
    rpb = inp["na_rpb"]
    g = rpb[:, :, dr, dc]
    g = np.where(valid[None, None], g, f(-30000.0)).astype(f)
    nab = np.ascontiguousarray(g.transpose(0, 2, 1, 3).reshape(L, 128, 4 * NA_COLS))
    return pv, nab


_PROG_CACHE = {}


def _get_prog(nl, nseq, l0):
    key = (nl, nseq, l0)
    if key not in _PROG_CACHE:
        _PROG_CACHE[key] = build_program(nl, nseq, l0)[0]
    return _PROG_CACHE[key]


def run_layers(inp, x, nl, l0, ncores, nseq):
    cmat, rc, rs, idf = _const_mats()
    pv, nab = _host_layout(inp)
    nc = _get_prog(nl, nseq, l0)
    c32 = lambda a: np.ascontiguousarray(np.asarray(a, dtype=np.float32))
    shared = {
        "w_in": c32(inp["w_in"]), "w_branch": c32(inp["w_branch"]), "w_out": c32(inp["w_out"]),
        "w_up": c32(inp["w_up"]), "w_down": c32(inp["w_down"]), "pvec": pv, "nab": nab,
        "cmat": cmat, "ropec": rc, "ropes": rs, "identf": idf,
    }
    in_maps = []
    for c in range(ncores):
        m = dict(shared)
        m["x"] = np.ascontiguousarray(x[c * nseq:(c + 1) * nseq])
        in_maps.append(m)
    res = run_bass_kernel_spmd(nc, in_maps, core_ids=list(range(ncores)))
    return np.concatenate([np.asarray(r["out"]) for r in res.results], axis=0)


def kernel(**inputs):
    inp = {k: np.asarray(v) for k, v in inputs.items()}
    x = np.ascontiguousarray(inp["x"].astype(np.float32))
    out = run_layers(inp, x, DEPTH, 0, 8, 2)
    return out.astype(np.float32)
```
